# Optimizing a Trainium2 kernel written in Bass

```python
import math
import jax, jax.numpy as jnp
from jax import lax
import numpy as np

D_MODEL = 2048
BATCH = 8
SEQ = 2048
DEPTH = 2

N_A_LAYERS = DEPTH // 2
N_B_LAYERS = DEPTH - N_A_LAYERS
SSM_GROUP = 16
SSM_GROUPS = D_MODEL // SSM_GROUP
SSM_STATE = 64
SCAN_CHUNK = 128
DT_MIN = 1e-3
DT_MAX = 1e-1
HEAD_DIM = 128
N_HEADS = D_MODEL // (2 * HEAD_DIM)
D_FF = 4 * D_MODEL
ROPE_THETA = 10000.0
Q_BLOCK = 128
EPS = 1e-6
LAMBDA_STD = 0.1

kernel_name = 's5_diffattn_yoco_hybrid'


def rms_norm(x, g):
    xf = x.astype(jnp.float32)
    y = xf * lax.rsqrt(jnp.mean(xf * xf, axis=-1, keepdims=True) + EPS)
    return (y * g.astype(jnp.float32)).astype(x.dtype)


def rope_tables(L):
    pos = jnp.arange(L, dtype=jnp.float32)
    inv_freq = 1.0 / (ROPE_THETA ** (jnp.arange(0, HEAD_DIM, 2, dtype=jnp.float32) / HEAD_DIM))
    ang = pos[:, None] * inv_freq[None, :]
    emb = jnp.concatenate([ang, ang], axis=-1)
    return jnp.cos(emb)[:, None, :], jnp.sin(emb)[:, None, :]


def apply_rope(t, cos, sin):
    tf = t.astype(jnp.float32)
    t1, t2 = jnp.split(tf, 2, axis=-1)
    rot = jnp.concatenate([-t2, t1], axis=-1)
    return (tf * cos + rot * sin).astype(t.dtype)


def _cmul(ar, ai, br, bi):
    return ar * br - ai * bi, ar * bi + ai * br


def _scan_combine(e1, e2):
    a1r, a1i, b1r, b1i = e1
    a2r, a2i, b2r, b2i = e2
    ar, ai = _cmul(a2r, a2i, a1r, a1i)
    br, bi = _cmul(a2r, a2i, b1r, b1i)
    return ar, ai, br + b2r, bi + b2i


def s5_mixer(x, w_in, a_re, a_im, log_dt, b_re, b_im, c_re, c_im, d_skip, w_glu):
    f32 = jnp.float32
    Bsz, L, _ = x.shape
    u = (x @ w_in).astype(f32).reshape(Bsz, L, SSM_GROUPS, SSM_GROUP)
    step = jnp.exp(log_dt.astype(f32))[:, None]
    lam_re = jnp.minimum(a_re.astype(f32), -1e-4)
    lam_im = a_im.astype(f32)
    mag = jnp.exp(step * lam_re)
    abar_re = mag * jnp.cos(step * lam_im)
    abar_im = mag * jnp.sin(step * lam_im)
    den = lam_re * lam_re + lam_im * lam_im
    nr = abar_re - 1.0
    ni = abar_im
    coef_re = (nr * lam_re + ni * lam_im) / den
    coef_im = (ni * lam_re - nr * lam_im) / den
    bbar_re, bbar_im = _cmul(coef_re[..., None], coef_im[..., None], b_re.astype(f32), b_im.astype(f32))
    c_r = c_re.astype(f32)
    c_i = c_im.astype(f32)

    n_chunks = L // SCAN_CHUNK
    u_chunks = u.reshape(Bsz, n_chunks, SCAN_CHUNK, SSM_GROUPS, SSM_GROUP).transpose(1, 0, 2, 3, 4)

    def chunk_step(carry, u_c):
        s_re, s_im = carry
        bu_re = jnp.einsum('btgp,gnp->btgn', u_c, bbar_re)
        bu_im = jnp.einsum('btgp,gnp->btgn', u_c, bbar_im)
        a_r = jnp.broadcast_to(abar_re, bu_re.shape)
        a_i = jnp.broadcast_to(abar_im, bu_re.shape)
        acum_re, acum_im, h_re, h_im = lax.associative_scan(
            _scan_combine, (a_r, a_i, bu_re, bu_im), axis=1)
        cr, ci = _cmul(acum_re, acum_im, s_re[:, None], s_im[:, None])
        st_re = h_re + cr
        st_im = h_im + ci
        y = jnp.einsum('btgn,gpn->btgp', st_re, c_r) - jnp.einsum('btgn,gpn->btgp', st_im, c_i)
        return (st_re[:, -1], st_im[:, -1]), y

    init = (jnp.zeros((Bsz, SSM_GROUPS, SSM_STATE), f32), jnp.zeros((Bsz, SSM_GROUPS, SSM_STATE), f32))
    _, ys = lax.scan(chunk_step, init, u_chunks)
    y = ys.transpose(1, 0, 2, 3, 4).reshape(Bsz, L, D_MODEL)
    y = y + d_skip.astype(f32) * u.reshape(Bsz, L, D_MODEL)
    z = jax.nn.gelu(y).astype(x.dtype)
    val, gate = jnp.split(z @ w_glu, 2, axis=-1)
    return val * jax.nn.sigmoid(gate)


def shared_kv(h, g_kv, w_kv, cos, sin):
    Bsz, L, _ = h.shape
    kv = rms_norm(h, g_kv) @ w_kv
    k, v = jnp.split(kv, 2, axis=-1)
    k = k.reshape(Bsz, L, N_HEADS, 2, HEAD_DIM)
    k1 = apply_rope(k[..., 0, :], cos, sin)
    k2 = apply_rope(k[..., 1, :], cos, sin)
    v = v.reshape(Bsz, L, N_HEADS, 2 * HEAD_DIM)
    return k1, k2, v


def diff_attention(x, w_q, lq1, lk1, lq2, lk2, g_sub, w_o, k1, k2, v, cos, sin, lambda_init):
    f32 = jnp.float32
    Bsz, L, _ = x.shape
    scale = HEAD_DIM ** -0.5
    q = (x @ w_q).reshape(Bsz, L, N_HEADS, 2, HEAD_DIM)
    q1 = apply_rope(q[..., 0, :], cos, sin) * scale
    q2 = apply_rope(q[..., 1, :], cos, sin) * scale
    lam = (jnp.exp(jnp.sum(lq1.astype(f32) * lk1.astype(f32)))
           - jnp.exp(jnp.sum(lq2.astype(f32) * lk2.astype(f32))) + lambda_init)

    outs = []
    for i in range(L // Q_BLOCK):
        s0, e = i * Q_BLOCK, (i + 1) * Q_BLOCK
        causal = jnp.arange(e)[None, :] <= (s0 + jnp.arange(Q_BLOCK))[:, None]

        def probs(qb, kp):
            sc = jnp.einsum('bqhd,bkhd->bhqk', qb, kp).astype(f32)
            sc = jnp.where(causal, sc, -jnp.inf)
            return jax.nn.softmax(sc, axis=-1)

        p = probs(q1[:, s0:e], k1[:, :e]) - lam * probs(q2[:, s0:e], k2[:, :e])
        outs.append(jnp.einsum('bhqk,bkhe->bqhe', p.astype(v.dtype), v[:, :e]))
    o = jnp.concatenate(outs, axis=1)
    o = rms_norm(o, g_sub) * (1.0 - lambda_init)
    return o.reshape(Bsz, L, D_MODEL) @ w_o


def sq_relu_mlp(x, w_up, w_down):
    return jnp.square(jax.nn.relu(x @ w_up)) @ w_down


def setup_inputs(seed: int = 0) -> dict:
    key = jax.random.key(seed)
    ks = jax.random.split(key, 32)
    f32 = jnp.float32
    D, G, N, P = D_MODEL, SSM_GROUPS, SSM_STATE, SSM_GROUP

    def nrm(k, shape, std):
        return jax.random.normal(k, shape, f32) * std

    def gain(k, shape):
        return 1.0 + 0.02 * jax.random.normal(k, shape, f32)

    n_idx = jnp.arange(N, dtype=f32)
    return {
        'x': jax.random.normal(ks[0], (BATCH, SEQ, D), f32),
        'mix_pre_g': gain(ks[1], (DEPTH, D)),
        'mix_post_g': gain(ks[2], (DEPTH, D)),
        'mlp_pre_g': gain(ks[3], (DEPTH, D)),
        'mlp_post_g': gain(ks[4], (DEPTH, D)),
        'ssm_w_in': nrm(ks[5], (N_A_LAYERS, D, D), D ** -0.5),
        'ssm_a_re': -0.5 + 0.01 * jax.random.normal(ks[6], (N_A_LAYERS, G, N), f32),
        'ssm_a_im': math.pi * n_idx + 0.01 * jax.random.normal(ks[7], (N_A_LAYERS, G, N), f32),
        'ssm_log_dt': jax.random.uniform(ks[8], (N_A_LAYERS, G), f32, math.log(DT_MIN), math.log(DT_MAX)),
        'ssm_b_re': nrm(ks[9], (N_A_LAYERS, G, N, P), (0.5 / P) ** 0.5),
        'ssm_b_im': nrm(ks[10], (N_A_LAYERS, G, N, P), (0.5 / P) ** 0.5),
        'ssm_c_re': nrm(ks[11], (N_A_LAYERS, G, P, N), (0.5 / N) ** 0.5),
        'ssm_c_im': nrm(ks[12], (N_A_LAYERS, G, P, N), (0.5 / N) ** 0.5),
        'ssm_d': nrm(ks[13], (N_A_LAYERS, D), 1.0),
        'ssm_w_glu': nrm(ks[14], (N_A_LAYERS, D, 2 * D), D ** -0.5),
        'kv_norm_g': gain(ks[15], (D,)),
        'w_kv': nrm(ks[16], (D, 2 * D), D ** -0.5),
        'attn_w_q': nrm(ks[17], (N_B_LAYERS, D, D), D ** -0.5),
        'lam_q1': nrm(ks[18], (N_B_LAYERS, HEAD_DIM), LAMBDA_STD),
        'lam_k1': nrm(ks[19], (N_B_LAYERS, HEAD_DIM), LAMBDA_STD),
        'lam_q2': nrm(ks[20], (N_B_LAYERS, HEAD_DIM), LAMBDA_STD),
        'lam_k2': nrm(ks[21], (N_B_LAYERS, HEAD_DIM), LAMBDA_STD),
        'attn_subln_g': gain(ks[22], (N_B_LAYERS, 2 * HEAD_DIM)),
        'attn_w_o': nrm(ks[23], (N_B_LAYERS, D, D), D ** -0.5),
        'mlp_w_up': nrm(ks[24], (DEPTH, D, D_FF), D ** -0.5),
        'mlp_w_down': nrm(ks[25], (DEPTH, D_FF, D), D_FF ** -0.5),
    }


def reference(x, mix_pre_g, mix_post_g, mlp_pre_g, mlp_post_g,
              ssm_w_in, ssm_a_re, ssm_a_im, ssm_log_dt, ssm_b_re, ssm_b_im, ssm_c_re, ssm_c_im,
              ssm_d, ssm_w_glu, kv_norm_g, w_kv,
              attn_w_q, lam_q1, lam_k1, lam_q2, lam_k2, attn_subln_g, attn_w_o,
              mlp_w_up, mlp_w_down):
    L = x.shape[1]
    cos, sin = rope_tables(L)
    h = x
    k1 = k2 = v = None
    for l in range(DEPTH):
        hn = rms_norm(h, mix_pre_g[l])
        if l < N_A_LAYERS:
            a = l
            mix = s5_mixer(hn, ssm_w_in[a], ssm_a_re[a], ssm_a_im[a], ssm_log_dt[a],
                           ssm_b_re[a], ssm_b_im[a], ssm_c_re[a], ssm_c_im[a], ssm_d[a], ssm_w_glu[a])
        else:
            b = l - N_A_LAYERS
            lambda_init = 0.8 - 0.6 * math.exp(-0.3 * l)
            mix = diff_attention(hn, attn_w_q[b], lam_q1[b], lam_k1[b], lam_q2[b], lam_k2[b],
                                 attn_subln_g[b], attn_w_o[b], k1, k2, v, cos, sin, lambda_init)
        h = h + rms_norm(mix, mix_post_g[l])
        ff = sq_relu_mlp(rms_norm(h, mlp_pre_g[l]), mlp_w_up[l], mlp_w_down[l])
        h = h + rms_norm(ff, mlp_post_g[l])
        if l == N_A_LAYERS - 1:
            k1, k2, v = shared_kv(h, kv_norm_g, w_kv, cos, sin)
    return h
```

```python
import contextlib
import math
import numpy as np
import concourse.bass as bass
import concourse.mybir as mybir
from concourse.bass_utils import run_bass_kernel_spmd

F32 = mybir.dt.float32
BF16 = mybir.dt.bfloat16
I32 = mybir.dt.int32
ALU = mybir.AluOpType
AF = mybir.ActivationFunctionType

D = 2048
L = 2048
DFF = 8192
NCORES = 8
EPS = 1e-6
ENGS = ("pe", "act", "dve", "pool", "sp")
SEM_CHUNK = 30000
TWO_PI = 2.0 * math.pi
FIX_ENG = "dve"
SAME_ENG_ALL = True


class Buf:
    __slots__ = ("name", "last_w", "readers", "dsem", "dbase", "dcount", "last_group", "cur_group_id", "psum")

    def __init__(self, name, psum=False):
        self.name = name
        self.last_w = None
        self.readers = []
        self.dsem = None
        self.dcount = 0
        self.last_group = []
        self.cur_group_id = None
        self.psum = psum


class Op:
    __slots__ = ("eng", "fn", "deps", "is_dma", "dbuf", "dgroup", "ticket", "has_dep")

    def __init__(self, eng, fn):
        self.eng = eng
        self.fn = fn
        self.deps = []
        self.is_dma = False
        self.dbuf = None
        self.dgroup = None
        self.ticket = None
        self.has_dep = False


class Group:
    __slots__ = ("end",)

    def __init__(self):
        self.end = 0


class Tile:
    def __init__(self, sched, t, name, psum=False):
        self.s = sched
        self.t = t
        self.name = name
        self.psum = psum
        self._b = None
        self._ph = -1

    @property
    def b(self):
        if self._ph != self.s.phase_id:
            self._b = Buf(self.name, self.psum)
            self._ph = self.s.phase_id
        return self._b


class Sched:
    def __init__(self, nc, nsem_eng=4, ndma=64):
        self.nc = nc
        self.stack = contextlib.ExitStack()
        self.esem = {e: [self.stack.enter_context(nc.semaphore(f"s_{e}{i}")) for i in range(nsem_eng)]
                     for e in ENGS if e != "sp"}
        self.dpool = [[self.stack.enter_context(nc.semaphore(f"d{i}")), 0] for i in range(ndma)]
        self.ticket = {e: 0 for e in ENGS}
        self.ops = {e: [] for e in ENGS}
        self.phase_id = 0
        self.phase_dma_bufs = []
        self.barrier = []
        self.pstack = None
        self.ntile = 0
        self.total_ops = 0

    def ptile(self, shape, dtype, name=None):
        self.ntile += 1
        name = name or f"t{self.ntile}"
        t = self.stack.enter_context(self.nc.sbuf_tensor(name, list(shape), dtype))
        return Tile(self, t, name)

    def tile(self, shape, dtype, name=None):
        self.ntile += 1
        name = (name or "t") + f"_{self.ntile}"
        t = self.pstack.enter_context(self.nc.sbuf_tensor(name, list(shape), dtype))
        return Tile(self, t, name)

    def psum_tile(self, shape, dtype, name):
        t = self.stack.enter_context(self.nc.psum_tensor(name, list(shape), dtype))
        return Tile(self, t, name, psum=True)

    def dbuf(self, name):
        return Buf(name)

    @staticmethod
    def _b(x):
        return x.b if isinstance(x, Tile) else x

    def _track(self, op, reads, writes):
        deps = op.deps
        wr = [self._b(w) for w in writes]
        for r in reads:
            b = self._b(r)
            if b.psum:
                wr.append(b)
                continue
            if b.last_w is not None:
                deps.append(("raw", b.last_w))
            b.readers.append(op)
        for b in wr:
            if b.last_w is not None:
                deps.append(("raw" if b.psum else "waw", b.last_w))
            for r in b.readers:
                if r is not op:
                    deps.append(("war", r))
            b.readers = []
            b.last_w = op

    def op(self, eng, fn, reads=(), writes=()):
        o = Op(eng, fn)
        self._track(o, reads, writes)
        self.ops[eng].append(o)
        return o

    def dma(self, eng, fn, dtile, reads=(), writes=(), group=None, background=False):
        dbuf = self._b(dtile)
        o = Op(eng, fn)
        o.is_dma = True
        o.dbuf = dbuf
        if dbuf.dsem is None:
            ent = self.dpool.pop(0)
            dbuf.dsem = ent
            dbuf.dcount = ent[1]
            if not background:
                self.phase_dma_bufs.append(dbuf)
        if group is None or dbuf.cur_group_id != group or not dbuf.last_group:
            if dbuf.last_group:
                o.deps.append(("dmaser", dbuf.last_group[-1]))
            dbuf.last_group = [o]
            dbuf.cur_group_id = group
            o.dgroup = Group()
        else:
            first = dbuf.last_group[0]
            for k, d in first.deps:
                if k == "dmaser":
                    o.deps.append((k, d))
            o.dgroup = first.dgroup
            dbuf.last_group.append(o)
        dbuf.dcount += 16
        o.dgroup.end = dbuf.dcount
        self._track(o, reads, writes)
        self.ops[eng].append(o)
        return o

    @contextlib.contextmanager
    def phase(self, name):
        self.pstack = contextlib.ExitStack()
        with self.pstack:
            yield
            self._emit()
        self.pstack = None
        self.phase_id += 1

    def _semval(self, e, tk):
        tk -= 1
        return self.esem[e][tk // SEM_CHUNK], tk % SEM_CHUNK + 1

    def _emit(self, final=False):
        nc = self.nc
        for e in ENGS:
            for o in self.ops[e]:
                for kind, d in o.deps:
                    if d.is_dma:
                        continue
                    if d.eng == o.eng and (d.eng == "pe" or (kind != "raw" and not SAME_ENG_ALL)):
                        continue
                    d.has_dep = True
            for o in reversed(self.ops[e]):
                if not o.is_dma:
                    o.has_dep = True
                    break
        newbar = []
        for e in ENGS:
            t = self.ticket[e]
            for o in self.ops[e]:
                if o.has_dep and not o.is_dma:
                    t += 1
                    o.ticket = t
            if t != self.ticket[e]:
                newbar.append(self._semval(e, t))
            self.ticket[e] = t
        barrier = self.barrier

        def run(e, eng):
            waited = {}
            for sem, val in barrier:
                eng.wait_ge(sem, val)
                waited[id(sem)] = val
            for o in self.ops[e]:
                need = {}
                for kind, d in o.deps:
                    if d.is_dma:
                        if o.is_dma and o.dgroup is d.dgroup:
                            continue
                        sem, val = d.dbuf.dsem[0], d.dgroup.end
                    else:
                        if d.eng == e and (e == "pe" or (kind != "raw" and not SAME_ENG_ALL)):
                            continue
                        sem, val = self._semval(d.eng, d.ticket)
                    key = id(sem)
                    if val > need.get(key, (None, 0))[1]:
                        need[key] = (sem, val)
                for key, (sem, val) in need.items():
                    if waited.get(key, 0) >= val:
                        continue
                    waited[key] = val
                    eng.wait_ge(sem, val)
                ins = o.fn(eng)
                if o.is_dma:
                    ins.then_inc(o.dbuf.dsem[0], 16)
                elif o.ticket is not None:
                    sem, _ = self._semval(e, o.ticket)
                    ins.then_inc(sem, 1)
            if final and e == "sp":
                for b in self.phase_dma_bufs:
                    eng.wait_ge(b.dsem[0], b.dcount)

        with nc.Block() as block:
            @block.tensor
            def _(eng):
                run("pe", eng)

            @block.scalar
            def _(eng):
                run("act", eng)

            @block.vector
            def _(eng):
                run("dve", eng)

            @block.gpsimd
            def _(eng):
                run("pool", eng)

            @block.sync
            def _(eng):
                run("sp", eng)
        bar = {id(s): (s, v) for s, v in self.barrier}
        for s, v in newbar:
            bar[id(s)] = (s, v)
        for b in self.phase_dma_bufs:
            b.dsem[1] = b.dcount
            bar[id(b.dsem[0])] = (b.dsem[0], b.dcount)
            self.dpool.append(b.dsem)
        self.barrier = list(bar.values())
        self.total_ops += sum(len(v) for v in self.ops.values())
        self.ops = {e: [] for e in ENGS}
        self.phase_dma_bufs = []


INPUT_SHAPES = {
    "x": [L, D], "mix_pre_g": [2, D], "mix_post_g": [2, D], "mlp_pre_g": [2, D], "mlp_post_g": [2, D],
    "ssm_w_in": [D, D], "ssm_a_re": [128, 64], "ssm_a_im": [128, 64], "ssm_log_dt": [128],
    "ssm_b_re": [128, 64, 16], "ssm_b_im": [128, 64, 16], "ssm_c_re": [128, 16, 64], "ssm_c_im": [128, 16, 64],
    "ssm_d": [D], "ssm_w_glu": [D, 2 * D], "kv_norm_g": [D], "w_kv": [D, 2 * D], "attn_w_q": [D, D],
    "lam_q1": [128], "lam_k1": [128], "lam_q2": [128], "lam_k2": [128], "attn_subln_g": [256],
    "attn_w_o": [D, D], "mlp_w_up": [2, D, DFF], "mlp_w_down": [2, DFF, D],
}
ALL_PHASES = ("conv", "prep", "l0a", "l0b", "l0c", "mlp0", "qkv", "attn", "mlp1")


class Prog:
    def __init__(self, phases=ALL_PHASES, dbg=(), ext_in=()):
        self.phases = phases
        self.dbg = set(dbg)
        self.ext_in = set(ext_in)
        self.nc = bass.Bass("TRN2", target_bir_lowering=False)
        self.s = Sched(self.nc)
        self.inp = {}
        self.scr = {}
        self.outputs = []
        self.dbg_done = False

    def input(self, name):
        if name not in self.inp:
            self.inp[name] = self.nc.dram_tensor(name, INPUT_SHAPES[name], F32, kind="ExternalInput").ap()
        return self.inp[name]

    def scratch(self, name, shape, dtype):
        if name not in self.scr:
            if name in self.ext_in:
                kind = "ExternalInput"
            elif name in self.dbg or name == "out":
                kind = "ExternalOutput"
                self.outputs.append(name)
            else:
                kind = "Internal"
            self.scr[name] = self.nc.dram_tensor(name, list(shape), dtype, kind=kind).ap()
        return self.scr[name]

    def build(self):
        s = self.s
        nc = self.nc
        self.pb = [s.psum_tile([128, 512], F32, f"pb{i}") for i in range(6)]
        self.pt = [s.psum_tile([128, 1024], BF16, f"pt{i}") for i in range(2)]
        self.ident = s.ptile([128, 128], BF16, "ident")
        self.cmask = s.ptile([128, 128], BF16, "cmask")
        self.rcos = s.ptile([128, 16, 64], F32, "rcos")
        self.rsin = s.ptile([128, 16, 64], F32, "rsin")
        self.lam = s.ptile([128, 1], F32, "lam")
        self.gsub = s.ptile([128, 256], F32, "gsub")
        self.consts_ready = False
        ph = self.phases
        if "conv" in ph:
            self.phase_conv()
        if "prep" in ph:
            self.phase_prep()
        if "l0a" in ph:
            self.phase_l0a()
        if "l0b" in ph:
            self.phase_l0b()
        if "l0c" in ph:
            self.phase_l0c()
        if "mlp0" in ph:
            self.phase_mlp(0, "hA", "hB")
        if "qkv" in ph:
            self.phase_qkv()
        if "attn" in ph:
            self.phase_attn()
        if "mlp1" in ph:
            self.phase_mlp(1, "hC", "out")
        with s.phase("final"):
            fin = s.tile([128, 1], F32, "fin")
            s.op("dve", lambda e: e.memset(fin.t[:], 1.0), writes=[fin])
            if self.dbg_done:
                dn = self.nc.dram_tensor("done", [128, 1], F32, kind="ExternalOutput").ap()
                s.dma("sp", lambda e: e.dma_start(out=dn, in_=fin.t[:]), fin, reads=[fin])
        with nc.Block() as block:
            @block.sync
            def _(eng):
                for sem, val in s.barrier:
                    eng.wait_ge(sem, val)
        s.stack.close()
        return nc

    def wb(self, name, shape):
        return self.scratch(name + "_bf", shape, BF16)

    def norm_stats(self, src, rs, junk):
        s = self.s
        ss = rs
        s.op("act", lambda e: e.activation(out=junk.t[:], in_=src.t[:], func=AF.Square, accum_out=ss.t[:]),
             reads=[src], writes=[junk, ss])
        s.op("act", lambda e: e.activation(out=rs.t[:], in_=ss.t[:], func=AF.Sqrt, scale=1.0 / D, bias=self.eps_t.t[:]),
             reads=[ss, self.eps_t], writes=[rs])
        s.op("dve", lambda e: e.reciprocal(out=rs.t[:], in_=rs.t[:]), reads=[rs], writes=[rs])

    def transpose_to(self, xn, dstT, col0, pti):
        s = self.s
        for half in range(2):
            pt = self.pt[(pti + half) % 2]
            for k in range(8):
                kc = half * 8 + k
                s.op("pe", lambda e, pt=pt, k=k, kc=kc: e.transpose(
                    out=pt.t[:, k * 128:(k + 1) * 128], in_=xn.t[:, kc * 128:(kc + 1) * 128], identity=self.ident.t[:]),
                    reads=[xn, self.ident], writes=[pt])
            eng = "act" if half == 0 else "dve"
            if eng == "act":
                s.op("act", lambda e, pt=pt, half=half: e.copy(
                    out=dstT.t[:, half * 8:(half + 1) * 8, col0:col0 + 128],
                    in_=pt.t[:].rearrange("p (k t) -> p k t", k=8)), reads=[pt], writes=[dstT])
            else:
                s.op("dve", lambda e, pt=pt, half=half: e.tensor_copy(
                    out=dstT.t[:, half * 8:(half + 1) * 8, col0:col0 + 128],
                    in_=pt.t[:].rearrange("p (k t) -> p k t", k=8)), reads=[pt], writes=[dstT])

    def load_gain(self, tile_, src_ap):
        self.s.dma("sp", lambda e: e.dma_start(out=tile_.t[:], in_=src_ap.partition_broadcast(128)), tile_, writes=[tile_])

    def post_norm_residual(self, val, hres, gpost, rs, junk, out_ap):
        s = self.s
        self.norm_stats(val, rs, junk)
        s.op("dve", lambda e: e.scalar_tensor_tensor(out=val.t[:], in0=val.t[:], scalar=rs.t[:], in1=gpost.t[:],
                                                     op0=ALU.mult, op1=ALU.mult), reads=[val, rs, gpost], writes=[val])
        s.op("pool", lambda e: e.tensor_tensor(out=val.t[:], in0=val.t[:], in1=hres.t[:], op=ALU.add),
             reads=[val, hres], writes=[val])
        s.dma("sp", lambda e: e.dma_start(out=out_ap, in_=val.t[:]), val, reads=[val])

    def mk_eps(self):
        s = self.s
        self.eps_t = s.tile([128, 1], F32, "eps")
        s.op("pool", lambda e: e.memset(self.eps_t.t[:], EPS), writes=[self.eps_t])

    CONV_SPECS = {"ssm_w_in": ("ssm_w_in", None, D, D), "ssm_w_glu": ("ssm_w_glu", None, D, 2 * D),
                  "mlp_w_up0": ("mlp_w_up", 0, D, DFF), "mlp_w_down0": ("mlp_w_down", 0, DFF, D),
                  "w_kv": ("w_kv", None, D, 2 * D), "attn_w_q": ("attn_w_q", None, D, D), "attn_w_o": ("attn_w_o", None, D, D),
                  "mlp_w_up1": ("mlp_w_up", 1, D, DFF), "mlp_w_down1": ("mlp_w_down", 1, DFF, D)}
    CONV_NEED = {"l0a": ["ssm_w_in"], "l0c": ["ssm_w_glu"], "mlp0": ["mlp_w_up0", "mlp_w_down0"],
                 "qkv": ["w_kv", "attn_w_q"], "attn": ["attn_w_o"], "mlp1": ["mlp_w_up1", "mlp_w_down1"]}
    CONV_PLAN = {"ssm_w_in": "conv", "ssm_w_glu": "l0a", "mlp_w_up0": "l0b", "mlp_w_down0": "l0b", "w_kv": "l0b",
                 "attn_w_q": "l0b", "attn_w_o": "l0b", "mlp_w_up1": "mlp0", "mlp_w_down1": "mlp0"}

    def issue_conv(self, phase):
        s = self.s
        import os
        if phase != "conv":
            return
        wanted = set(sum([self.CONV_NEED.get(p, []) for p in self.phases], []))
        if os.environ.get("CONV_ALL"):
            wanted = set(self.CONV_SPECS)
        self.conv_bufs = {}
        for wname, (name, idx, R, C) in self.CONV_SPECS.items():
            if wname not in wanted:
                continue
            src = self.input(name)
            if idx is not None:
                src = src[idx]
            dst = self.wb(wname, [R, C])
            rows = max(128, (2 * 1024 * 1024) // C)
            b = s.dbuf(f"cv_{wname}")
            self.conv_bufs[wname] = b
            for r0 in range(0, R, rows):
                s.dma("pool", lambda e, r0=r0, src=src, dst=dst, rows=rows: e.dma_start(
                    out=dst[r0:r0 + rows, :], in_=src[r0:r0 + rows, :]), b, group="cv", background=True)

    def conv_wait(self, phase):
        for wname in self.CONV_NEED.get(phase, []):
            b = getattr(self, "conv_bufs", {}).get(wname)
            if b is not None:
                self.s.barrier.append((b.dsem[0], b.dcount))

    def phase_conv(self):
        s = self.s
        with s.phase("conv"):
            self.issue_conv("conv")

    def phase_prep(self):
        s = self.s
        P = self
        with s.phase("prep"):
            ones = s.tile([128, 128], F32, "ones")
            s.op("pool", lambda e: e.memset(ones.t[:], 1.0), writes=[ones])
            s.op("pool", lambda e: e.affine_select(out=P.ident.t[:], in_=ones.t[:], pattern=[[-1, 128]],
                                                   compare_op=ALU.is_equal, fill=0.0, base=0, channel_multiplier=1),
                 reads=[ones], writes=[P.ident])
            s.op("pool", lambda e: e.affine_select(out=P.cmask.t[:], in_=ones.t[:], pattern=[[1, 128]],
                                                   compare_op=ALU.is_ge, fill=0.0, base=0, channel_multiplier=-1),
                 reads=[ones], writes=[P.cmask])
            import os
            parts = os.environ.get("PREP_PARTS", "rope,ssm,lambda").split(",")
            if "rope" in parts:
                self.prep_rope()
            if "ssm" in parts:
                self.prep_ssm()
            if "lambda" in parts:
                self.prep_lambda()
        self.consts_ready = True

    def sincos(self, ang, shape, sin_out, cos_out, tag):
        s = self.s
        n = len(shape)
        ki = s.tile(shape, I32, "ki" + tag)
        kf = s.tile(shape, F32, "kf" + tag)
        m2 = s.tile(shape, F32, "m2" + tag)
        c1 = 6.28125
        c2 = TWO_PI - c1
        lim = 3.1415925
        s.op("dve", lambda e: e.tensor_scalar(out=ki.t[:], in0=ang.t[:], scalar1=1.0 / TWO_PI, scalar2=None, op0=ALU.mult),
             reads=[ang], writes=[ki])
        s.op("dve", lambda e: e.tensor_copy(out=kf.t[:], in_=ki.t[:]), reads=[ki], writes=[kf])
        s.op("dve", lambda e: e.scalar_tensor_tensor(out=ang.t[:], in0=kf.t[:], scalar=-c1, in1=ang.t[:], op0=ALU.mult, op1=ALU.add),
             reads=[kf, ang], writes=[ang])
        s.op("dve", lambda e: e.scalar_tensor_tensor(out=ang.t[:], in0=kf.t[:], scalar=-c2, in1=ang.t[:], op0=ALU.mult, op1=ALU.add),
             reads=[kf, ang], writes=[ang])
        s.op("dve", lambda e: e.tensor_scalar(out=m2.t[:], in0=ang.t[:], scalar1=math.pi / 2, scalar2=-TWO_PI, op0=ALU.is_gt, op1=ALU.mult),
             reads=[ang], writes=[m2])
        s.op("dve", lambda e: e.scalar_tensor_tensor(out=m2.t[:], in0=ang.t[:], scalar=math.pi / 2, in1=m2.t[:], op0=ALU.add, op1=ALU.add),
             reads=[ang, m2], writes=[m2])
        s.op("dve", lambda e: e.tensor_scalar(out=m2.t[:], in0=m2.t[:], scalar1=-lim, scalar2=lim, op0=ALU.max, op1=ALU.min),
             reads=[m2], writes=[m2])
        s.op("dve", lambda e: e.tensor_scalar(out=ang.t[:], in0=ang.t[:], scalar1=-lim, scalar2=lim, op0=ALU.max, op1=ALU.min),
             reads=[ang], writes=[ang])
        s.op("act", lambda e: e.activation(out=sin_out.t[:], in_=ang.t[:], func=AF.Sin), reads=[ang], writes=[sin_out])
        s.op("act", lambda e: e.activation(out=cos_out.t[:], in_=m2.t[:], func=AF.Sin), reads=[m2], writes=[cos_out])

    def prep_rope(self):
        s = self.s
        P = self
        posf = s.tile([128, 16], F32, "posf")
        fidx = s.tile([128, 64], F32, "fidx")
        invf = s.tile([128, 64], F32, "invf")
        ang = s.tile([128, 16, 64], F32, "rang")
        s.op("pool", lambda e: e.iota(posf.t[:], pattern=[[128, 16]], base=0, channel_multiplier=1,
                                      allow_small_or_imprecise_dtypes=True), writes=[posf])
        s.op("pool", lambda e: e.iota(fidx.t[:], pattern=[[1, 64]], base=0, channel_multiplier=0,
                                      allow_small_or_imprecise_dtypes=True), writes=[fidx])
        s.op("act", lambda e: e.activation(out=invf.t[:], in_=fidx.t[:], func=AF.Exp, scale=-math.log(10000.0) / 64.0),
             reads=[fidx], writes=[invf])
        s.op("dve", lambda e: e.tensor_tensor(out=ang.t[:], in0=posf.t[:].unsqueeze(2).to_broadcast([128, 16, 64]),
                                              in1=invf.t[:].unsqueeze(1).to_broadcast([128, 16, 64]), op=ALU.mult),
             reads=[posf, invf], writes=[ang])
        self.sincos(ang, [128, 16, 64], P.rsin, P.rcos, "r")

    def abar(self, are, aim, ldt_b, shape, tag):
        s = self.s
        step = s.tile(shape, F32, "step" + tag)
        sr = s.tile(shape, F32, "sr" + tag)
        si = s.tile(shape, F32, "si" + tag)
        mag = s.tile(shape, F32, "mag" + tag)
        sn = s.tile(shape, F32, "sn" + tag)
        cs = s.tile(shape, F32, "cs" + tag)
        ar = s.tile(shape, F32, "ar" + tag)
        ai = s.tile(shape, F32, "ai" + tag)
        ldt_tile, ldt_ap = ldt_b
        s.op("act", lambda e: e.activation(out=step.t[:], in_=ldt_ap(), func=AF.Exp), reads=[ldt_tile], writes=[step])
        s.op("dve", lambda e: e.tensor_scalar(out=are.t[:], in0=are.t[:], scalar1=-1e-4, scalar2=None, op0=ALU.min),
             reads=[are], writes=[are])
        s.op("dve", lambda e: e.tensor_tensor(out=sr.t[:], in0=step.t[:], in1=are.t[:], op=ALU.mult), reads=[step, are], writes=[sr])
        s.op("dve", lambda e: e.tensor_tensor(out=si.t[:], in0=step.t[:], in1=aim.t[:], op=ALU.mult), reads=[step, aim], writes=[si])
        s.op("act", lambda e: e.activation(out=mag.t[:], in_=sr.t[:], func=AF.Exp), reads=[sr], writes=[mag])
        self.sincos(si, shape, sn, cs, tag)
        s.op("dve", lambda e: e.tensor_tensor(out=ar.t[:], in0=mag.t[:], in1=cs.t[:], op=ALU.mult), reads=[mag, cs], writes=[ar])
        s.op("dve", lambda e: e.tensor_tensor(out=ai.t[:], in0=mag.t[:], in1=sn.t[:], op=ALU.mult), reads=[mag, sn], writes=[ai])
        return ar, ai

    def prep_ssm(self):
        s = self.s
        P = self
        a_re = self.input("ssm_a_re")
        a_im = self.input("ssm_a_im")
        ldt = self.input("ssm_log_dt")
        b_re = self.input("ssm_b_re")
        b_im = self.input("ssm_b_im")
        c_re = self.input("ssm_c_re")
        c_im = self.input("ssm_c_im")
        dsk = self.input("ssm_d")
        onesf = s.tile([128, 128], F32, "onesf")
        identf = s.tile([128, 128], F32, "identf")
        s.op("pool", lambda e: e.memset(onesf.t[:], 1.0), writes=[onesf])
        s.op("pool", lambda e: e.affine_select(out=identf.t[:], in_=onesf.t[:], pattern=[[-1, 128]],
                                               compare_op=ALU.is_equal, fill=0.0, base=0, channel_multiplier=1),
             reads=[onesf], writes=[identf])
        sel = s.tile([128, 64], F32, "sel")
        s.op("pool", lambda e: e.affine_select(out=sel.t[:], in_=onesf.t[:, 0:64], pattern=[[-2, 64]],
                                               compare_op=ALU.is_ge, fill=0.0, base=0, channel_multiplier=1),
             reads=[onesf], writes=[sel])
        s.op("pool", lambda e: e.affine_select(out=sel.t[:], in_=sel.t[:], pattern=[[2, 64]],
                                               compare_op=ALU.is_ge, fill=0.0, base=1, channel_multiplier=-1),
             reads=[sel], writes=[sel])
        pidx = s.tile([128, 1], I32, "pidx")
        pi2 = s.tile([128, 1], I32, "pi2")
        par1 = s.tile([128, 1], F32, "par1")
        par0 = s.tile([128, 1], F32, "par0")
        s.op("pool", lambda e: e.iota(pidx.t[:], pattern=[[0, 1]], base=0, channel_multiplier=1), writes=[pidx])
        s.op("dve", lambda e: e.tensor_scalar(out=pi2.t[:], in0=pidx.t[:], scalar1=1, scalar2=None, op0=ALU.bitwise_and),
             reads=[pidx], writes=[pi2])
        s.op("dve", lambda e: e.tensor_copy(out=par1.t[:], in_=pi2.t[:]), reads=[pi2], writes=[par1])
        s.op("dve", lambda e: e.tensor_scalar(out=par0.t[:], in0=par1.t[:], scalar1=-1.0, scalar2=1.0, op0=ALU.mult, op1=ALU.add),
             reads=[par1], writes=[par0])
        areS = s.tile([128, 64], F32, "areS")
        aimS = s.tile([128, 64], F32, "aimS")
        ldtS = s.tile([128, 64], F32, "ldtS")
        ldtc = s.tile([128, 1], F32, "ldtc")
        s.dma("sp", lambda e: e.dma_start(out=ldtc.t[:], in_=ldt.rearrange("(g o) -> g o", o=1)), ldtc, writes=[ldtc])
        for k, (dstS, srcD) in enumerate(((areS, a_re), (aimS, a_im), (ldtS, None))):
            ext = s.tile([128, 2, 64], F32, "aext")
            if srcD is not None:
                nat = s.tile([128, 64], F32, "anat")
                s.dma("sp", lambda e, nat=nat, srcD=srcD: e.dma_start(out=nat.t[:], in_=srcD), nat, writes=[nat])
                for two, par in enumerate((par0, par1)):
                    s.op("dve", lambda e, ext=ext, nat=nat, two=two, par=par: e.tensor_scalar(
                        out=ext.t[:, two, :], in0=nat.t[:], scalar1=par.t[:], scalar2=None, op0=ALU.mult),
                        reads=[nat, par], writes=[ext])
            else:
                for two, par in enumerate((par0, par1)):
                    s.op("dve", lambda e, ext=ext, two=two, par=par: e.tensor_scalar(
                        out=ext.t[:, two, :], in0=onesf.t[:, 0:64], scalar1=ldtc.t[:], scalar2=par.t[:], op0=ALU.mult, op1=ALU.mult),
                        reads=[onesf, ldtc, par], writes=[ext])
            bank = self.pb[k % 4]
            s.op("pe", lambda e, bank=bank, ext=ext: e.matmul(bank.t[:, 0:64], lhsT=ext.t[:].rearrange("p a n -> p (a n)"),
                                                              rhs=sel.t[:], start=True, stop=True), reads=[ext, sel], writes=[bank])
            s.op("act", lambda e, bank=bank, dstS=dstS: e.copy(out=dstS.t[:], in_=bank.t[:, 0:64]), reads=[bank], writes=[dstS])
        arS, aiS = self.abar(areS, aimS, (ldtS, lambda: ldtS.t[:]), [128, 64], "S")
        P.APW = s.tile([128, 8, 2, 2, 64], F32, "APW")
        APW = P.APW
        pt1 = s.tile([128, 64], F32, "pw1")
        pt2 = s.tile([128, 64], F32, "pw2")
        s.op("dve", lambda e: e.tensor_copy(out=APW.t[:, 0, 0, 0, :], in_=arS.t[:]), reads=[arS], writes=[APW])
        s.op("dve", lambda e: e.tensor_copy(out=APW.t[:, 0, 0, 1, :], in_=aiS.t[:]), reads=[aiS], writes=[APW])
        for m in range(1, 8):
            s.op("dve", lambda e, m=m: e.tensor_tensor(out=pt1.t[:], in0=APW.t[:, m - 1, 0, 0, :], in1=arS.t[:], op=ALU.mult),
                 reads=[APW, arS], writes=[pt1])
            s.op("dve", lambda e, m=m: e.tensor_tensor(out=pt2.t[:], in0=APW.t[:, m - 1, 0, 1, :], in1=aiS.t[:], op=ALU.mult),
                 reads=[APW, aiS], writes=[pt2])
            s.op("dve", lambda e, m=m: e.tensor_tensor(out=APW.t[:, m, 0, 0, :], in0=pt1.t[:], in1=pt2.t[:], op=ALU.subtract),
                 reads=[pt1, pt2], writes=[APW])
            s.op("dve", lambda e, m=m: e.tensor_tensor(out=pt1.t[:], in0=APW.t[:, m - 1, 0, 0, :], in1=aiS.t[:], op=ALU.mult),
                 reads=[APW, aiS], writes=[pt1])
            s.op("dve", lambda e, m=m: e.tensor_tensor(out=pt2.t[:], in0=APW.t[:, m - 1, 0, 1, :], in1=arS.t[:], op=ALU.mult),
                 reads=[APW, arS], writes=[pt2])
            s.op("dve", lambda e, m=m: e.tensor_tensor(out=APW.t[:, m, 0, 1, :], in0=pt1.t[:], in1=pt2.t[:], op=ALU.add),
                 reads=[pt1, pt2], writes=[APW])
        s.op("dve", lambda e: e.tensor_scalar(out=APW.t[:, :, 1, 0, :], in0=APW.t[:, :, 0, 1, :], scalar1=-1.0, scalar2=None, op0=ALU.mult),
             reads=[APW], writes=[APW])
        s.op("dve", lambda e: e.tensor_copy(out=APW.t[:, :, 1, 1, :], in_=APW.t[:, :, 0, 0, :]), reads=[APW], writes=[APW])
        if self._stage() < 1:
            return
        shS = [128, 64]
        den = s.tile(shS, F32, "den")
        t1 = s.tile(shS, F32, "t1")
        cre = s.tile(shS, F32, "cre")
        cim = s.tile(shS, F32, "cim")
        nr = s.tile(shS, F32, "nr")
        TT = lambda out, a, b, op: s.op("dve", lambda e: e.tensor_tensor(out=out.t[:], in0=a.t[:], in1=b.t[:], op=op),
                                        reads=[a, b], writes=[out])
        TT(den, areS, areS, ALU.mult)
        TT(t1, aimS, aimS, ALU.mult)
        TT(den, den, t1, ALU.add)
        s.op("dve", lambda e: e.reciprocal(out=den.t[:], in_=den.t[:]), reads=[den], writes=[den])
        s.op("dve", lambda e: e.tensor_scalar(out=nr.t[:], in0=arS.t[:], scalar1=-1.0, scalar2=None, op0=ALU.add),
             reads=[arS], writes=[nr])
        TT(cre, nr, areS, ALU.mult)
        TT(t1, aiS, aimS, ALU.mult)
        TT(cre, cre, t1, ALU.add)
        TT(cre, cre, den, ALU.mult)
        TT(cim, aiS, areS, ALU.mult)
        TT(t1, nr, aimS, ALU.mult)
        TT(cim, cim, t1, ALU.subtract)
        TT(cim, cim, den, ALU.mult)
        if self._stage() < 2:
            return
        shB = [128, 64, 16]
        bS = [s.tile(shB, F32, "bSre"), s.tile(shB, F32, "bSim")]
        for bt, bsrc in zip(bS, (b_re, b_im)):
            b2 = bsrc.rearrange("(j two) n q -> two n j q", two=2)
            for two in range(2):
                for jq in range(4):
                    s.dma("sp", lambda e, bt=bt, b2=b2, two=two, jq=jq: e.dma_start(
                        out=bt.t[two * 64:(two + 1) * 64, jq * 16:(jq + 1) * 16, :], in_=b2[two][:, jq * 16:(jq + 1) * 16, :]),
                        bt, writes=[bt], group="b")
        bb = [s.tile(shB, F32, "bbre"), s.tile(shB, F32, "bbim")]
        tb = s.tile(shB, F32, "tb")
        bc = lambda t_: t_.t[:].unsqueeze(2).to_broadcast(shB)
        TB = lambda out, co, bsrc, op=ALU.mult: s.op("dve", lambda e: e.tensor_tensor(out=out.t[:], in0=bsrc.t[:], in1=bc(co), op=op),
                                                     reads=[bsrc, co], writes=[out])
        TB(bb[0], cre, bS[0])
        TB(tb, cim, bS[1])
        TT(bb[0], bb[0], tb, ALU.subtract)
        TB(bb[1], cre, bS[1])
        TB(tb, cim, bS[0])
        TT(bb[1], bb[1], tb, ALU.add)
        if self._stage() < 3:
            return
        P.BW = [s.tile([128, 16, 128], F32, "BWre"), s.tile([128, 16, 128], F32, "BWim")]
        tcnt = 0
        for bw, bsrc in zip(P.BW, bb):
            bext = s.tile([128, 64, 2, 16], F32, "bext")
            s.op("pool", lambda e, bext=bext: e.memset(bext.t[:], 0.0), writes=[bext])
            s.op("dve", lambda e, bext=bext, bsrc=bsrc: e.tensor_copy(out=bext.t[0:64, :, 0, :], in_=bsrc.t[0:64, :, :]),
                 reads=[bsrc], writes=[bext])
            s.op("dve", lambda e, bext=bext, bsrc=bsrc: e.tensor_copy(out=bext.t[64:128, :, 1, :], in_=bsrc.t[64:128, :, :]),
                 reads=[bsrc], writes=[bext])
            for c in range(16):
                bank = self.pb[tcnt % 4]
                tcnt += 1
                s.op("pe", lambda e, bank=bank, bext=bext, c=c: e.transpose(
                    out=bank.t[:, 0:128], in_=bext.t[:, 4 * c:4 * c + 4, :, :].rearrange("p a b q -> p (a b q)"),
                    identity=identf.t[:]), reads=[bext, identf], writes=[bank])
                s.op("act", lambda e, bank=bank, bw=bw, c=c: e.copy(out=bw.t[:, c, :], in_=bank.t[:, 0:128]),
                     reads=[bank], writes=[bw])
        if self._stage() < 4:
            return
        pi = s.tile([128, 1], I32, "pidx4")
        m1 = s.tile([128, 1], F32, "m1")
        m0 = s.tile([128, 1], F32, "m0")
        s.op("dve", lambda e: e.tensor_scalar(out=pi.t[:], in0=pidx.t[:], scalar1=4, scalar2=1, op0=ALU.arith_shift_right,
                                              op1=ALU.bitwise_and), reads=[pidx], writes=[pi])
        s.op("dve", lambda e: e.tensor_copy(out=m1.t[:], in_=pi.t[:]), reads=[pi], writes=[m1])
        s.op("dve", lambda e: e.tensor_scalar(out=m0.t[:], in0=m1.t[:], scalar1=-1.0, scalar2=1.0, op0=ALU.mult, op1=ALU.add),
             reads=[m1], writes=[m0])
        P.CW = [s.tile([128, 64, 32], F32, "CWre"), s.tile([128, 64, 32], F32, "CWim")]
        for cw, csrc, sgn in ((P.CW[0], c_re, 1.0), (P.CW[1], c_im, -1.0)):
            cx = s.tile([128, 16, 64], F32, "cx")
            cext = s.tile([128, 16, 2, 64], F32, "cext")
            cv = csrc.rearrange("(c r) p n -> (r p) c n", r=8)
            for cq in range(4):
                s.dma("sp", lambda e, cx=cx, cv=cv, cq=cq: e.dma_start(out=cx.t[:, cq * 4:(cq + 1) * 4, :], in_=cv[:, cq * 4:(cq + 1) * 4, :]),
                      cx, writes=[cx], group="c")
            s.op("dve", lambda e, cx=cx, cext=cext, sgn=sgn: e.tensor_scalar(out=cext.t[:, :, 0, :], in0=cx.t[:], scalar1=m0.t[:],
                                                                           scalar2=sgn, op0=ALU.mult, op1=ALU.mult),
                 reads=[cx, m0], writes=[cext])
            s.op("dve", lambda e, cx=cx, cext=cext, sgn=sgn: e.tensor_scalar(out=cext.t[:, :, 1, :], in0=cx.t[:], scalar1=m1.t[:],
                                                                           scalar2=sgn, op0=ALU.mult, op1=ALU.mult),
                 reads=[cx, m1], writes=[cext])
            for c in range(16):
                bank = self.pb[tcnt % 4]
                tcnt += 1
                s.op("pe", lambda e, bank=bank, cext=cext, c=c: e.transpose(
                    out=bank.t[:, 0:128], in_=cext.t[:, c, :, :].rearrange("p a n -> p (a n)"),
                    identity=identf.t[:]), reads=[cext, identf], writes=[bank])
                s.op("act", lambda e, bank=bank, cw=cw, c=c: e.copy(
                    out=cw.t[:, 4 * c:4 * c + 4, :], in_=bank.t[:, 0:128].rearrange("p (a m) -> p a m", a=4)),
                    reads=[bank], writes=[cw])
        if self._stage() < 5:
            return
        P.Dcol = s.tile([128, 16], F32, "Dcol")
        dnat = s.tile([16, 128], F32, "dnat")
        s.dma("sp", lambda e: e.dma_start(out=dnat.t[:], in_=dsk.rearrange("(c p) -> c p", p=128)), dnat, writes=[dnat])
        bank = self.pb[tcnt % 4]
        s.op("pe", lambda e: e.transpose(out=bank.t[:, 0:16], in_=dnat.t[:], identity=identf.t[0:16, 0:16]),
             reads=[dnat, identf], writes=[bank])
        s.op("act", lambda e: e.copy(out=P.Dcol.t[:], in_=bank.t[:, 0:16]), reads=[bank], writes=[P.Dcol])
        if self._stage() < 6:
            return
        for nm, tl in self.ssm_const_list():
            dst = self.scratch(nm, [128, int(np.prod(list(tl.t.shape)[1:]))], F32)
            s.dma("sp", lambda e, dst=dst, tl=tl: e.dma_start(out=dst, in_=self.flat2(tl)), tl, reads=[tl])

    def _stage(self):
        import os
        return int(os.environ.get("SSM_STAGE", "99"))

    @staticmethod
    def flat2(tl):
        n = len(list(tl.t.shape))
        if n == 2:
            return tl.t[:]
        if n == 3:
            return tl.t[:].rearrange("p a b -> p (a b)")
        if n == 5:
            return tl.t[:].rearrange("p a b c d -> p (a b c d)")
        return tl.t[:].rearrange("p a b c -> p (a b c)")

    def ssm_const_list(self):
        P = self
        return [("c_APW", P.APW), ("c_BWre", P.BW[0]), ("c_BWim", P.BW[1]),
                ("c_CWre", P.CW[0]), ("c_CWim", P.CW[1]), ("c_D", P.Dcol)]

    def load_ssm_consts(self):
        s = self.s
        P = self
        P.APW = s.tile([128, 8, 2, 2, 64], F32, "APW")
        P.BW = [s.tile([128, 16, 128], F32, "BWre"), s.tile([128, 16, 128], F32, "BWim")]
        P.CW = [s.tile([128, 64, 32], F32, "CWre"), s.tile([128, 64, 32], F32, "CWim")]
        P.Dcol = s.tile([128, 16], F32, "Dcol")
        for nm, tl in self.ssm_const_list():
            src_ = self.scratch(nm, [128, int(np.prod(list(tl.t.shape)[1:]))], F32)
            s.dma("sp", lambda e, src_=src_, tl=tl: e.dma_start(out=self.flat2(tl), in_=src_), tl, writes=[tl])

    def prep_lambda(self):
        s = self.s
        P = self
        lam_init = 0.8 - 0.6 * math.exp(-0.3 * 1)
        P.lam_init = lam_init
        tl = [s.tile([128, 128], F32, f"lam{i}") for i in range(4)]
        for t_, nm in zip(tl, ("lam_q1", "lam_k1", "lam_q2", "lam_k2")):
            self.load_gain(t_, self.input(nm))
        d1 = s.tile([128, 1], F32, "d1")
        d2 = s.tile([128, 1], F32, "d2")
        jk = s.tile([128, 128], F32, "ljk")
        s.op("dve", lambda e: e.tensor_tensor(out=jk.t[:], in0=tl[0].t[:], in1=tl[1].t[:], op=ALU.mult), reads=[tl[0], tl[1]], writes=[jk])
        s.op("dve", lambda e: e.reduce_sum(out=d1.t[:], in_=jk.t[:], axis=mybir.AxisListType.X), reads=[jk], writes=[d1])
        s.op("dve", lambda e: e.tensor_tensor(out=jk.t[:], in0=tl[2].t[:], in1=tl[3].t[:], op=ALU.mult), reads=[tl[2], tl[3]], writes=[jk])
        s.op("dve", lambda e: e.reduce_sum(out=d2.t[:], in_=jk.t[:], axis=mybir.AxisListType.X), reads=[jk], writes=[d2])
        s.op("act", lambda e: e.activation(out=d1.t[:], in_=d1.t[:], func=AF.Exp), reads=[d1], writes=[d1])
        s.op("act", lambda e: e.activation(out=d2.t[:], in_=d2.t[:], func=AF.Exp), reads=[d2], writes=[d2])
        s.op("dve", lambda e: e.scalar_tensor_tensor(out=P.lam.t[:], in0=d1.t[:], scalar=lam_init, in1=d2.t[:], op0=ALU.add,
                                                     op1=ALU.subtract), reads=[d1, d2], writes=[P.lam])
        self.load_gain(P.gsub, self.input("attn_subln_g"))
        s.op("dve", lambda e: e.tensor_scalar(out=P.gsub.t[:], in0=P.gsub.t[:], scalar1=1.0 - lam_init, scalar2=None, op0=ALU.mult),
             reads=[P.gsub], writes=[P.gsub])

    def phase_l0a(self):
        s = self.s
        x = self.input("x")
        win = self.wb("ssm_w_in", [D, D]).rearrange("(kc p) n -> p kc n", p=128)
        UT = self.scratch("UT", [16, 128, L], F32)
        with s.phase("l0a"):
            self.conv_wait("l0a")
            self.mk_eps()
            gpre = s.tile([128, D], F32, "gpre")
            self.load_gain(gpre, self.input("mix_pre_g")[0])
            hb = [s.tile([128, D], F32, "hb") for _ in range(2)]
            junk = s.tile([128, D], BF16, "junk")
            xn = [s.tile([128, D], BF16, "xn") for _ in range(2)]
            rs = [s.tile([128, 1], F32, "rs") for _ in range(2)]
            xnT = s.tile([128, 16, 512], BF16, "xnT")
            wt = [s.tile([128, 16, 512], BF16, "win") for _ in range(4)]
            for cg in range(4):
                s.dma("sp", lambda e, cg=cg: e.dma_start(out=wt[cg].t[:], in_=win[:, :, cg * 512:(cg + 1) * 512]), wt[cg], writes=[wt[cg]])
            ut = [s.tile([128, 16, 512], F32, "ut") for _ in range(1)]
            cnt = 0
            wc = 0
            for blk in range(4):
                for ti in range(4):
                    tok = blk * 512 + ti * 128
                    h = hb[cnt % 2]
                    s.dma("sp", lambda e, h=h, tok=tok: e.dma_start(out=h.t[:], in_=x[tok:tok + 128, :]), h, writes=[h])
                    r = rs[cnt % 2]
                    xx = xn[cnt % 2]
                    self.norm_stats(h, r, junk)
                    s.op("dve", lambda e, xx=xx, h=h, r=r: e.scalar_tensor_tensor(
                        out=xx.t[:], in0=h.t[:], scalar=r.t[:], in1=gpre.t[:], op0=ALU.mult, op1=ALU.mult),
                        reads=[h, r, gpre], writes=[xx])
                    self.transpose_to(xx, xnT, ti * 128, 0)
                    cnt += 1
                u = ut[0]
                for cg in range(4):
                    w = wt[cg]
                    for j in range(4):
                        c = cg * 4 + j
                        bank = self.pb[c % 2]
                        for kc in range(16):
                            s.op("pe", lambda e, bank=bank, w=w, kc=kc, j=j: e.matmul(
                                bank.t[:], lhsT=w.t[:, kc, j * 128:(j + 1) * 128], rhs=xnT.t[:, kc, :],
                                start=(kc == 0), stop=(kc == 15)), reads=[w, xnT], writes=[bank])
                        s.op("act", lambda e, bank=bank, c=c: e.copy(out=u.t[:, c, :], in_=bank.t[:]), reads=[bank], writes=[u])
                s.dma("sp", lambda e, blk=blk: e.dma_start(out=UT[:, :, blk * 512:(blk + 1) * 512].rearrange("c p t -> p c t"),
                                                         in_=u.t[:]), u, reads=[u])

    def phase_l0b(self):
        s = self.s
        P = self
        UT = self.scratch("UT", [16, 128, L], F32)
        ZT = self.scratch("ZT", [16, 128, L], BF16)
        TS = 64
        R = 8
        NB = TS // R
        UB = 128
        with s.phase("l0b"):
            self.load_ssm_consts()
            ut = [s.tile([128, 16, UB], F32, "utb") for _ in range(2)]
            xb = [s.tile([128, TS, 2, 64], F32, "xb") for _ in range(2)]
            xr = [[s.dbuf(f"xr{i}_{r}") for r in range(R)] for i in range(2)]
            zt = [s.tile([128, 16, 512], BF16, "ztb") for _ in range(2)]
            zero = s.tile([128, 2, 64], F32, "zero")
            s.op("pool", lambda e: e.memset(zero.t[:], 0.0), writes=[zero])
            T1 = s.tile([128, NB, 2, 64], F32, "T1")
            T2 = s.tile([128, NB, 2, 64], F32, "T2")
            F1 = s.tile([128, NB, 2, 64], F32, "F1")
            F2 = s.tile([128, NB, 2, 64], F32, "F2")
            CAR = s.tile([128, NB, 2, 64], F32, "CAR")
            c1 = s.tile([128, 2, 64], F32, "c1")
            c2 = s.tile([128, 2, 64], F32, "c2")
            last = [s.tile([128, 2, 64], F32, "last") for _ in range(2)]
            tmp = [s.tile([128, 8, TS], F32, "gtmp") for _ in range(2)]
            yv = [s.tile([128, 8, TS], F32, "gyv") for _ in range(2)]
            sq = [s.tile([128, 8, TS], F32, "gsq") for _ in range(2)]
            sg = [s.tile([128, 8, TS], F32, "gsg") for _ in range(2)]
            APW, BW, CW, Dcol = P.APW, P.BW, P.CW, P.Dcol
            sh4 = [128, NB, 2, 64]
            ybank = [self.pb[4], self.pb[5]]
            gi = 0

            def emit_bu(tc):
                ub = (tc * TS) // UB
                t0 = (tc * TS) % UB
                u = ut[ub % 2]
                if t0 == 0:
                    s.dma("sp", lambda e, u=u, ub=ub: e.dma_start(
                        out=u.t[:], in_=UT[:, :, ub * UB:(ub + 1) * UB].rearrange("c p t -> p c t")), u, writes=[u])
                X = xb[tc % 2]
                XR = xr[tc % 2]
                for cgrp in range(4):
                    for cp in range(4):
                        c = cgrp * 4 + cp
                        for reim in range(2):
                            for r in range(4):
                                bank = self.pb[r]
                                o0 = (cp * 2 + reim) * TS
                                s.op("pe", lambda e, bank=bank, o0=o0, r=r, c=c, reim=reim, u=u, t0=t0: e.matmul(
                                    bank.t[:, o0:o0 + TS], lhsT=BW[reim].t[32 * r:32 * r + 32, c, :],
                                    rhs=u.t[32 * r:32 * r + 32, c, t0:t0 + TS], start=True, stop=True,
                                    tile_position=(32 * r, 0)), reads=[BW[reim], u], writes=[bank])
                    for r in range(4):
                        bank = self.pb[r]
                        j0 = 16 * cgrp + r
                        s.op("act", lambda e, bank=bank, j0=j0, X=X: e.copy(
                            out=X.t[:, :, :, j0:j0 + 13:4].rearrange("p t r c -> p c r t"),
                            in_=bank.t[:].rearrange("p (c r t) -> p c r t", c=4, r=2)), reads=[bank], writes=XR)

            def emit_rest(tc):
                nonlocal gi
                blk = tc // 8
                ub = (tc * TS) // UB
                t0 = (tc * TS) % UB
                z0 = (tc % 8) * TS
                u = ut[ub % 2]
                z = zt[blk % 2]
                X = xb[tc % 2]
                XR = xr[tc % 2]
                Xv = X.t[:].rearrange("p (k r) c j -> p k r c j", r=R)
                A1A = APW.t[:, 0, 0, :, :].unsqueeze(1).to_broadcast(sh4)
                A1B = APW.t[:, 0, 1, :, :].unsqueeze(1).to_broadcast(sh4)
                for r in range(1, R):
                    pre = Xv[:, :, r - 1, 0, :].unsqueeze(2).to_broadcast(sh4)
                    pim = Xv[:, :, r - 1, 1, :].unsqueeze(2).to_broadcast(sh4)
                    s.op("dve", lambda e, pre=pre: e.tensor_tensor(out=T1.t[:], in0=A1A, in1=pre, op=ALU.mult),
                         reads=[APW, XR[r - 1]], writes=[T1])
                    s.op("dve", lambda e, pim=pim: e.tensor_tensor(out=T2.t[:], in0=A1B, in1=pim, op=ALU.mult),
                         reads=[APW, XR[r - 1]], writes=[T2])
                    s.op("dve", lambda e: e.tensor_tensor(out=T1.t[:], in0=T1.t[:], in1=T2.t[:], op=ALU.add),
                         reads=[T1, T2], writes=[T1])
                    s.op("dve", lambda e, r=r, Xv=Xv: e.tensor_tensor(out=Xv[:, :, r, :, :], in0=Xv[:, :, r, :, :], in1=T1.t[:], op=ALU.add),
                         reads=[XR[r], T1], writes=[XR[r]])
                if tc == 0:
                    prev_b, prev_re, prev_im, prev_full = zero, zero.t[:, 0, :], zero.t[:, 1, :], zero.t[:]
                else:
                    Lp = last[(tc - 1) % 2]
                    prev_b = Lp
                    prev_re, prev_im, prev_full = Lp.t[:, 0, :], Lp.t[:, 1, :], Lp.t[:]
                carry_src = (prev_b, prev_full)
                for k in range(NB):
                    if k > 0:
                        prev_b = XR[R - 1]
                        prev_re, prev_im = X.t[:, k * R - 1, 0, :], X.t[:, k * R - 1, 1, :]
                    s.op("dve", lambda e, prev_re=prev_re: e.tensor_tensor(
                        out=c1.t[:], in0=APW.t[:, R - 1, 0, :, :], in1=prev_re.unsqueeze(1).to_broadcast([128, 2, 64]), op=ALU.mult),
                        reads=[APW, prev_b], writes=[c1])
                    s.op("dve", lambda e, prev_im=prev_im: e.tensor_tensor(
                        out=c2.t[:], in0=APW.t[:, R - 1, 1, :, :], in1=prev_im.unsqueeze(1).to_broadcast([128, 2, 64]), op=ALU.mult),
                        reads=[APW, prev_b], writes=[c2])
                    s.op("dve", lambda e: e.tensor_tensor(out=c1.t[:], in0=c1.t[:], in1=c2.t[:], op=ALU.add),
                         reads=[c1, c2], writes=[c1])
                    tk = k * R + R - 1
                    s.op("dve", lambda e, X=X, tk=tk: e.tensor_tensor(out=X.t[:, tk, :, :], in0=X.t[:, tk, :, :], in1=c1.t[:], op=ALU.add),
                         reads=[XR[R - 1], c1], writes=[XR[R - 1]])
                s.op("dve", lambda e, X=X, tc=tc: e.tensor_copy(out=last[tc % 2].t[:], in_=X.t[:, TS - 1, :, :]),
                     reads=[XR[R - 1]], writes=[last[tc % 2]])
                s.op(FIX_ENG, lambda e, src_=carry_src[1]: e.tensor_copy(out=CAR.t[:, 0, :, :], in_=src_),
                     reads=[carry_src[0]], writes=[CAR])
                s.op(FIX_ENG, lambda e, Xv=Xv: e.tensor_copy(out=CAR.t[:, 1:NB, :, :], in_=Xv[:, 0:NB - 1, R - 1, :, :]),
                     reads=[XR[R - 1]], writes=[CAR])
                cre = CAR.t[:, :, 0, :].unsqueeze(2).to_broadcast(sh4)
                cim = CAR.t[:, :, 1, :].unsqueeze(2).to_broadcast(sh4)
                for r in range(R - 1):
                    ApA = APW.t[:, r, 0, :, :].unsqueeze(1).to_broadcast(sh4)
                    ApB = APW.t[:, r, 1, :, :].unsqueeze(1).to_broadcast(sh4)
                    s.op(FIX_ENG, lambda e, ApA=ApA: e.tensor_tensor(out=F1.t[:], in0=ApA, in1=cre, op=ALU.mult),
                         reads=[APW, CAR], writes=[F1])
                    s.op(FIX_ENG, lambda e, ApB=ApB: e.tensor_tensor(out=F2.t[:], in0=ApB, in1=cim, op=ALU.mult),
                         reads=[APW, CAR], writes=[F2])
                    s.op(FIX_ENG, lambda e: e.tensor_tensor(out=F1.t[:], in0=F1.t[:], in1=F2.t[:], op=ALU.add),
                         reads=[F1, F2], writes=[F1])
                    s.op(FIX_ENG, lambda e, r=r, Xv=Xv: e.tensor_tensor(out=Xv[:, :, r, :, :], in0=Xv[:, :, r, :, :], in1=F1.t[:], op=ALU.add),
                         reads=[XR[r], F1], writes=[XR[r]])
                for half in range(2):
                    yb = ybank[half]
                    for cp in range(8):
                        c = half * 8 + cp
                        for r in range(4):
                            j = 4 * c + r
                            for reim in range(2):
                                s.op("pe", lambda e, yb=yb, cp=cp, r=r, j=j, reim=reim, X=X: e.matmul(
                                    yb.t[32 * r:32 * r + 32, cp * TS:(cp + 1) * TS], lhsT=CW[reim].t[:, j, :],
                                    rhs=X.t[:, :, reim, j], start=(reim == 0), stop=(reim == 1),
                                    tile_position=(0, 32 * r)), reads=[CW[reim]] + XR, writes=[yb])
                    tm, y_, sq_, sg_ = tmp[gi % 2], yv[gi % 2], sq[gi % 2], sg[gi % 2]
                    gi += 1
                    cs = slice(half * 8, half * 8 + 8)
                    s.op("pool", lambda e, tm=tm, u=u, cs=cs, t0=t0: e.tensor_tensor(
                        out=tm.t[:], in0=u.t[:, cs, t0:t0 + TS], in1=Dcol.t[:, cs].unsqueeze(2).to_broadcast([128, 8, TS]),
                        op=ALU.mult), reads=[u, Dcol], writes=[tm])
                    s.op("act", lambda e, y_=y_, yb=yb: e.copy(out=y_.t[:], in_=yb.t[:].rearrange("p (c t) -> p c t", c=8)),
                         reads=[yb], writes=[y_])
                    s.op("pool", lambda e, y_=y_, tm=tm: e.tensor_tensor(out=y_.t[:], in0=y_.t[:], in1=tm.t[:], op=ALU.add),
                         reads=[y_, tm], writes=[y_])
                    s.op("act", lambda e, y_=y_, sq_=sq_: e.activation(out=sq_.t[:], in_=y_.t[:], func=AF.Square),
                         reads=[y_], writes=[sq_])
                    s.op("pool", lambda e, sq_=sq_: e.tensor_scalar(out=sq_.t[:], in0=sq_.t[:], scalar1=0.044715, scalar2=1.0,
                                                                   op0=ALU.mult, op1=ALU.add), reads=[sq_], writes=[sq_])
                    s.op("pool", lambda e, sq_=sq_, y_=y_: e.tensor_tensor(out=sq_.t[:], in0=sq_.t[:], in1=y_.t[:], op=ALU.mult),
                         reads=[sq_, y_], writes=[sq_])
                    s.op("act", lambda e, sq_=sq_, sg_=sg_: e.activation(out=sg_.t[:], in_=sq_.t[:], func=AF.Sigmoid,
                                                                         scale=2.0 * math.sqrt(2.0 / math.pi)),
                         reads=[sq_], writes=[sg_])
                    s.op("pool", lambda e, sg_=sg_, y_=y_, z=z, cs=cs, z0=z0: e.tensor_tensor(
                        out=z.t[:, cs, z0:z0 + TS], in0=sg_.t[:], in1=y_.t[:], op=ALU.mult), reads=[sg_, y_], writes=[z])
                if tc % 8 == 7:
                    s.dma("sp", lambda e, z=z, blk=blk: e.dma_start(
                        out=ZT[:, :, blk * 512:(blk + 1) * 512].rearrange("c p t -> p c t"), in_=z.t[:]), z, reads=[z])

            NCH = L // TS
            emit_bu(0)
            for tc in range(NCH):
                if tc + 1 < NCH:
                    emit_bu(tc + 1)
                emit_rest(tc)

    def phase_l0c(self):
        s = self.s
        x = self.input("x")
        ZT = self.scratch("ZT", [16, 128, L], BF16)
        wglu = self.wb("ssm_w_glu", [D, 2 * D]).rearrange("(kc p) n -> p kc n", p=128)
        hA = self.scratch("hA", [L, D], F32)
        with s.phase("l0c"):
            self.conv_wait("l0c")
            self.mk_eps()
            gpost = s.tile([128, D], F32, "gpost")
            self.load_gain(gpost, self.input("mix_post_g")[0])
            zt = [s.tile([128, 16, 512], BF16, "zt") for _ in range(2)]
            wv = [s.tile([128, 16, 512], BF16, "wv") for _ in range(2)]
            wg = [s.tile([128, 16, 512], BF16, "wg") for _ in range(2)]
            sgt = [s.tile([128, 512], F32, "sgt") for _ in range(2)]
            mix = [s.tile([128, D], F32, "mix") for _ in range(8)]
            hb = [s.tile([128, D], F32, "hb") for _ in range(2)]
            junk = s.tile([128, D], BF16, "junk")
            rs = [s.tile([128, 1], F32, "rs") for _ in range(2)]
            wc = 0
            k = 0
            cnt = 0
            for bp in range(2):
                for b2 in range(2):
                    blk = bp * 2 + b2
                    z = zt[b2]
                    s.dma("sp", lambda e, z=z, blk=blk: e.dma_start(
                        out=z.t[:], in_=ZT[:, :, blk * 512:(blk + 1) * 512].rearrange("c p t -> p c t")), z, writes=[z])
                for j in range(4):
                    a, g = wv[wc % 2], wg[wc % 2]
                    wc += 1
                    s.dma("sp", lambda e, a=a, j=j: e.dma_start(out=a.t[:], in_=wglu[:, :, j * 512:(j + 1) * 512]), a, writes=[a])
                    s.dma("sp", lambda e, g=g, j=j: e.dma_start(out=g.t[:], in_=wglu[:, :, D + j * 512:D + (j + 1) * 512]), g, writes=[g])
                    for b2 in range(2):
                        z = zt[b2]
                        for ti in range(4):
                            pv, pg = self.pb[(k % 2) * 2], self.pb[(k % 2) * 2 + 1]
                            st = sgt[k % 2]
                            k += 1
                            for kc in range(16):
                                s.op("pe", lambda e, pg=pg, z=z, kc=kc, ti=ti, g=g: e.matmul(
                                    pg.t[:], lhsT=z.t[:, kc, ti * 128:(ti + 1) * 128], rhs=g.t[:, kc, :],
                                    start=(kc == 0), stop=(kc == 15)), reads=[z, g], writes=[pg])
                            for kc in range(16):
                                s.op("pe", lambda e, pv=pv, z=z, kc=kc, ti=ti, a=a: e.matmul(
                                    pv.t[:], lhsT=z.t[:, kc, ti * 128:(ti + 1) * 128], rhs=a.t[:, kc, :],
                                    start=(kc == 0), stop=(kc == 15)), reads=[z, a], writes=[pv])
                            s.op("act", lambda e, st=st, pg=pg: e.activation(out=st.t[:], in_=pg.t[:], func=AF.Sigmoid),
                                 reads=[pg], writes=[st])
                            m = mix[b2 * 4 + ti]
                            s.op("dve", lambda e, m=m, pv=pv, st=st, j=j: e.tensor_tensor(
                                out=m.t[:, j * 512:(j + 1) * 512], in0=pv.t[:], in1=st.t[:], op=ALU.mult),
                                reads=[pv, st], writes=[m])
                for b2 in range(2):
                    blk = bp * 2 + b2
                    for ti in range(4):
                        tok = blk * 512 + ti * 128
                        h = hb[cnt % 2]
                        r = rs[cnt % 2]
                        cnt += 1
                        s.dma("sp", lambda e, h=h, tok=tok: e.dma_start(out=h.t[:], in_=x[tok:tok + 128, :]), h, writes=[h])
                        self.post_norm_residual(mix[b2 * 4 + ti], h, gpost, r, junk, hA[tok:tok + 128, :])

    def phase_mlp(self, layer, hin_name, hout_name):
        s = self.s
        hin = self.scratch(hin_name, [L, D], F32)
        hout = self.scratch(hout_name, [L, D], F32)
        wup = self.wb(f"mlp_w_up{layer}", [D, DFF]).rearrange("(kc p) n -> p kc n", p=128)
        wdn = self.wb(f"mlp_w_down{layer}", [DFF, D]).rearrange("(fc p) n -> p fc n", p=128)
        with s.phase(f"mlp{layer}"):
            self.conv_wait(f"mlp{layer}")
            self.mk_eps()
            gpre = s.tile([128, D], F32, "gpre")
            gpost = s.tile([128, D], F32, "gpost")
            self.load_gain(gpre, self.input("mlp_pre_g")[layer])
            self.load_gain(gpost, self.input("mlp_post_g")[layer])
            hb = [s.tile([128, D], F32, "hb") for _ in range(2)]
            junk = s.tile([128, D], BF16, "junk")
            xn = [s.tile([128, D], BF16, "xn") for _ in range(2)]
            rs = [s.tile([128, 1], F32, "rs") for _ in range(2)]
            xnT = s.tile([128, 16, 512], BF16, "xnT")
            hidT = s.tile([128, 64, 512], BF16, "hidT")
            wu = [s.tile([128, 16, 256], BF16, "wu") for _ in range(2)]
            wd = [s.tile([128, 8, 512], BF16, "wd") for _ in range(2)]
            r32 = [s.tile([128, 512], F32, "r32") for _ in range(2)]
            ff = [s.tile([128, D], F32, "ff") for _ in range(4)]
            cnt = 0
            uc = 0
            dc = 0
            fcc = 0
            for blk in range(4):
                for ti in range(4):
                    tok = blk * 512 + ti * 128
                    h = hb[cnt % 2]
                    r = rs[cnt % 2]
                    xx = xn[cnt % 2]
                    cnt += 1
                    s.dma("sp", lambda e, h=h, tok=tok: e.dma_start(out=h.t[:], in_=hin[tok:tok + 128, :]), h, writes=[h])
                    self.norm_stats(h, r, junk)
                    s.op("dve", lambda e, xx=xx, h=h, r=r: e.scalar_tensor_tensor(
                        out=xx.t[:], in0=h.t[:], scalar=r.t[:], in1=gpre.t[:], op0=ALU.mult, op1=ALU.mult),
                        reads=[h, r, gpre], writes=[xx])
                    self.transpose_to(xx, xnT, ti * 128, 0)
                for fg in range(32):
                    w = wu[uc % 2]
                    uc += 1
                    s.dma("sp", lambda e, w=w, fg=fg: e.dma_start(out=w.t[:], in_=wup[:, :, fg * 256:(fg + 1) * 256]), w, writes=[w])
                    for j in range(2):
                        fc = fg * 2 + j
                        bank = self.pb[4 + fcc % 2]
                        rr = r32[fcc % 2]
                        fcc += 1
                        for kc in range(16):
                            s.op("pe", lambda e, bank=bank, w=w, kc=kc, j=j: e.matmul(
                                bank.t[:], lhsT=w.t[:, kc, j * 128:(j + 1) * 128], rhs=xnT.t[:, kc, :],
                                start=(kc == 0), stop=(kc == 15)), reads=[w, xnT], writes=[bank])
                        s.op("act", lambda e, rr=rr, bank=bank: e.activation(out=rr.t[:], in_=bank.t[:], func=AF.Relu),
                             reads=[bank], writes=[rr])
                        s.op("pool", lambda e, rr=rr, fc=fc: e.tensor_tensor(out=hidT.t[:, fc, :], in0=rr.t[:], in1=rr.t[:], op=ALU.mult),
                             reads=[rr], writes=[hidT])
                for c in range(4):
                    for fg in range(8):
                        w = wd[dc % 2]
                        dc += 1
                        s.dma("sp", lambda e, w=w, fg=fg, c=c: e.dma_start(
                            out=w.t[:], in_=wdn[:, fg * 8:(fg + 1) * 8, c * 512:(c + 1) * 512]), w, writes=[w])
                        for ti in range(4):
                            bank = self.pb[ti]
                            for j in range(8):
                                fc = fg * 8 + j
                                s.op("pe", lambda e, bank=bank, fc=fc, ti=ti, w=w, j=j: e.matmul(
                                    bank.t[:], lhsT=hidT.t[:, fc, ti * 128:(ti + 1) * 128], rhs=w.t[:, j, :],
                                    start=(fc == 0), stop=(fc == 63)), reads=[hidT, w], writes=[bank])
                    for ti in range(4):
                        bank = self.pb[ti]
                        f = ff[ti]
                        if ti % 2 == 0:
                            s.op("act", lambda e, f=f, bank=bank, c=c: e.copy(out=f.t[:, c * 512:(c + 1) * 512], in_=bank.t[:]),
                                 reads=[bank], writes=[f])
                        else:
                            s.op("dve", lambda e, f=f, bank=bank, c=c: e.tensor_copy(out=f.t[:, c * 512:(c + 1) * 512], in_=bank.t[:]),
                                 reads=[bank], writes=[f])
                for ti in range(4):
                    tok = blk * 512 + ti * 128
                    h = hb[cnt % 2]
                    r = rs[cnt % 2]
                    cnt += 1
                    s.dma("sp", lambda e, h=h, tok=tok: e.dma_start(out=h.t[:], in_=hin[tok:tok + 128, :]), h, writes=[h])
                    self.post_norm_residual(ff[ti], h, gpost, r, junk, hout[tok:tok + 128, :])

    def rope(self, src, dst, nhm, ti, tA, tB, scale=None):
        s = self.s
        P = self
        cos = P.rcos.t[:, ti, :]
        sin = P.rsin.t[:, ti, :]
        A = src.t[:].rearrange("p (h two f) -> p h two f", two=2, f=64)
        O = dst.t[:].rearrange("p (h two f) -> p h two f", two=2, f=64)
        M1 = tA.t[:].rearrange("p (h two f) -> p h two f", two=2, f=64)
        M2 = tB.t[:].rearrange("p (h two f) -> p h two f", two=2, f=64)
        cb4 = cos.unsqueeze(1).unsqueeze(1).to_broadcast([128, nhm, 2, 64])
        sb3 = sin.unsqueeze(1).to_broadcast([128, nhm, 64])
        s.op("pool", lambda e: e.tensor_tensor(out=M1, in0=A, in1=cb4, op=ALU.mult), reads=[src, P.rcos], writes=[tA])
        s.op("pool", lambda e: e.tensor_tensor(out=M2[:, :, 0, :], in0=A[:, :, 1, :], in1=sb3, op=ALU.mult),
             reads=[src, P.rsin], writes=[tB])
        s.op("pool", lambda e: e.tensor_tensor(out=M2[:, :, 1, :], in0=A[:, :, 0, :], in1=sb3, op=ALU.mult),
             reads=[src, P.rsin], writes=[tB])
        s.op("dve", lambda e: e.tensor_tensor(out=O[:, :, 0, :], in0=M1[:, :, 0, :], in1=M2[:, :, 0, :], op=ALU.subtract),
             reads=[tA, tB], writes=[dst])
        s.op("dve", lambda e: e.tensor_tensor(out=O[:, :, 1, :], in0=M1[:, :, 1, :], in1=M2[:, :, 1, :], op=ALU.add),
             reads=[tA, tB], writes=[dst])

    def phase_qkv(self):
        s = self.s
        P = self
        hB = self.scratch("hB", [L, D], F32)
        wkv = self.wb("w_kv", [D, 2 * D]).rearrange("(kc p) n -> p kc n", p=128)
        wq = self.wb("attn_w_q", [D, D]).rearrange("(kc p) n -> p kc n", p=128)
        KT = self.scratch("KT", [16, 128, L], BF16)
        QT = self.scratch("QT", [16, 128, L], BF16)
        V = self.scratch("V", [L, D], BF16)
        scale = 128 ** -0.5
        with s.phase("qkv"):
            self.conv_wait("qkv")
            self.mk_eps()
            gkv = s.tile([128, D], F32, "gkv")
            gq = s.tile([128, D], F32, "gq")
            self.load_gain(gkv, self.input("kv_norm_g"))
            self.load_gain(gq, self.input("mix_pre_g")[1])
            hb = [s.tile([128, D], F32, "hb") for _ in range(2)]
            junk = s.tile([128, D], BF16, "junk")
            xk = [s.tile([128, D], BF16, "xk") for _ in range(2)]
            xq = [s.tile([128, D], BF16, "xq") for _ in range(2)]
            rs = [s.tile([128, 1], F32, "rs") for _ in range(2)]
            xkT = s.tile([128, 16, 512], BF16, "xkT")
            xqT = s.tile([128, 16, 512], BF16, "xqT")
            wt = [s.tile([128, 16, 512], BF16, "wt") for _ in range(2)]
            kx = [s.tile([128, 512], F32, "kx") for _ in range(2)]
            kr = [s.tile([128, 512], BF16, "kr") for _ in range(3)]
            tA = [s.tile([128, 512], F32, "tA") for _ in range(2)]
            tB = [s.tile([128, 512], F32, "tB") for _ in range(2)]
            kTb = [s.tile([128, 16, 512], BF16, "kTb") for _ in range(2)]
            vb = s.tile([128, 4, D], BF16, "vb")
            cnt = 0
            wc = 0
            pc = 0
            for blk in range(4):
                for ti in range(4):
                    tok = blk * 512 + ti * 128
                    h = hb[cnt % 2]
                    r = rs[cnt % 2]
                    a, b = xk[cnt % 2], xq[cnt % 2]
                    cnt += 1
                    s.dma("sp", lambda e, h=h, tok=tok: e.dma_start(out=h.t[:], in_=hB[tok:tok + 128, :]), h, writes=[h])
                    self.norm_stats(h, r, junk)
                    s.op("dve", lambda e, a=a, h=h, r=r: e.scalar_tensor_tensor(
                        out=a.t[:], in0=h.t[:], scalar=r.t[:], in1=gkv.t[:], op0=ALU.mult, op1=ALU.mult),
                        reads=[h, r, gkv], writes=[a])
                    s.op("dve", lambda e, b=b, h=h, r=r: e.scalar_tensor_tensor(
                        out=b.t[:], in0=h.t[:], scalar=r.t[:], in1=gq.t[:], op0=ALU.mult, op1=ALU.mult),
                        reads=[h, r, gq], writes=[b])
                    self.transpose_to(a, xkT, ti * 128, 0)
                    self.transpose_to(b, xqT, ti * 128, 0)
                for which in range(2):
                    pending = None
                    wsrc = wkv if which == 0 else wq
                    xT = xkT if which == 0 else xqT
                    dstT = kTb[which]
                    for cb in range(4):
                        w = wt[wc % 2]
                        wc += 1
                        s.dma("sp", lambda e, w=w, wsrc=wsrc, cb=cb: e.dma_start(out=w.t[:], in_=wsrc[:, :, cb * 512:(cb + 1) * 512]),
                              w, writes=[w])
                        for ti in range(4):
                            bank = self.pb[pc % 2]
                            kx_, kr_ = kx[pc % 2], kr[pc % 3]
                            pc += 1
                            for kc in range(16):
                                s.op("pe", lambda e, bank=bank, xT=xT, kc=kc, ti=ti, w=w: e.matmul(
                                    bank.t[:], lhsT=xT.t[:, kc, ti * 128:(ti + 1) * 128], rhs=w.t[:, kc, :],
                                    start=(kc == 0), stop=(kc == 15)), reads=[xT, w], writes=[bank])
                            if which == 0:
                                s.op("act", lambda e, kx_=kx_, bank=bank: e.copy(out=kx_.t[:], in_=bank.t[:]), reads=[bank], writes=[kx_])
                            else:
                                s.op("act", lambda e, kx_=kx_, bank=bank: e.mul(out=kx_.t[:], in_=bank.t[:], mul=scale),
                                     reads=[bank], writes=[kx_])
                            self.rope(kx_, kr_, 4, blk * 4 + ti, tA[pc % 2], tB[pc % 2])
                            if pending is not None:
                                pending()

                            def mk(pt=self.pt[pc % 2], kr_=kr_, dstT=dstT, cb=cb, ti=ti):
                                def f():
                                    for hm in range(4):
                                        s.op("pe", lambda e, hm=hm: e.transpose(
                                            out=pt.t[:, hm * 128:(hm + 1) * 128], in_=kr_.t[:, hm * 128:(hm + 1) * 128],
                                            identity=P.ident.t[:]), reads=[kr_, P.ident], writes=[pt])
                                    s.op("dve", lambda e: e.tensor_copy(
                                        out=dstT.t[:, cb * 4:(cb + 1) * 4, ti * 128:(ti + 1) * 128],
                                        in_=pt.t[:, 0:512].rearrange("p (h t) -> p h t", h=4)), reads=[pt], writes=[dstT])
                                return f
                            pending = mk()
                    if pending is not None:
                        pending()
                        pending = None
                    dd = KT if which == 0 else QT
                    s.dma("sp", lambda e, dd=dd, dstT=dstT, blk=blk: e.dma_start(
                        out=dd[:, :, blk * 512:(blk + 1) * 512].rearrange("h p t -> p h t"), in_=dstT.t[:]), dstT, reads=[dstT])
                for cb in range(4):
                    w = wt[wc % 2]
                    wc += 1
                    s.dma("sp", lambda e, w=w, cb=cb: e.dma_start(out=w.t[:], in_=wkv[:, :, D + cb * 512:D + (cb + 1) * 512]), w, writes=[w])
                    for ti in range(4):
                        bank = self.pb[pc % 2]
                        pc += 1
                        for kc in range(16):
                            s.op("pe", lambda e, bank=bank, kc=kc, ti=ti, w=w: e.matmul(
                                bank.t[:], lhsT=xkT.t[:, kc, ti * 128:(ti + 1) * 128], rhs=w.t[:, kc, :],
                                start=(kc == 0), stop=(kc == 15)), reads=[xkT, w], writes=[bank])
                        s.op("act", lambda e, bank=bank, ti=ti, cb=cb: e.copy(out=vb.t[:, ti, cb * 512:(cb + 1) * 512], in_=bank.t[:]),
                             reads=[bank], writes=[vb])
                s.dma("sp", lambda e, blk=blk: e.dma_start(
                    out=V[blk * 512:(blk + 1) * 512, :].rearrange("(t p) n -> p t n", p=128), in_=vb.t[:]), vb, reads=[vb])

    def phase_attn(self):
        s = self.s
        P = self
        hB = self.scratch("hB", [L, D], F32)
        hC = self.scratch("hC", [L, D], F32)
        KT = self.scratch("KT", [16, 128, L], BF16)
        QT = self.scratch("QT", [16, 128, L], BF16)
        V = self.scratch("V", [L, D], BF16)
        wo = self.wb("attn_w_o", [D, D]).rearrange("(kc p) n -> p kc n", p=128)
        with s.phase("attn"):
            self.conv_wait("attn")
            self.mk_eps()
            gpost = s.tile([128, D], F32, "gpost")
            self.load_gain(gpost, self.input("mix_post_g")[1])
            kT = [s.tile([128, L], BF16, "kT") for _ in range(2)]
            qT = [s.tile([128, 512], BF16, "qT") for _ in range(2)]
            vt = [s.tile([128, 16, 257], BF16, "vt") for _ in range(2)]
            for v_ in vt:
                s.op("pool", lambda e, v_=v_: e.memset(v_.t[:, :, 256:257], 1.0), writes=[v_])
            pT = [s.tile([128, 512], BF16, "pT") for _ in range(3)]
            om4 = [s.tile([128, 4, 257], F32, "om") for _ in range(4)]
            obf = s.tile([128, 4, D], BF16, "obf")
            oT = s.tile([128, 16, 512], BF16, "oT")
            rr = [s.tile([128, 1], F32, "rr") for _ in range(4)]
            tt = s.tile([128, 256], F32, "tt")
            oo = s.tile([128, 256], F32, "oo")
            jk2 = s.tile([128, 256], F32, "jk2")
            wt = [s.tile([128, 16, 512], BF16, "wo") for _ in range(2)]
            att = [s.tile([128, D], F32, "att") for _ in range(4)]
            hb = [s.tile([128, D], F32, "hb") for _ in range(2)]
            junk = s.tile([128, D], BF16, "junk")
            rs = [s.tile([128, 1], F32, "rs") for _ in range(2)]
            eps256 = s.tile([128, 1], F32, "eps256")
            s.op("pool", lambda e: e.memset(eps256.t[:], EPS), writes=[eps256])
            stb = [self.pb[4], self.pb[5]]
            kc_ = 0
            pc = 0
            stc = 0
            wc = 0
            cnt = 0
            for sb in range(4):
                nkt = 4 * (sb + 1)
                for hh in range(8):
                    om = om4[(hh % 2) * 2:(hh % 2) * 2 + 2]
                    v_ = vt[(sb * 8 + hh) % 2]
                    s.dma("sp", lambda e, v_=v_, hh=hh, nkt=nkt: e.dma_start(
                        out=v_.t[:, 0:nkt, 0:256],
                        in_=V[0:nkt * 128, hh * 256:(hh + 1) * 256].rearrange("(t p) n -> p t n", p=128)), v_, writes=[v_])
                    for m in range(2):
                        hm = hh * 2 + m
                        k_ = kT[kc_ % 2]
                        q_ = qT[kc_ % 2]
                        kc_ += 1
                        s.dma("sp", lambda e, k_=k_, hm=hm, nkt=nkt: e.dma_start(out=k_.t[:, 0:nkt * 128], in_=KT[hm, :, 0:nkt * 128]),
                              k_, writes=[k_])
                        s.dma("sp", lambda e, q_=q_, hm=hm, sb=sb: e.dma_start(out=q_.t[:], in_=QT[hm, :, sb * 512:(sb + 1) * 512]),
                              q_, writes=[q_])
                        def emit_st(j):
                            nonlocal stc, pc
                            qlo = max(0, j - 4 * sb)
                            n = 512 - qlo * 128
                            st = stb[stc % 2]
                            stc += 1
                            p_ = pT[pc % 3]
                            pc += 1
                            s.op("pe", lambda e, st=st, k_=k_, q_=q_, j=j, qlo=qlo, n=n: e.matmul(
                                st.t[:, 0:n], lhsT=k_.t[:, j * 128:(j + 1) * 128], rhs=q_.t[:, qlo * 128:512],
                                start=True, stop=True), reads=[k_, q_], writes=[st])
                            s.op("act", lambda e, st=st, p_=p_, n=n: e.activation(out=p_.t[:, 0:n], in_=st.t[:, 0:n], func=AF.Exp),
                                 reads=[st], writes=[p_])
                            if j >= 4 * sb:
                                s.op("pool", lambda e, p_=p_: e.tensor_tensor(out=p_.t[:, 0:128], in0=p_.t[:, 0:128],
                                                                              in1=P.cmask.t[:], op=ALU.mult),
                                     reads=[p_, P.cmask], writes=[p_])
                            return p_, qlo
                        nxt = emit_st(0)
                        for j in range(nkt):
                            p_, qlo = nxt
                            if j + 1 < nkt:
                                nxt = emit_st(j + 1)
                            for qt in range(qlo, 4):
                                ob = self.pb[qt]
                                s.op("pe", lambda e, ob=ob, p_=p_, qt=qt, qlo=qlo, v_=v_, j=j, sb=sb: e.matmul(
                                    ob.t[:, 0:257], lhsT=p_.t[:, (qt - qlo) * 128:(qt - qlo + 1) * 128], rhs=v_.t[:, j, :],
                                    start=(j == 0), stop=(j == 4 * sb + qt)), reads=[p_, v_], writes=[ob])
                        o_ = om[m]
                        for qt in range(4):
                            ob = self.pb[qt]
                            if qt % 2 == 0:
                                s.op("act", lambda e, o_=o_, ob=ob, qt=qt: e.copy(out=o_.t[:, qt, :], in_=ob.t[:, 0:257]),
                                     reads=[ob], writes=[o_])
                            else:
                                s.op("dve", lambda e, o_=o_, ob=ob, qt=qt: e.tensor_copy(out=o_.t[:, qt, :], in_=ob.t[:, 0:257]),
                                     reads=[ob], writes=[o_])
                    for qt in range(4):
                        r1, r2, r3 = rr[0], rr[1], rr[2]
                        s.op("dve", lambda e, qt=qt, om0=om[0], om1=om[1]: e.reciprocal(out=r1.t[:], in_=om0.t[:, qt, 256:257]), reads=[om[0]], writes=[r1])
                        s.op("dve", lambda e, qt=qt, om0=om[0], om1=om[1]: e.reciprocal(out=r2.t[:], in_=om1.t[:, qt, 256:257]), reads=[om[1]], writes=[r2])
                        s.op("dve", lambda e: e.tensor_tensor(out=r2.t[:], in0=r2.t[:], in1=P.lam.t[:], op=ALU.mult),
                             reads=[r2, P.lam], writes=[r2])
                        s.op("dve", lambda e, qt=qt, om0=om[0], om1=om[1]: e.tensor_scalar(out=tt.t[:], in0=om1.t[:, qt, 0:256], scalar1=r2.t[:], scalar2=None,
                                                                     op0=ALU.mult), reads=[om[1], r2], writes=[tt])
                        s.op("dve", lambda e, qt=qt, om0=om[0], om1=om[1]: e.scalar_tensor_tensor(out=oo.t[:], in0=om0.t[:, qt, 0:256], scalar=r1.t[:],
                                                                            in1=tt.t[:], op0=ALU.mult, op1=ALU.subtract),
                             reads=[om[0], r1, tt], writes=[oo])
                        s.op("act", lambda e: e.activation(out=jk2.t[:], in_=oo.t[:], func=AF.Square, accum_out=r3.t[:]),
                             reads=[oo], writes=[jk2, r3])
                        s.op("act", lambda e: e.activation(out=r3.t[:], in_=r3.t[:], func=AF.Sqrt, scale=1.0 / 256, bias=eps256.t[:]),
                             reads=[r3, eps256], writes=[r3])
                        s.op("dve", lambda e: e.reciprocal(out=r3.t[:], in_=r3.t[:]), reads=[r3], writes=[r3])
                        s.op("dve", lambda e, qt=qt, hh=hh: e.scalar_tensor_tensor(
                            out=obf.t[:, qt, hh * 256:(hh + 1) * 256], in0=oo.t[:], scalar=r3.t[:], in1=P.gsub.t[:],
                            op0=ALU.mult, op1=ALU.mult), reads=[oo, r3, P.gsub], writes=[obf])
                for qt in range(4):
                    for half in range(2):
                        pt = self.pt[half]
                        for k in range(8):
                            kc = half * 8 + k
                            s.op("pe", lambda e, pt=pt, k=k, kc=kc, qt=qt: e.transpose(
                                out=pt.t[:, k * 128:(k + 1) * 128], in_=obf.t[:, qt, kc * 128:(kc + 1) * 128],
                                identity=P.ident.t[:]), reads=[obf, P.ident], writes=[pt])
                        s.op("dve" if half else "act",
                             (lambda e, pt=pt, half=half, qt=qt: e.tensor_copy(
                                 out=oT.t[:, half * 8:(half + 1) * 8, qt * 128:(qt + 1) * 128],
                                 in_=pt.t[:].rearrange("p (k t) -> p k t", k=8))) if half else
                             (lambda e, pt=pt, half=half, qt=qt: e.copy(
                                 out=oT.t[:, half * 8:(half + 1) * 8, qt * 128:(qt + 1) * 128],
                                 in_=pt.t[:].rearrange("p (k t) -> p k t", k=8))),
                             reads=[pt], writes=[oT])
                for cb in range(4):
                    w = wt[wc % 2]
                    wc += 1
                    s.dma("sp", lambda e, w=w, cb=cb: e.dma_start(out=w.t[:], in_=wo[:, :, cb * 512:(cb + 1) * 512]), w, writes=[w])
                    for qt in range(4):
                        bank = stb[stc % 2]
                        stc += 1
                        for kc in range(16):
                            s.op("pe", lambda e, bank=bank, kc=kc, qt=qt, w=w: e.matmul(
                                bank.t[:], lhsT=oT.t[:, kc, qt * 128:(qt + 1) * 128], rhs=w.t[:, kc, :],
                                start=(kc == 0), stop=(kc == 15)), reads=[oT, w], writes=[bank])
                        a_ = att[qt]
                        s.op("act", lambda e, a_=a_, bank=bank, cb=cb: e.copy(out=a_.t[:, cb * 512:(cb + 1) * 512], in_=bank.t[:]),
                             reads=[bank], writes=[a_])
                for qt in range(4):
                    tok = sb * 512 + qt * 128
                    h = hb[cnt % 2]
                    r = rs[cnt % 2]
                    cnt += 1
                    s.dma("sp", lambda e, h=h, tok=tok: e.dma_start(out=h.t[:], in_=hB[tok:tok + 128, :]), h, writes=[h])
                    self.post_norm_residual(att[qt], h, gpost, r, junk, hC[tok:tok + 128, :])


def _in_map(inputs, b):
    m = {}
    for k, v in inputs.items():
        a = np.asarray(v)
        if k == "x":
            a = a[b]
        m[k] = np.ascontiguousarray(a.reshape(INPUT_SHAPES[k]), dtype=np.float32)
    return m


def kernel(**inputs):
    prog = Prog()
    nc = prog.build()
    used = set(prog.inp.keys())
    in_maps = []
    for b in range(NCORES):
        m = _in_map(inputs, b)
        in_maps.append({k: v for k, v in m.items() if k in used})
    res = run_bass_kernel_spmd(nc, in_maps, core_ids=list(range(NCORES)))
    out = np.stack([np.asarray(res.results[b]["out"], dtype=np.float32) for b in range(NCORES)], axis=0)
    return out
```

```python
import contextlib
import math
import numpy as np
import concourse.bass as bass
import concourse.mybir as mybir
from concourse.bass_utils import run_bass_kernel_spmd

F32 = mybir.dt.float32
BF16 = mybir.dt.bfloat16
I32 = mybir.dt.int32
ALU = mybir.AluOpType
AF = mybir.ActivationFunctionType

D = 2048
L = 2048
DFF = 8192
NCORES = 8
EPS = 1e-6
ENGS = ("pe", "act", "dve", "pool", "sp")
SEM_CHUNK = 30000
TWO_PI = 2.0 * math.pi
FIX_ENG = "dve"
SAME_ENG_ALL = True


class Buf:
    __slots__ = ("name", "last_w", "readers", "dsem", "dbase", "dcount", "last_group", "cur_group_id", "psum")

    def __init__(self, name, psum=False):
        self.name = name
        self.last_w = None
        self.readers = []
        self.dsem = None
        self.dcount = 0
        self.last_group = []
        self.cur_group_id = None
        self.psum = psum


class Op:
    __slots__ = ("eng", "fn", "deps", "is_dma", "dbuf", "dgroup", "ticket", "has_dep")

    def __init__(self, eng, fn):
        self.eng = eng
        self.fn = fn
        self.deps = []
        self.is_dma = False
        self.dbuf = None
        self.dgroup = None
        self.ticket = None
        self.has_dep = False


class Group:
    __slots__ = ("end",)

    def __init__(self):
        self.end = 0


class Tile:
    def __init__(self, sched, t, name, psum=False):
        self.s = sched
        self.t = t
        self.name = name
        self.psum = psum
        self._b = None
        self._ph = -1

    @property
    def b(self):
        if self._ph != self.s.phase_id:
            self._b = Buf(self.name, self.psum)
            self._ph = self.s.phase_id
        return self._b


class Sched:
    def __init__(self, nc, nsem_eng=4, ndma=64):
        self.nc = nc
        self.stack = contextlib.ExitStack()
        self.esem = {e: [self.stack.enter_context(nc.semaphore(f"s_{e}{i}")) for i in range(nsem_eng)]
                     for e in ENGS if e != "sp"}
        self.dpool = [[self.stack.enter_context(nc.semaphore(f"d{i}")), 0] for i in range(ndma)]
        self.ticket = {e: 0 for e in ENGS}
        self.ops = {e: [] for e in ENGS}
        self.phase_id = 0
        self.phase_dma_bufs = []
        self.barrier = []
        self.pstack = None
        self.ntile = 0
        self.total_ops = 0

    def ptile(self, shape, dtype, name=None):
        self.ntile += 1
        name = name or f"t{self.ntile}"
        t = self.stack.enter_context(self.nc.sbuf_tensor(name, list(shape), dtype))
        return Tile(self, t, name)

    def tile(self, shape, dtype, name=None):
        self.ntile += 1
        name = (name or "t") + f"_{self.ntile}"
        t = self.pstack.enter_context(self.nc.sbuf_tensor(name, list(shape), dtype))
        return Tile(self, t, name)

    def psum_tile(self, shape, dtype, name):
        t = self.stack.enter_context(self.nc.psum_tensor(name, list(shape), dtype))
        return Tile(self, t, name, psum=True)

    def dbuf(self, name):
        return Buf(name)

    @staticmethod
    def _b(x):
        return x.b if isinstance(x, Tile) else x

    def _track(self, op, reads, writes):
        deps = op.deps
        wr = [self._b(w) for w in writes]
        for r in reads:
            b = self._b(r)
            if b.psum:
                wr.append(b)
                continue
            if b.last_w is not None:
                deps.append(("raw", b.last_w))
            b.readers.append(op)
        for b in wr:
            if b.last_w is not None:
                deps.append(("raw" if b.psum else "waw", b.last_w))
            for r in b.readers:
                if r is not op:
                    deps.append(("war", r))
            b.readers = []
            b.last_w = op

    def op(self, eng, fn, reads=(), writes=()):
        o = Op(eng, fn)
        self._track(o, reads, writes)
        self.ops[eng].append(o)
        return o

    def dma(self, eng, fn, dtile, reads=(), writes=(), group=None, background=False):
        dbuf = self._b(dtile)
        o = Op(eng, fn)
        o.is_dma = True
        o.dbuf = dbuf
        if dbuf.dsem is None:
            ent = self.dpool.pop(0)
            dbuf.dsem = ent
            dbuf.dcount = ent[1]
            if not background:
                self.phase_dma_bufs.append(dbuf)
        if group is None or dbuf.cur_group_id != group or not dbuf.last_group:
            if dbuf.last_group:
                o.deps.append(("dmaser", dbuf.last_group[-1]))
            dbuf.last_group = [o]
            dbuf.cur_group_id = group
            o.dgroup = Group()
        else:
            first = dbuf.last_group[0]
            for k, d in first.deps:
                if k == "dmaser":
                    o.deps.append((k, d))
            o.dgroup = first.dgroup
            dbuf.last_group.append(o)
        dbuf.dcount += 16
        o.dgroup.end = dbuf.dcount
        self._track(o, reads, writes)
        self.ops[eng].append(o)
        return o

    @contextlib.contextmanager
    def phase(self, name):
        self.pstack = contextlib.ExitStack()
        with self.pstack:
            yield
            self._emit()
        self.pstack = None
        self.phase_id += 1

    def _semval(self, e, tk):
        tk -= 1
        return self.esem[e][tk // SEM_CHUNK], tk % SEM_CHUNK + 1

    def _emit(self, final=False):
        nc = self.nc
        for e in ENGS:
            for o in self.ops[e]:
                for kind, d in o.deps:
                    if d.is_dma:
                        continue
                    if d.eng == o.eng and (d.eng == "pe" or (kind != "raw" and not SAME_ENG_ALL)):
                        continue
                    d.has_dep = True
            for o in reversed(self.ops[e]):
                if not o.is_dma:
                    o.has_dep = True
                    break
        newbar = []
        for e in ENGS:
            t = self.ticket[e]
            for o in self.ops[e]:
                if o.has_dep and not o.is_dma:
                    t += 1
                    o.ticket = t
            if t != self.ticket[e]:
                newbar.append(self._semval(e, t))
            self.ticket[e] = t
        barrier = self.barrier

        def run(e, eng):
            waited = {}
            for sem, val in barrier:
                eng.wait_ge(sem, val)
                waited[id(sem)] = val
            for o in self.ops[e]:
                need = {}
                for kind, d in o.deps:
                    if d.is_dma:
                        if o.is_dma and o.dgroup is d.dgroup:
                            continue
                        sem, val = d.dbuf.dsem[0], d.dgroup.end
                    else:
                        if d.eng == e and (e == "pe" or (kind != "raw" and not SAME_ENG_ALL)):
                            continue
                        sem, val = self._semval(d.eng, d.ticket)
                    key = id(sem)
                    if val > need.get(key, (None, 0))[1]:
                        need[key] = (sem, val)
                for key, (sem, val) in need.items():
                    if waited.get(key, 0) >= val:
                        continue
                    waited[key] = val
                    eng.wait_ge(sem, val)
                ins = o.fn(eng)
                if o.is_dma:
                    ins.then_inc(o.dbuf.dsem[0], 16)
                elif o.ticket is not None:
                    sem, _ = self._semval(e, o.ticket)
                    ins.then_inc(sem, 1)
            if final and e == "sp":
                for b in self.phase_dma_bufs:
                    eng.wait_ge(b.dsem[0], b.dcount)

        with nc.Block(no_gpsimd_drain=True) as block:
            @block.tensor
            def _(eng):
                run("pe", eng)

            @block.scalar
            def _(eng):
                run("act", eng)

            @block.vector
            def _(eng):
                run("dve", eng)

            @block.gpsimd
            def _(eng):
                run("pool", eng)

            @block.sync
            def _(eng):
                run("sp", eng)
        bar = {id(s): (s, v) for s, v in self.barrier}
        for s, v in newbar:
            bar[id(s)] = (s, v)
        for b in self.phase_dma_bufs:
            b.dsem[1] = b.dcount
            bar[id(b.dsem[0])] = (b.dsem[0], b.dcount)
            self.dpool.append(b.dsem)
        self.barrier = list(bar.values())
        self.total_ops += sum(len(v) for v in self.ops.values())
        self.ops = {e: [] for e in ENGS}
        self.phase_dma_bufs = []


INPUT_SHAPES = {
    "x": [L, D], "mix_pre_g": [2, D], "mix_post_g": [2, D], "mlp_pre_g": [2, D], "mlp_post_g": [2, D],
    "ssm_w_in": [D, D], "ssm_a_re": [128, 64], "ssm_a_im": [128, 64], "ssm_log_dt": [128],
    "ssm_b_re": [128, 64, 16], "ssm_b_im": [128, 64, 16], "ssm_c_re": [128, 16, 64], "ssm_c_im": [128, 16, 64],
    "ssm_d": [D], "ssm_w_glu": [D, 2 * D], "kv_norm_g": [D], "w_kv": [D, 2 * D], "attn_w_q": [D, D],
    "lam_q1": [128], "lam_k1": [128], "lam_q2": [128], "lam_k2": [128], "attn_subln_g": [256],
    "attn_w_o": [D, D], "mlp_w_up": [2, D, DFF], "mlp_w_down": [2, DFF, D],
}
ALL_PHASES = ("conv", "prep", "l0a", "l0b", "l0c", "mlp0", "qkv", "attn", "mlp1")


class Prog:
    def __init__(self, phases=ALL_PHASES, dbg=(), ext_in=()):
        self.phases = phases
        self.dbg = set(dbg)
        self.ext_in = set(ext_in)
        self.nc = bass.Bass("TRN2", target_bir_lowering=False)
        self.s = Sched(self.nc)
        self.inp = {}
        self.scr = {}
        self.outputs = []
        self.dbg_done = False

    def input(self, name):
        if name not in self.inp:
            self.inp[name] = self.nc.dram_tensor(name, INPUT_SHAPES[name], F32, kind="ExternalInput").ap()
        return self.inp[name]

    def scratch(self, name, shape, dtype):
        if name not in self.scr:
            if name in self.ext_in:
                kind = "ExternalInput"
            elif name in self.dbg or name == "out":
                kind = "ExternalOutput"
                self.outputs.append(name)
            else:
                kind = "Internal"
            self.scr[name] = self.nc.dram_tensor(name, list(shape), dtype, kind=kind).ap()
        return self.scr[name]

    def build(self):
        s = self.s
        nc = self.nc
        self.pb = [s.psum_tile([128, 512], F32, f"pb{i}") for i in range(6)]
        self.pt = [s.psum_tile([128, 1024], BF16, f"pt{i}") for i in range(2)]
        self.ident = s.ptile([128, 128], BF16, "ident")
        self.cmask = s.ptile([128, 128], BF16, "cmask")
        self.rcos = s.ptile([128, 16, 64], F32, "rcos")
        self.rsin = s.ptile([128, 16, 64], F32, "rsin")
        self.lam = s.ptile([128, 1], F32, "lam")
        self.gsub = s.ptile([128, 256], F32, "gsub")
        self.consts_ready = False
        ph = self.phases
        if "conv" in ph:
            self.phase_conv()
        if "prep" in ph:
            self.phase_prep()
        if "l0a" in ph:
            self.phase_l0a()
        if "l0b" in ph:
            self.phase_l0b()
        if "l0c" in ph:
            self.phase_l0c()
        if "mlp0" in ph:
            self.phase_mlp(0, "hA", "hB")
        if "qkv" in ph:
            self.phase_qkv()
        if "attn" in ph:
            self.phase_attn()
        if "mlp1" in ph:
            self.phase_mlp(1, "hC", "out")
        with s.phase("final"):
            fin = s.tile([128, 1], F32, "fin")
            s.op("dve", lambda e: e.memset(fin.t[:], 1.0), writes=[fin])
            if self.dbg_done:
                dn = self.nc.dram_tensor("done", [128, 1], F32, kind="ExternalOutput").ap()
                s.dma("sp", lambda e: e.dma_start(out=dn, in_=fin.t[:]), fin, reads=[fin])
        with nc.Block(no_gpsimd_drain=True) as block:
            @block.sync
            def _(eng):
                for sem, val in s.barrier:
                    eng.wait_ge(sem, val)
        s.stack.close()
        return nc

    def wb(self, name, shape):
        return self.scratch(name + "_bf", shape, BF16)

    def norm_stats(self, src, rs, junk):
        s = self.s
        ss = rs
        s.op("act", lambda e: e.activation(out=junk.t[:], in_=src.t[:], func=AF.Square, accum_out=ss.t[:]),
             reads=[src], writes=[junk, ss])
        s.op("act", lambda e: e.activation(out=rs.t[:], in_=ss.t[:], func=AF.Sqrt, scale=1.0 / D, bias=self.eps_t.t[:]),
             reads=[ss, self.eps_t], writes=[rs])
        s.op("dve", lambda e: e.reciprocal(out=rs.t[:], in_=rs.t[:]), reads=[rs], writes=[rs])

    def transpose_to(self, xn, dstT, col0, pti):
        s = self.s
        for half in range(2):
            pt = self.pt[(pti + half) % 2]
            for k in range(8):
                kc = half * 8 + k
                s.op("pe", lambda e, pt=pt, k=k, kc=kc: e.transpose(
                    out=pt.t[:, k * 128:(k + 1) * 128], in_=xn.t[:, kc * 128:(kc + 1) * 128], identity=self.ident.t[:]),
                    reads=[xn, self.ident], writes=[pt])
            eng = "act" if half == 0 else "dve"
            if eng == "act":
                s.op("act", lambda e, pt=pt, half=half: e.copy(
                    out=dstT.t[:, half * 8:(half + 1) * 8, col0:col0 + 128],
                    in_=pt.t[:].rearrange("p (k t) -> p k t", k=8)), reads=[pt], writes=[dstT])
            else:
                s.op("dve", lambda e, pt=pt, half=half: e.tensor_copy(
                    out=dstT.t[:, half * 8:(half + 1) * 8, col0:col0 + 128],
                    in_=pt.t[:].rearrange("p (k t) -> p k t", k=8)), reads=[pt], writes=[dstT])

    def load_gain(self, tile_, src_ap):
        self.s.dma("sp", lambda e: e.dma_start(out=tile_.t[:], in_=src_ap.partition_broadcast(128)), tile_, writes=[tile_])

    def post_norm_residual(self, val, hres, gpost, rs, junk, out_ap):
        s = self.s
        self.norm_stats(val, rs, junk)
        s.op("dve", lambda e: e.scalar_tensor_tensor(out=val.t[:], in0=val.t[:], scalar=rs.t[:], in1=gpost.t[:],
                                                     op0=ALU.mult, op1=ALU.mult), reads=[val, rs, gpost], writes=[val])
        s.op("pool", lambda e: e.tensor_tensor(out=val.t[:], in0=val.t[:], in1=hres.t[:], op=ALU.add),
             reads=[val, hres], writes=[val])
        s.dma("sp", lambda e: e.dma_start(out=out_ap, in_=val.t[:]), val, reads=[val])

    def mk_eps(self):
        s = self.s
        self.eps_t = s.tile([128, 1], F32, "eps")
        s.op("pool", lambda e: e.memset(self.eps_t.t[:], EPS), writes=[self.eps_t])

    CONV_SPECS = {"ssm_w_in": ("ssm_w_in", None, D, D), "ssm_w_glu": ("ssm_w_glu", None, D, 2 * D),
                  "mlp_w_up0": ("mlp_w_up", 0, D, DFF), "mlp_w_down0": ("mlp_w_down", 0, DFF, D),
                  "w_kv": ("w_kv", None, D, 2 * D), "attn_w_q": ("attn_w_q", None, D, D), "attn_w_o": ("attn_w_o", None, D, D),
                  "mlp_w_up1": ("mlp_w_up", 1, D, DFF), "mlp_w_down1": ("mlp_w_down", 1, DFF, D)}
    CONV_NEED = {"l0a": ["ssm_w_in"], "l0c": ["ssm_w_glu"], "mlp0": ["mlp_w_up0", "mlp_w_down0"],
                 "qkv": ["w_kv", "attn_w_q"], "attn": ["attn_w_o"], "mlp1": ["mlp_w_up1", "mlp_w_down1"]}
    CONV_PLAN = {"ssm_w_in": "conv", "ssm_w_glu": "l0a", "mlp_w_up0": "l0b", "mlp_w_down0": "l0b", "w_kv": "l0b",
                 "attn_w_q": "l0b", "attn_w_o": "l0b", "mlp_w_up1": "mlp0", "mlp_w_down1": "mlp0"}

    def issue_conv(self, phase):
        s = self.s
        import os
        if phase != "conv":
            return
        wanted = set(sum([self.CONV_NEED.get(p, []) for p in self.phases], []))
        if os.environ.get("CONV_ALL"):
            wanted = set(self.CONV_SPECS)
        self.conv_bufs = {}
        for wname, (name, idx, R, C) in self.CONV_SPECS.items():
            if wname not in wanted:
                continue
            src = self.input(name)
            if idx is not None:
                src = src[idx]
            dst = self.wb(wname, [R, C])
            rows = max(128, (2 * 1024 * 1024) // C)
            b = s.dbuf(f"cv_{wname}")
            self.conv_bufs[wname] = b
            for r0 in range(0, R, rows):
                s.dma("pool", lambda e, r0=r0, src=src, dst=dst, rows=rows: e.dma_start(
                    out=dst[r0:r0 + rows, :], in_=src[r0:r0 + rows, :]), b, group="cv", background=True)

    def conv_wait(self, phase):
        for wname in self.CONV_NEED.get(phase, []):
            b = getattr(self, "conv_bufs", {}).get(wname)
            if b is not None:
                self.s.barrier.append((b.dsem[0], b.dcount))

    def phase_conv(self):
        s = self.s
        with s.phase("conv"):
            self.issue_conv("conv")

    def phase_prep(self):
        s = self.s
        P = self
        with s.phase("prep"):
            ones = s.tile([128, 128], F32, "ones")
            s.op("pool", lambda e: e.memset(ones.t[:], 1.0), writes=[ones])
            s.op("pool", lambda e: e.affine_select(out=P.ident.t[:], in_=ones.t[:], pattern=[[-1, 128]],
                                                   compare_op=ALU.is_equal, fill=0.0, base=0, channel_multiplier=1),
                 reads=[ones], writes=[P.ident])
            s.op("pool", lambda e: e.affine_select(out=P.cmask.t[:], in_=ones.t[:], pattern=[[1, 128]],
                                                   compare_op=ALU.is_ge, fill=0.0, base=0, channel_multiplier=-1),
                 reads=[ones], writes=[P.cmask])
            import os
            parts = os.environ.get("PREP_PARTS", "rope,ssm,lambda").split(",")
            if "rope" in parts:
                self.prep_rope()
            if "ssm" in parts:
                self.prep_ssm()
            if "lambda" in parts:
                self.prep_lambda()
        self.consts_ready = True

    def sincos(self, ang, shape, sin_out, cos_out, tag):
        s = self.s
        n = len(shape)
        ki = s.tile(shape, I32, "ki" + tag)
        kf = s.tile(shape, F32, "kf" + tag)
        m2 = s.tile(shape, F32, "m2" + tag)
        c1 = 6.28125
        c2 = TWO_PI - c1
        lim = 3.1415925
        s.op("dve", lambda e: e.tensor_scalar(out=ki.t[:], in0=ang.t[:], scalar1=1.0 / TWO_PI, scalar2=None, op0=ALU.mult),
             reads=[ang], writes=[ki])
        s.op("dve", lambda e: e.tensor_copy(out=kf.t[:], in_=ki.t[:]), reads=[ki], writes=[kf])
        s.op("dve", lambda e: e.scalar_tensor_tensor(out=ang.t[:], in0=kf.t[:], scalar=-c1, in1=ang.t[:], op0=ALU.mult, op1=ALU.add),
             reads=[kf, ang], writes=[ang])
        s.op("dve", lambda e: e.scalar_tensor_tensor(out=ang.t[:], in0=kf.t[:], scalar=-c2, in1=ang.t[:], op0=ALU.mult, op1=ALU.add),
             reads=[kf, ang], writes=[ang])
        s.op("dve", lambda e: e.tensor_scalar(out=m2.t[:], in0=ang.t[:], scalar1=math.pi / 2, scalar2=-TWO_PI, op0=ALU.is_gt, op1=ALU.mult),
             reads=[ang], writes=[m2])
        s.op("dve", lambda e: e.scalar_tensor_tensor(out=m2.t[:], in0=ang.t[:], scalar=math.pi / 2, in1=m2.t[:], op0=ALU.add, op1=ALU.add),
             reads=[ang, m2], writes=[m2])
        s.op("dve", lambda e: e.tensor_scalar(out=m2.t[:], in0=m2.t[:], scalar1=-lim, scalar2=lim, op0=ALU.max, op1=ALU.min),
             reads=[m2], writes=[m2])
        s.op("dve", lambda e: e.tensor_scalar(out=ang.t[:], in0=ang.t[:], scalar1=-lim, scalar2=lim, op0=ALU.max, op1=ALU.min),
             reads=[ang], writes=[ang])
        s.op("act", lambda e: e.activation(out=sin_out.t[:], in_=ang.t[:], func=AF.Sin), reads=[ang], writes=[sin_out])
        s.op("act", lambda e: e.activation(out=cos_out.t[:], in_=m2.t[:], func=AF.Sin), reads=[m2], writes=[cos_out])

    def prep_rope(self):
        s = self.s
        P = self
        posf = s.tile([128, 16], F32, "posf")
        fidx = s.tile([128, 64], F32, "fidx")
        invf = s.tile([128, 64], F32, "invf")
        ang = s.tile([128, 16, 64], F32, "rang")
        s.op("pool", lambda e: e.iota(posf.t[:], pattern=[[128, 16]], base=0, channel_multiplier=1,
                                      allow_small_or_imprecise_dtypes=True), writes=[posf])
        s.op("pool", lambda e: e.iota(fidx.t[:], pattern=[[1, 64]], base=0, channel_multiplier=0,
                                      allow_small_or_imprecise_dtypes=True), writes=[fidx])
        s.op("act", lambda e: e.activation(out=invf.t[:], in_=fidx.t[:], func=AF.Exp, scale=-math.log(10000.0) / 64.0),
             reads=[fidx], writes=[invf])
        s.op("dve", lambda e: e.tensor_tensor(out=ang.t[:], in0=posf.t[:].unsqueeze(2).to_broadcast([128, 16, 64]),
                                              in1=invf.t[:].unsqueeze(1).to_broadcast([128, 16, 64]), op=ALU.mult),
             reads=[posf, invf], writes=[ang])
        self.sincos(ang, [128, 16, 64], P.rsin, P.rcos, "r")

    def abar(self, are, aim, ldt_b, shape, tag):
        s = self.s
        step = s.tile(shape, F32, "step" + tag)
        sr = s.tile(shape, F32, "sr" + tag)
        si = s.tile(shape, F32, "si" + tag)
        mag = s.tile(shape, F32, "mag" + tag)
        sn = s.tile(shape, F32, "sn" + tag)
        cs = s.tile(shape, F32, "cs" + tag)
        ar = s.tile(shape, F32, "ar" + tag)
        ai = s.tile(shape, F32, "ai" + tag)
        ldt_tile, ldt_ap = ldt_b
        s.op("act", lambda e: e.activation(out=step.t[:], in_=ldt_ap(), func=AF.Exp), reads=[ldt_tile], writes=[step])
        s.op("dve", lambda e: e.tensor_scalar(out=are.t[:], in0=are.t[:], scalar1=-1e-4, scalar2=None, op0=ALU.min),
             reads=[are], writes=[are])
        s.op("dve", lambda e: e.tensor_tensor(out=sr.t[:], in0=step.t[:], in1=are.t[:], op=ALU.mult), reads=[step, are], writes=[sr])
        s.op("dve", lambda e: e.tensor_tensor(out=si.t[:], in0=step.t[:], in1=aim.t[:], op=ALU.mult), reads=[step, aim], writes=[si])
        s.op("act", lambda e: e.activation(out=mag.t[:], in_=sr.t[:], func=AF.Exp), reads=[sr], writes=[mag])
        self.sincos(si, shape, sn, cs, tag)
        s.op("dve", lambda e: e.tensor_tensor(out=ar.t[:], in0=mag.t[:], in1=cs.t[:], op=ALU.mult), reads=[mag, cs], writes=[ar])
        s.op("dve", lambda e: e.tensor_tensor(out=ai.t[:], in0=mag.t[:], in1=sn.t[:], op=ALU.mult), reads=[mag, sn], writes=[ai])
        return ar, ai

    def prep_ssm(self):
        s = self.s
        P = self
        a_re = self.input("ssm_a_re")
        a_im = self.input("ssm_a_im")
        ldt = self.input("ssm_log_dt")
        b_re = self.input("ssm_b_re")
        b_im = self.input("ssm_b_im")
        c_re = self.input("ssm_c_re")
        c_im = self.input("ssm_c_im")
        dsk = self.input("ssm_d")
        onesf = s.tile([128, 128], F32, "onesf")
        identf = s.tile([128, 128], F32, "identf")
        s.op("pool", lambda e: e.memset(onesf.t[:], 1.0), writes=[onesf])
        s.op("pool", lambda e: e.affine_select(out=identf.t[:], in_=onesf.t[:], pattern=[[-1, 128]],
                                               compare_op=ALU.is_equal, fill=0.0, base=0, channel_multiplier=1),
             reads=[onesf], writes=[identf])
        sel = s.tile([128, 64], F32, "sel")
        s.op("pool", lambda e: e.affine_select(out=sel.t[:], in_=onesf.t[:, 0:64], pattern=[[-2, 64]],
                                               compare_op=ALU.is_ge, fill=0.0, base=0, channel_multiplier=1),
             reads=[onesf], writes=[sel])
        s.op("pool", lambda e: e.affine_select(out=sel.t[:], in_=sel.t[:], pattern=[[2, 64]],
                                               compare_op=ALU.is_ge, fill=0.0, base=1, channel_multiplier=-1),
             reads=[sel], writes=[sel])
        pidx = s.tile([128, 1], I32, "pidx")
        pi2 = s.tile([128, 1], I32, "pi2")
        par1 = s.tile([128, 1], F32, "par1")
        par0 = s.tile([128, 1], F32, "par0")
        s.op("pool", lambda e: e.iota(pidx.t[:], pattern=[[0, 1]], base=0, channel_multiplier=1), writes=[pidx])
        s.op("dve", lambda e: e.tensor_scalar(out=pi2.t[:], in0=pidx.t[:], scalar1=1, scalar2=None, op0=ALU.bitwise_and),
             reads=[pidx], writes=[pi2])
        s.op("dve", lambda e: e.tensor_copy(out=par1.t[:], in_=pi2.t[:]), reads=[pi2], writes=[par1])
        s.op("dve", lambda e: e.tensor_scalar(out=par0.t[:], in0=par1.t[:], scalar1=-1.0, scalar2=1.0, op0=ALU.mult, op1=ALU.add),
             reads=[par1], writes=[par0])
        areS = s.tile([128, 64], F32, "areS")
        aimS = s.tile([128, 64], F32, "aimS")
        ldtS = s.tile([128, 64], F32, "ldtS")
        ldtc = s.tile([128, 1], F32, "ldtc")
        s.dma("sp", lambda e: e.dma_start(out=ldtc.t[:], in_=ldt.rearrange("(g o) -> g o", o=1)), ldtc, writes=[ldtc])
        for k, (dstS, srcD) in enumerate(((areS, a_re), (aimS, a_im), (ldtS, None))):
            ext = s.tile([128, 2, 64], F32, "aext")
            if srcD is not None:
                nat = s.tile([128, 64], F32, "anat")
                s.dma("sp", lambda e, nat=nat, srcD=srcD: e.dma_start(out=nat.t[:], in_=srcD), nat, writes=[nat])
                for two, par in enumerate((par0, par1)):
                    s.op("dve", lambda e, ext=ext, nat=nat, two=two, par=par: e.tensor_scalar(
                        out=ext.t[:, two, :], in0=nat.t[:], scalar1=par.t[:], scalar2=None, op0=ALU.mult),
                        reads=[nat, par], writes=[ext])
            else:
                for two, par in enumerate((par0, par1)):
                    s.op("dve", lambda e, ext=ext, two=two, par=par: e.tensor_scalar(
                        out=ext.t[:, two, :], in0=onesf.t[:, 0:64], scalar1=ldtc.t[:], scalar2=par.t[:], op0=ALU.mult, op1=ALU.mult),
                        reads=[onesf, ldtc, par], writes=[ext])
            bank = self.pb[k % 4]
            s.op("pe", lambda e, bank=bank, ext=ext: e.matmul(bank.t[:, 0:64], lhsT=ext.t[:].rearrange("p a n -> p (a n)"),
                                                              rhs=sel.t[:], start=True, stop=True), reads=[ext, sel], writes=[bank])
            s.op("act", lambda e, bank=bank, dstS=dstS: e.copy(out=dstS.t[:], in_=bank.t[:, 0:64]), reads=[bank], writes=[dstS])
        arS, aiS = self.abar(areS, aimS, (ldtS, lambda: ldtS.t[:]), [128, 64], "S")
        P.APW = s.tile([128, 8, 2, 2, 64], F32, "APW")
        APW = P.APW
        pt1 = s.tile([128, 64], F32, "pw1")
        pt2 = s.tile([128, 64], F32, "pw2")
        s.op("dve", lambda e: e.tensor_copy(out=APW.t[:, 0, 0, 0, :], in_=arS.t[:]), reads=[arS], writes=[APW])
        s.op("dve", lambda e: e.tensor_copy(out=APW.t[:, 0, 0, 1, :], in_=aiS.t[:]), reads=[aiS], writes=[APW])
        for m in range(1, 8):
            s.op("dve", lambda e, m=m: e.tensor_tensor(out=pt1.t[:], in0=APW.t[:, m - 1, 0, 0, :], in1=arS.t[:], op=ALU.mult),
                 reads=[APW, arS], writes=[pt1])
            s.op("dve", lambda e, m=m: e.tensor_tensor(out=pt2.t[:], in0=APW.t[:, m - 1, 0, 1, :], in1=aiS.t[:], op=ALU.mult),
                 reads=[APW, aiS], writes=[pt2])
            s.op("dve", lambda e, m=m: e.tensor_tensor(out=APW.t[:, m, 0, 0, :], in0=pt1.t[:], in1=pt2.t[:], op=ALU.subtract),
                 reads=[pt1, pt2], writes=[APW])
            s.op("dve", lambda e, m=m: e.tensor_tensor(out=pt1.t[:], in0=APW.t[:, m - 1, 0, 0, :], in1=aiS.t[:], op=ALU.mult),
                 reads=[APW, aiS], writes=[pt1])
            s.op("dve", lambda e, m=m: e.tensor_tensor(out=pt2.t[:], in0=APW.t[:, m - 1, 0, 1, :], in1=arS.t[:], op=ALU.mult),
                 reads=[APW, arS], writes=[pt2])
            s.op("dve", lambda e, m=m: e.tensor_tensor(out=APW.t[:, m, 0, 1, :], in0=pt1.t[:], in1=pt2.t[:], op=ALU.add),
                 reads=[pt1, pt2], writes=[APW])
        s.op("dve", lambda e: e.tensor_scalar(out=APW.t[:, :, 1, 0, :], in0=APW.t[:, :, 0, 1, :], scalar1=-1.0, scalar2=None, op0=ALU.mult),
             reads=[APW], writes=[APW])
        s.op("dve", lambda e: e.tensor_copy(out=APW.t[:, :, 1, 1, :], in_=APW.t[:, :, 0, 0, :]), reads=[APW], writes=[APW])
        if self._stage() < 1:
            return
        shS = [128, 64]
        den = s.tile(shS, F32, "den")
        t1 = s.tile(shS, F32, "t1")
        cre = s.tile(shS, F32, "cre")
        cim = s.tile(shS, F32, "cim")
        nr = s.tile(shS, F32, "nr")
        TT = lambda out, a, b, op: s.op("dve", lambda e: e.tensor_tensor(out=out.t[:], in0=a.t[:], in1=b.t[:], op=op),
                                        reads=[a, b], writes=[out])
        TT(den, areS, areS, ALU.mult)
        TT(t1, aimS, aimS, ALU.mult)
        TT(den, den, t1, ALU.add)
        s.op("dve", lambda e: e.reciprocal(out=den.t[:], in_=den.t[:]), reads=[den], writes=[den])
        s.op("dve", lambda e: e.tensor_scalar(out=nr.t[:], in0=arS.t[:], scalar1=-1.0, scalar2=None, op0=ALU.add),
             reads=[arS], writes=[nr])
        TT(cre, nr, areS, ALU.mult)
        TT(t1, aiS, aimS, ALU.mult)
        TT(cre, cre, t1, ALU.add)
        TT(cre, cre, den, ALU.mult)
        TT(cim, aiS, areS, ALU.mult)
        TT(t1, nr, aimS, ALU.mult)
        TT(cim, cim, t1, ALU.subtract)
        TT(cim, cim, den, ALU.mult)
        if self._stage() < 2:
            return
        shB = [128, 64, 16]
        bS = [s.tile(shB, F32, "bSre"), s.tile(shB, F32, "bSim")]
        for bt, bsrc in zip(bS, (b_re, b_im)):
            b2 = bsrc.rearrange("(j two) n q -> two n j q", two=2)
            for two in range(2):
                for jq in range(4):
                    s.dma("sp", lambda e, bt=bt, b2=b2, two=two, jq=jq: e.dma_start(
                        out=bt.t[two * 64:(two + 1) * 64, jq * 16:(jq + 1) * 16, :], in_=b2[two][:, jq * 16:(jq + 1) * 16, :]),
                        bt, writes=[bt], group="b")
        bb = [s.tile(shB, F32, "bbre"), s.tile(shB, F32, "bbim")]
        tb = s.tile(shB, F32, "tb")
        bc = lambda t_: t_.t[:].unsqueeze(2).to_broadcast(shB)
        TB = lambda out, co, bsrc, op=ALU.mult: s.op("dve", lambda e: e.tensor_tensor(out=out.t[:], in0=bsrc.t[:], in1=bc(co), op=op),
                                                     reads=[bsrc, co], writes=[out])
        TB(bb[0], cre, bS[0])
        TB(tb, cim, bS[1])
        TT(bb[0], bb[0], tb, ALU.subtract)
        TB(bb[1], cre, bS[1])
        TB(tb, cim, bS[0])
        TT(bb[1], bb[1], tb, ALU.add)
        if self._stage() < 3:
            return
        P.BW = [s.tile([128, 16, 128], F32, "BWre"), s.tile([128, 16, 128], F32, "BWim")]
        tcnt = 0
        for bw, bsrc in zip(P.BW, bb):
            bext = s.tile([128, 64, 2, 16], F32, "bext")
            s.op("pool", lambda e, bext=bext: e.memset(bext.t[:], 0.0), writes=[bext])
            s.op("dve", lambda e, bext=bext, bsrc=bsrc: e.tensor_copy(out=bext.t[0:64, :, 0, :], in_=bsrc.t[0:64, :, :]),
                 reads=[bsrc], writes=[bext])
            s.op("dve", lambda e, bext=bext, bsrc=bsrc: e.tensor_copy(out=bext.t[64:128, :, 1, :], in_=bsrc.t[64:128, :, :]),
                 reads=[bsrc], writes=[bext])
            for c in range(16):
                bank = self.pb[tcnt % 4]
                tcnt += 1
                s.op("pe", lambda e, bank=bank, bext=bext, c=c: e.transpose(
                    out=bank.t[:, 0:128], in_=bext.t[:, 4 * c:4 * c + 4, :, :].rearrange("p a b q -> p (a b q)"),
                    identity=identf.t[:]), reads=[bext, identf], writes=[bank])
                s.op("act", lambda e, bank=bank, bw=bw, c=c: e.copy(out=bw.t[:, c, :], in_=bank.t[:, 0:128]),
                     reads=[bank], writes=[bw])
        if self._stage() < 4:
            return
        pi = s.tile([128, 1], I32, "pidx4")
        m1 = s.tile([128, 1], F32, "m1")
        m0 = s.tile([128, 1], F32, "m0")
        s.op("dve", lambda e: e.tensor_scalar(out=pi.t[:], in0=pidx.t[:], scalar1=4, scalar2=1, op0=ALU.arith_shift_right,
                                              op1=ALU.bitwise_and), reads=[pidx], writes=[pi])
        s.op("dve", lambda e: e.tensor_copy(out=m1.t[:], in_=pi.t[:]), reads=[pi], writes=[m1])
        s.op("dve", lambda e: e.tensor_scalar(out=m0.t[:], in0=m1.t[:], scalar1=-1.0, scalar2=1.0, op0=ALU.mult, op1=ALU.add),
             reads=[m1], writes=[m0])
        P.CW = [s.tile([128, 64, 32], F32, "CWre"), s.tile([128, 64, 32], F32, "CWim")]
        for cw, csrc, sgn in ((P.CW[0], c_re, 1.0), (P.CW[1], c_im, -1.0)):
            cx = s.tile([128, 16, 64], F32, "cx")
            cext = s.tile([128, 16, 2, 64], F32, "cext")
            cv = csrc.rearrange("(c r) p n -> (r p) c n", r=8)
            for cq in range(4):
                s.dma("sp", lambda e, cx=cx, cv=cv, cq=cq: e.dma_start(out=cx.t[:, cq * 4:(cq + 1) * 4, :], in_=cv[:, cq * 4:(cq + 1) * 4, :]),
                      cx, writes=[cx], group="c")
            s.op("dve", lambda e, cx=cx, cext=cext, sgn=sgn: e.tensor_scalar(out=cext.t[:, :, 0, :], in0=cx.t[:], scalar1=m0.t[:],
                                                                           scalar2=sgn, op0=ALU.mult, op1=ALU.mult),
                 reads=[cx, m0], writes=[cext])
            s.op("dve", lambda e, cx=cx, cext=cext, sgn=sgn: e.tensor_scalar(out=cext.t[:, :, 1, :], in0=cx.t[:], scalar1=m1.t[:],
                                                                           scalar2=sgn, op0=ALU.mult, op1=ALU.mult),
                 reads=[cx, m1], writes=[cext])
            for c in range(16):
                bank = self.pb[tcnt % 4]
                tcnt += 1
                s.op("pe", lambda e, bank=bank, cext=cext, c=c: e.transpose(
                    out=bank.t[:, 0:128], in_=cext.t[:, c, :, :].rearrange("p a n -> p (a n)"),
                    identity=identf.t[:]), reads=[cext, identf], writes=[bank])
                s.op("act", lambda e, bank=bank, cw=cw, c=c: e.copy(
                    out=cw.t[:, 4 * c:4 * c + 4, :], in_=bank.t[:, 0:128].rearrange("p (a m) -> p a m", a=4)),
                    reads=[bank], writes=[cw])
        if self._stage() < 5:
            return
        P.Dcol = s.tile([128, 16], F32, "Dcol")
        dnat = s.tile([16, 128], F32, "dnat")
        s.dma("sp", lambda e: e.dma_start(out=dnat.t[:], in_=dsk.rearrange("(c p) -> c p", p=128)), dnat, writes=[dnat])
        bank = self.pb[tcnt % 4]
        s.op("pe", lambda e: e.transpose(out=bank.t[:, 0:16], in_=dnat.t[:], identity=identf.t[0:16, 0:16]),
             reads=[dnat, identf], writes=[bank])
        s.op("act", lambda e: e.copy(out=P.Dcol.t[:], in_=bank.t[:, 0:16]), reads=[bank], writes=[P.Dcol])
        if self._stage() < 6:
            return
        for nm, tl in self.ssm_const_list():
            dst = self.scratch(nm, [128, int(np.prod(list(tl.t.shape)[1:]))], F32)
            s.dma("sp", lambda e, dst=dst, tl=tl: e.dma_start(out=dst, in_=self.flat2(tl)), tl, reads=[tl])

    def _stage(self):
        import os
        return int(os.environ.get("SSM_STAGE", "99"))

    @staticmethod
    def flat2(tl):
        n = len(list(tl.t.shape))
        if n == 2:
            return tl.t[:]
        if n == 3:
            return tl.t[:].rearrange("p a b -> p (a b)")
        if n == 5:
            return tl.t[:].rearrange("p a b c d -> p (a b c d)")
        return tl.t[:].rearrange("p a b c -> p (a b c)")

    def ssm_const_list(self):
        P = self
        return [("c_APW", P.APW), ("c_BWre", P.BW[0]), ("c_BWim", P.BW[1]),
                ("c_CWre", P.CW[0]), ("c_CWim", P.CW[1]), ("c_D", P.Dcol)]

    def load_ssm_consts(self):
        s = self.s
        P = self
        P.APW = s.tile([128, 8, 2, 2, 64], F32, "APW")
        P.BW = [s.tile([128, 16, 128], F32, "BWre"), s.tile([128, 16, 128], F32, "BWim")]
        P.CW = [s.tile([128, 64, 32], F32, "CWre"), s.tile([128, 64, 32], F32, "CWim")]
        P.Dcol = s.tile([128, 16], F32, "Dcol")
        for nm, tl in self.ssm_const_list():
            src_ = self.scratch(nm, [128, int(np.prod(list(tl.t.shape)[1:]))], F32)
            s.dma("sp", lambda e, src_=src_, tl=tl: e.dma_start(out=self.flat2(tl), in_=src_), tl, writes=[tl])

    def prep_lambda(self):
        s = self.s
        P = self
        lam_init = 0.8 - 0.6 * math.exp(-0.3 * 1)
        P.lam_init = lam_init
        tl = [s.tile([128, 128], F32, f"lam{i}") for i in range(4)]
        for t_, nm in zip(tl, ("lam_q1", "lam_k1", "lam_q2", "lam_k2")):
            self.load_gain(t_, self.input(nm))
        d1 = s.tile([128, 1], F32, "d1")
        d2 = s.tile([128, 1], F32, "d2")
        jk = s.tile([128, 128], F32, "ljk")
        s.op("dve", lambda e: e.tensor_tensor(out=jk.t[:], in0=tl[0].t[:], in1=tl[1].t[:], op=ALU.mult), reads=[tl[0], tl[1]], writes=[jk])
        s.op("dve", lambda e: e.reduce_sum(out=d1.t[:], in_=jk.t[:], axis=mybir.AxisListType.X), reads=[jk], writes=[d1])
        s.op("dve", lambda e: e.tensor_tensor(out=jk.t[:], in0=tl[2].t[:], in1=tl[3].t[:], op=ALU.mult), reads=[tl[2], tl[3]], writes=[jk])
        s.op("dve", lambda e: e.reduce_sum(out=d2.t[:], in_=jk.t[:], axis=mybir.AxisListType.X), reads=[jk], writes=[d2])
        s.op("act", lambda e: e.activation(out=d1.t[:], in_=d1.t[:], func=AF.Exp), reads=[d1], writes=[d1])
        s.op("act", lambda e: e.activation(out=d2.t[:], in_=d2.t[:], func=AF.Exp), reads=[d2], writes=[d2])
        s.op("dve", lambda e: e.scalar_tensor_tensor(out=P.lam.t[:], in0=d1.t[:], scalar=lam_init, in1=d2.t[:], op0=ALU.add,
                                                     op1=ALU.subtract), reads=[d1, d2], writes=[P.lam])
        self.load_gain(P.gsub, self.input("attn_subln_g"))
        s.op("dve", lambda e: e.tensor_scalar(out=P.gsub.t[:], in0=P.gsub.t[:], scalar1=1.0 - lam_init, scalar2=None, op0=ALU.mult),
             reads=[P.gsub], writes=[P.gsub])

    def phase_l0a(self):
        s = self.s
        x = self.input("x")
        win = self.wb("ssm_w_in", [D, D]).rearrange("(kc p) n -> p kc n", p=128)
        UT = self.scratch("UT", [16, 128, L], F32)
        with s.phase("l0a"):
            self.conv_wait("l0a")
            self.mk_eps()
            gpre = s.tile([128, D], F32, "gpre")
            self.load_gain(gpre, self.input("mix_pre_g")[0])
            hb = [s.tile([128, D], F32, "hb") for _ in range(2)]
            junk = s.tile([128, D], BF16, "junk")
            xn = [s.tile([128, D], BF16, "xn") for _ in range(2)]
            rs = [s.tile([128, 1], F32, "rs") for _ in range(2)]
            xnT = s.tile([128, 16, 512], BF16, "xnT")
            wt = [s.tile([128, 16, 512], BF16, "win") for _ in range(4)]
            for cg in range(4):
                s.dma("sp", lambda e, cg=cg: e.dma_start(out=wt[cg].t[:], in_=win[:, :, cg * 512:(cg + 1) * 512]), wt[cg], writes=[wt[cg]])
            ut = [s.tile([128, 16, 512], F32, "ut") for _ in range(1)]
            cnt = 0
            wc = 0
            for blk in range(4):
                for ti in range(4):
                    tok = blk * 512 + ti * 128
                    h = hb[cnt % 2]
                    s.dma("sp", lambda e, h=h, tok=tok: e.dma_start(out=h.t[:], in_=x[tok:tok + 128, :]), h, writes=[h])
                    r = rs[cnt % 2]
                    xx = xn[cnt % 2]
                    self.norm_stats(h, r, junk)
                    s.op("dve", lambda e, xx=xx, h=h, r=r: e.scalar_tensor_tensor(
                        out=xx.t[:], in0=h.t[:], scalar=r.t[:], in1=gpre.t[:], op0=ALU.mult, op1=ALU.mult),
                        reads=[h, r, gpre], writes=[xx])
                    self.transpose_to(xx, xnT, ti * 128, 0)
                    cnt += 1
                u = ut[0]
                for cg in range(4):
                    w = wt[cg]
                    for j in range(4):
                        c = cg * 4 + j
                        bank = self.pb[c % 2]
                        for kc in range(16):
                            s.op("pe", lambda e, bank=bank, w=w, kc=kc, j=j: e.matmul(
                                bank.t[:], lhsT=w.t[:, kc, j * 128:(j + 1) * 128], rhs=xnT.t[:, kc, :],
                                start=(kc == 0), stop=(kc == 15)), reads=[w, xnT], writes=[bank])
                        s.op("act", lambda e, bank=bank, c=c: e.copy(out=u.t[:, c, :], in_=bank.t[:]), reads=[bank], writes=[u])
                s.dma("sp", lambda e, blk=blk: e.dma_start(out=UT[:, :, blk * 512:(blk + 1) * 512].rearrange("c p t -> p c t"),
                                                         in_=u.t[:]), u, reads=[u])

    def phase_l0b(self):
        s = self.s
        P = self
        UT = self.scratch("UT", [16, 128, L], F32)
        ZT = self.scratch("ZT", [16, 128, L], BF16)
        TS = 64
        R = 8
        NB = TS // R
        UB = 128
        with s.phase("l0b"):
            self.load_ssm_consts()
            ut = [s.tile([128, 16, UB], F32, "utb") for _ in range(2)]
            xb = [s.tile([128, TS, 2, 64], F32, "xb") for _ in range(2)]
            xr = [[s.dbuf(f"xr{i}_{r}") for r in range(R)] for i in range(2)]
            zt = [s.tile([128, 16, 512], BF16, "ztb") for _ in range(2)]
            zero = s.tile([128, 2, 64], F32, "zero")
            s.op("pool", lambda e: e.memset(zero.t[:], 0.0), writes=[zero])
            T1 = s.tile([128, NB, 2, 64], F32, "T1")
            T2 = s.tile([128, NB, 2, 64], F32, "T2")
            F1 = s.tile([128, NB, 2, 64], F32, "F1")
            F2 = s.tile([128, NB, 2, 64], F32, "F2")
            CAR = s.tile([128, NB, 2, 64], F32, "CAR")
            c1 = s.tile([128, 2, 64], F32, "c1")
            c2 = s.tile([128, 2, 64], F32, "c2")
            last = [s.tile([128, 2, 64], F32, "last") for _ in range(2)]
            tmp = [s.tile([128, 8, TS], F32, "gtmp") for _ in range(2)]
            yv = [s.tile([128, 8, TS], F32, "gyv") for _ in range(2)]
            sq = [s.tile([128, 8, TS], F32, "gsq") for _ in range(2)]
            sg = [s.tile([128, 8, TS], F32, "gsg") for _ in range(2)]
            APW, BW, CW, Dcol = P.APW, P.BW, P.CW, P.Dcol
            sh4 = [128, NB, 2, 64]
            ybank = [self.pb[4], self.pb[5]]
            gi = 0

            def emit_bu(tc):
                ub = (tc * TS) // UB
                t0 = (tc * TS) % UB
                u = ut[ub % 2]
                if t0 == 0:
                    s.dma("sp", lambda e, u=u, ub=ub: e.dma_start(
                        out=u.t[:], in_=UT[:, :, ub * UB:(ub + 1) * UB].rearrange("c p t -> p c t")), u, writes=[u])
                X = xb[tc % 2]
                XR = xr[tc % 2]
                for cgrp in range(4):
                    for cp in range(4):
                        c = cgrp * 4 + cp
                        for reim in range(2):
                            for r in range(4):
                                bank = self.pb[r]
                                o0 = (cp * 2 + reim) * TS
                                s.op("pe", lambda e, bank=bank, o0=o0, r=r, c=c, reim=reim, u=u, t0=t0: e.matmul(
                                    bank.t[:, o0:o0 + TS], lhsT=BW[reim].t[32 * r:32 * r + 32, c, :],
                                    rhs=u.t[32 * r:32 * r + 32, c, t0:t0 + TS], start=True, stop=True,
                                    tile_position=(32 * r, 0)), reads=[BW[reim], u], writes=[bank])
                    for r in range(4):
                        bank = self.pb[r]
                        j0 = 16 * cgrp + r
                        s.op("act", lambda e, bank=bank, j0=j0, X=X: e.copy(
                            out=X.t[:, :, :, j0:j0 + 13:4].rearrange("p t r c -> p c r t"),
                            in_=bank.t[:].rearrange("p (c r t) -> p c r t", c=4, r=2)), reads=[bank], writes=XR)

            def emit_rest(tc):
                nonlocal gi
                blk = tc // 8
                ub = (tc * TS) // UB
                t0 = (tc * TS) % UB
                z0 = (tc % 8) * TS
                u = ut[ub % 2]
                z = zt[blk % 2]
                X = xb[tc % 2]
                XR = xr[tc % 2]
                Xv = X.t[:].rearrange("p (k r) c j -> p k r c j", r=R)
                A1A = APW.t[:, 0, 0, :, :].unsqueeze(1).to_broadcast(sh4)
                A1B = APW.t[:, 0, 1, :, :].unsqueeze(1).to_broadcast(sh4)
                for r in range(1, R):
                    pre = Xv[:, :, r - 1, 0, :].unsqueeze(2).to_broadcast(sh4)
                    pim = Xv[:, :, r - 1, 1, :].unsqueeze(2).to_broadcast(sh4)
                    s.op("dve", lambda e, pre=pre: e.tensor_tensor(out=T1.t[:], in0=A1A, in1=pre, op=ALU.mult),
                         reads=[APW, XR[r - 1]], writes=[T1])
                    s.op("dve", lambda e, pim=pim: e.tensor_tensor(out=T2.t[:], in0=A1B, in1=pim, op=ALU.mult),
                         reads=[APW, XR[r - 1]], writes=[T2])
                    s.op("dve", lambda e: e.tensor_tensor(out=T1.t[:], in0=T1.t[:], in1=T2.t[:], op=ALU.add),
                         reads=[T1, T2], writes=[T1])
                    s.op("dve", lambda e, r=r, Xv=Xv: e.tensor_tensor(out=Xv[:, :, r, :, :], in0=Xv[:, :, r, :, :], in1=T1.t[:], op=ALU.add),
                         reads=[XR[r], T1], writes=[XR[r]])
                if tc == 0:
                    prev_b, prev_re, prev_im, prev_full = zero, zero.t[:, 0, :], zero.t[:, 1, :], zero.t[:]
                else:
                    Lp = last[(tc - 1) % 2]
                    prev_b = Lp
                    prev_re, prev_im, prev_full = Lp.t[:, 0, :], Lp.t[:, 1, :], Lp.t[:]
                carry_src = (prev_b, prev_full)
                for k in range(NB):
                    if k > 0:
                        prev_b = XR[R - 1]
                        prev_re, prev_im = X.t[:, k * R - 1, 0, :], X.t[:, k * R - 1, 1, :]
                    s.op("dve", lambda e, prev_re=prev_re: e.tensor_tensor(
                        out=c1.t[:], in0=APW.t[:, R - 1, 0, :, :], in1=prev_re.unsqueeze(1).to_broadcast([128, 2, 64]), op=ALU.mult),
                        reads=[APW, prev_b], writes=[c1])
                    s.op("dve", lambda e, prev_im=prev_im: e.tensor_tensor(
                        out=c2.t[:], in0=APW.t[:, R - 1, 1, :, :], in1=prev_im.unsqueeze(1).to_broadcast([128, 2, 64]), op=ALU.mult),
                        reads=[APW, prev_b], writes=[c2])
                    s.op("dve", lambda e: e.tensor_tensor(out=c1.t[:], in0=c1.t[:], in1=c2.t[:], op=ALU.add),
                         reads=[c1, c2], writes=[c1])
                    tk = k * R + R - 1
                    s.op("dve", lambda e, X=X, tk=tk: e.tensor_tensor(out=X.t[:, tk, :, :], in0=X.t[:, tk, :, :], in1=c1.t[:], op=ALU.add),
                         reads=[XR[R - 1], c1], writes=[XR[R - 1]])
                s.op("dve", lambda e, X=X, tc=tc: e.tensor_copy(out=last[tc % 2].t[:], in_=X.t[:, TS - 1, :, :]),
                     reads=[XR[R - 1]], writes=[last[tc % 2]])
                s.op(FIX_ENG, lambda e, src_=carry_src[1]: e.tensor_copy(out=CAR.t[:, 0, :, :], in_=src_),
                     reads=[carry_src[0]], writes=[CAR])
                s.op(FIX_ENG, lambda e, Xv=Xv: e.tensor_copy(out=CAR.t[:, 1:NB, :, :], in_=Xv[:, 0:NB - 1, R - 1, :, :]),
                     reads=[XR[R - 1]], writes=[CAR])
                cre = CAR.t[:, :, 0, :].unsqueeze(2).to_broadcast(sh4)
                cim = CAR.t[:, :, 1, :].unsqueeze(2).to_broadcast(sh4)
                for r in range(R - 1):
                    ApA = APW.t[:, r, 0, :, :].unsqueeze(1).to_broadcast(sh4)
                    ApB = APW.t[:, r, 1, :, :].unsqueeze(1).to_broadcast(sh4)
                    s.op(FIX_ENG, lambda e, ApA=ApA: e.tensor_tensor(out=F1.t[:], in0=ApA, in1=cre, op=ALU.mult),
                         reads=[APW, CAR], writes=[F1])
                    s.op(FIX_ENG, lambda e, ApB=ApB: e.tensor_tensor(out=F2.t[:], in0=ApB, in1=cim, op=ALU.mult),
                         reads=[APW, CAR], writes=[F2])
                    s.op(FIX_ENG, lambda e: e.tensor_tensor(out=F1.t[:], in0=F1.t[:], in1=F2.t[:], op=ALU.add),
                         reads=[F1, F2], writes=[F1])
                    s.op(FIX_ENG, lambda e, r=r, Xv=Xv: e.tensor_tensor(out=Xv[:, :, r, :, :], in0=Xv[:, :, r, :, :], in1=F1.t[:], op=ALU.add),
                         reads=[XR[r], F1], writes=[XR[r]])
                for half in range(2):
                    yb = ybank[half]
                    for cp in range(8):
                        c = half * 8 + cp
                        for r in range(4):
                            j = 4 * c + r
                            for reim in range(2):
                                s.op("pe", lambda e, yb=yb, cp=cp, r=r, j=j, reim=reim, X=X: e.matmul(
                                    yb.t[32 * r:32 * r + 32, cp * TS:(cp + 1) * TS], lhsT=CW[reim].t[:, j, :],
                                    rhs=X.t[:, :, reim, j], start=(reim == 0), stop=(reim == 1),
                                    tile_position=(0, 32 * r)), reads=[CW[reim]] + XR, writes=[yb])
                    tm, y_, sq_, sg_ = tmp[gi % 2], yv[gi % 2], sq[gi % 2], sg[gi % 2]
                    gi += 1
                    cs = slice(half * 8, half * 8 + 8)
                    s.op("pool", lambda e, tm=tm, u=u, cs=cs, t0=t0: e.tensor_tensor(
                        out=tm.t[:], in0=u.t[:, cs, t0:t0 + TS], in1=Dcol.t[:, cs].unsqueeze(2).to_broadcast([128, 8, TS]),
                        op=ALU.mult), reads=[u, Dcol], writes=[tm])
                    s.op("act", lambda e, y_=y_, yb=yb: e.copy(out=y_.t[:], in_=yb.t[:].rearrange("p (c t) -> p c t", c=8)),
                         reads=[yb], writes=[y_])
                    s.op("pool", lambda e, y_=y_, tm=tm: e.tensor_tensor(out=y_.t[:], in0=y_.t[:], in1=tm.t[:], op=ALU.add),
                         reads=[y_, tm], writes=[y_])
                    s.op("act", lambda e, y_=y_, sq_=sq_: e.activation(out=sq_.t[:], in_=y_.t[:], func=AF.Square),
                         reads=[y_], writes=[sq_])
                    s.op("pool", lambda e, sq_=sq_: e.tensor_scalar(out=sq_.t[:], in0=sq_.t[:], scalar1=0.044715, scalar2=1.0,
                                                                   op0=ALU.mult, op1=ALU.add), reads=[sq_], writes=[sq_])
                    s.op("pool", lambda e, sq_=sq_, y_=y_: e.tensor_tensor(out=sq_.t[:], in0=sq_.t[:], in1=y_.t[:], op=ALU.mult),
                         reads=[sq_, y_], writes=[sq_])
                    s.op("act", lambda e, sq_=sq_, sg_=sg_: e.activation(out=sg_.t[:], in_=sq_.t[:], func=AF.Sigmoid,
                                                                         scale=2.0 * math.sqrt(2.0 / math.pi)),
                         reads=[sq_], writes=[sg_])
                    s.op("pool", lambda e, sg_=sg_, y_=y_, z=z, cs=cs, z0=z0: e.tensor_tensor(
                        out=z.t[:, cs, z0:z0 + TS], in0=sg_.t[:], in1=y_.t[:], op=ALU.mult), reads=[sg_, y_], writes=[z])
                if tc % 8 == 7:
                    s.dma("sp", lambda e, z=z, blk=blk: e.dma_start(
                        out=ZT[:, :, blk * 512:(blk + 1) * 512].rearrange("c p t -> p c t"), in_=z.t[:]), z, reads=[z])

            NCH = L // TS
            emit_bu(0)
            for tc in range(NCH):
                if tc + 1 < NCH:
                    emit_bu(tc + 1)
                emit_rest(tc)

    def phase_l0c(self):
        s = self.s
        x = self.input("x")
        ZT = self.scratch("ZT", [16, 128, L], BF16)
        wglu = self.wb("ssm_w_glu", [D, 2 * D]).rearrange("(kc p) n -> p kc n", p=128)
        hA = self.scratch("hA", [L, D], F32)
        with s.phase("l0c"):
            self.conv_wait("l0c")
            self.mk_eps()
            gpost = s.tile([128, D], F32, "gpost")
            self.load_gain(gpost, self.input("mix_post_g")[0])
            zt = [s.tile([128, 16, 512], BF16, "zt") for _ in range(2)]
            wv = [s.tile([128, 16, 512], BF16, "wv") for _ in range(2)]
            wg = [s.tile([128, 16, 512], BF16, "wg") for _ in range(2)]
            sgt = [s.tile([128, 512], F32, "sgt") for _ in range(2)]
            mix = [s.tile([128, D], F32, "mix") for _ in range(8)]
            hb = [s.tile([128, D], F32, "hb") for _ in range(2)]
            junk = s.tile([128, D], BF16, "junk")
            rs = [s.tile([128, 1], F32, "rs") for _ in range(2)]
            wc = 0
            k = 0
            cnt = 0
            for bp in range(2):
                for b2 in range(2):
                    blk = bp * 2 + b2
                    z = zt[b2]
                    s.dma("sp", lambda e, z=z, blk=blk: e.dma_start(
                        out=z.t[:], in_=ZT[:, :, blk * 512:(blk + 1) * 512].rearrange("c p t -> p c t")), z, writes=[z])
                for j in range(4):
                    a, g = wv[wc % 2], wg[wc % 2]
                    wc += 1
                    s.dma("sp", lambda e, a=a, j=j: e.dma_start(out=a.t[:], in_=wglu[:, :, j * 512:(j + 1) * 512]), a, writes=[a])
                    s.dma("sp", lambda e, g=g, j=j: e.dma_start(out=g.t[:], in_=wglu[:, :, D + j * 512:D + (j + 1) * 512]), g, writes=[g])
                    for b2 in range(2):
                        z = zt[b2]
                        for ti in range(4):
                            pv, pg = self.pb[(k % 2) * 2], self.pb[(k % 2) * 2 + 1]
                            st = sgt[k % 2]
                            k += 1
                            for kc in range(16):
                                s.op("pe", lambda e, pg=pg, z=z, kc=kc, ti=ti, g=g: e.matmul(
                                    pg.t[:], lhsT=z.t[:, kc, ti * 128:(ti + 1) * 128], rhs=g.t[:, kc, :],
                                    start=(kc == 0), stop=(kc == 15)), reads=[z, g], writes=[pg])
                            for kc in range(16):
                                s.op("pe", lambda e, pv=pv, z=z, kc=kc, ti=ti, a=a: e.matmul(
                                    pv.t[:], lhsT=z.t[:, kc, ti * 128:(ti + 1) * 128], rhs=a.t[:, kc, :],
                                    start=(kc == 0), stop=(kc == 15)), reads=[z, a], writes=[pv])
                            s.op("act", lambda e, st=st, pg=pg: e.activation(out=st.t[:], in_=pg.t[:], func=AF.Sigmoid),
                                 reads=[pg], writes=[st])
                            m = mix[b2 * 4 + ti]
                            s.op("dve", lambda e, m=m, pv=pv, st=st, j=j: e.tensor_tensor(
                                out=m.t[:, j * 512:(j + 1) * 512], in0=pv.t[:], in1=st.t[:], op=ALU.mult),
                                reads=[pv, st], writes=[m])
                for b2 in range(2):
                    blk = bp * 2 + b2
                    for ti in range(4):
                        tok = blk * 512 + ti * 128
                        h = hb[cnt % 2]
                        r = rs[cnt % 2]
                        cnt += 1
                        s.dma("sp", lambda e, h=h, tok=tok: e.dma_start(out=h.t[:], in_=x[tok:tok + 128, :]), h, writes=[h])
                        self.post_norm_residual(mix[b2 * 4 + ti], h, gpost, r, junk, hA[tok:tok + 128, :])

    def phase_mlp(self, layer, hin_name, hout_name):
        s = self.s
        hin = self.scratch(hin_name, [L, D], F32)
        hout = self.scratch(hout_name, [L, D], F32)
        wup = self.wb(f"mlp_w_up{layer}", [D, DFF]).rearrange("(kc p) n -> p kc n", p=128)
        wdn = self.wb(f"mlp_w_down{layer}", [DFF, D]).rearrange("(fc p) n -> p fc n", p=128)
        with s.phase(f"mlp{layer}"):
            self.conv_wait(f"mlp{layer}")
            self.mk_eps()
            gpre = s.tile([128, D], F32, "gpre")
            gpost = s.tile([128, D], F32, "gpost")
            self.load_gain(gpre, self.input("mlp_pre_g")[layer])
            self.load_gain(gpost, self.input("mlp_post_g")[layer])
            hb = [s.tile([128, D], F32, "hb") for _ in range(2)]
            junk = s.tile([128, D], BF16, "junk")
            xn = [s.tile([128, D], BF16, "xn") for _ in range(2)]
            rs = [s.tile([128, 1], F32, "rs") for _ in range(2)]
            xnT = s.tile([128, 16, 512], BF16, "xnT")
            hidT = s.tile([128, 64, 512], BF16, "hidT")
            wu = [s.tile([128, 16, 256], BF16, "wu") for _ in range(2)]
            wd = [s.tile([128, 8, 512], BF16, "wd") for _ in range(2)]
            r32 = [s.tile([128, 512], F32, "r32") for _ in range(2)]
            ff = [s.tile([128, D], F32, "ff") for _ in range(4)]
            cnt = 0
            uc = 0
            dc = 0
            fcc = 0
            for blk in range(4):
                for ti in range(4):
                    tok = blk * 512 + ti * 128
                    h = hb[cnt % 2]
                    r = rs[cnt % 2]
                    xx = xn[cnt % 2]
                    cnt += 1
                    s.dma("sp", lambda e, h=h, tok=tok: e.dma_start(out=h.t[:], in_=hin[tok:tok + 128, :]), h, writes=[h])
                    self.norm_stats(h, r, junk)
                    s.op("dve", lambda e, xx=xx, h=h, r=r: e.scalar_tensor_tensor(
                        out=xx.t[:], in0=h.t[:], scalar=r.t[:], in1=gpre.t[:], op0=ALU.mult, op1=ALU.mult),
                        reads=[h, r, gpre], writes=[xx])
                    self.transpose_to(xx, xnT, ti * 128, 0)
                for fg in range(32):
                    w = wu[uc % 2]
                    uc += 1
                    s.dma("sp", lambda e, w=w, fg=fg: e.dma_start(out=w.t[:], in_=wup[:, :, fg * 256:(fg + 1) * 256]), w, writes=[w])
                    for j in range(2):
                        fc = fg * 2 + j
                        bank = self.pb[4 + fcc % 2]
                        rr = r32[fcc % 2]
                        fcc += 1
                        for kc in range(16):
                            s.op("pe", lambda e, bank=bank, w=w, kc=kc, j=j: e.matmul(
                                bank.t[:], lhsT=w.t[:, kc, j * 128:(j + 1) * 128], rhs=xnT.t[:, kc, :],
                                start=(kc == 0), stop=(kc == 15)), reads=[w, xnT], writes=[bank])
                        s.op("act", lambda e, rr=rr, bank=bank: e.activation(out=rr.t[:], in_=bank.t[:], func=AF.Relu),
                             reads=[bank], writes=[rr])
                        s.op("pool", lambda e, rr=rr, fc=fc: e.tensor_tensor(out=hidT.t[:, fc, :], in0=rr.t[:], in1=rr.t[:], op=ALU.mult),
                             reads=[rr], writes=[hidT])
                for c in range(4):
                    for fg in range(8):
                        w = wd[dc % 2]
                        dc += 1
                        s.dma("sp", lambda e, w=w, fg=fg, c=c: e.dma_start(
                            out=w.t[:], in_=wdn[:, fg * 8:(fg + 1) * 8, c * 512:(c + 1) * 512]), w, writes=[w])
                        for ti in range(4):
                            bank = self.pb[ti]
                            for j in range(8):
                                fc = fg * 8 + j
                                s.op("pe", lambda e, bank=bank, fc=fc, ti=ti, w=w, j=j: e.matmul(
                                    bank.t[:], lhsT=hidT.t[:, fc, ti * 128:(ti + 1) * 128], rhs=w.t[:, j, :],
                                    start=(fc == 0), stop=(fc == 63)), reads=[hidT, w], writes=[bank])
                    for ti in range(4):
                        bank = self.pb[ti]
                        f = ff[ti]
                        if ti % 2 == 0:
                            s.op("act", lambda e, f=f, bank=bank, c=c: e.copy(out=f.t[:, c * 512:(c + 1) * 512], in_=bank.t[:]),
                                 reads=[bank], writes=[f])
                        else:
                            s.op("dve", lambda e, f=f, bank=bank, c=c: e.tensor_copy(out=f.t[:, c * 512:(c + 1) * 512], in_=bank.t[:]),
                                 reads=[bank], writes=[f])
                for ti in range(4):
                    tok = blk * 512 + ti * 128
                    h = hb[cnt % 2]
                    r = rs[cnt % 2]
                    cnt += 1
                    s.dma("sp", lambda e, h=h, tok=tok: e.dma_start(out=h.t[:], in_=hin[tok:tok + 128, :]), h, writes=[h])
                    self.post_norm_residual(ff[ti], h, gpost, r, junk, hout[tok:tok + 128, :])

    def rope(self, src, dst, nhm, ti, tA, tB, scale=None):
        s = self.s
        P = self
        cos = P.rcos.t[:, ti, :]
        sin = P.rsin.t[:, ti, :]
        A = src.t[:].rearrange("p (h two f) -> p h two f", two=2, f=64)
        O = dst.t[:].rearrange("p (h two f) -> p h two f", two=2, f=64)
        M1 = tA.t[:].rearrange("p (h two f) -> p h two f", two=2, f=64)
        M2 = tB.t[:].rearrange("p (h two f) -> p h two f", two=2, f=64)
        cb4 = cos.unsqueeze(1).unsqueeze(1).to_broadcast([128, nhm, 2, 64])
        sb3 = sin.unsqueeze(1).to_broadcast([128, nhm, 64])
        s.op("pool", lambda e: e.tensor_tensor(out=M1, in0=A, in1=cb4, op=ALU.mult), reads=[src, P.rcos], writes=[tA])
        s.op("pool", lambda e: e.tensor_tensor(out=M2[:, :, 0, :], in0=A[:, :, 1, :], in1=sb3, op=ALU.mult),
             reads=[src, P.rsin], writes=[tB])
        s.op("pool", lambda e: e.tensor_tensor(out=M2[:, :, 1, :], in0=A[:, :, 0, :], in1=sb3, op=ALU.mult),
             reads=[src, P.rsin], writes=[tB])
        s.op("dve", lambda e: e.tensor_tensor(out=O[:, :, 0, :], in0=M1[:, :, 0, :], in1=M2[:, :, 0, :], op=ALU.subtract),
             reads=[tA, tB], writes=[dst])
        s.op("dve", lambda e: e.tensor_tensor(out=O[:, :, 1, :], in0=M1[:, :, 1, :], in1=M2[:, :, 1, :], op=ALU.add),
             reads=[tA, tB], writes=[dst])

    def phase_qkv(self):
        s = self.s
        P = self
        hB = self.scratch("hB", [L, D], F32)
        wkv = self.wb("w_kv", [D, 2 * D]).rearrange("(kc p) n -> p kc n", p=128)
        wq = self.wb("attn_w_q", [D, D]).rearrange("(kc p) n -> p kc n", p=128)
        KT = self.scratch("KT", [16, 128, L], BF16)
        QT = self.scratch("QT", [16, 128, L], BF16)
        V = self.scratch("V", [L, D], BF16)
        scale = 128 ** -0.5
        with s.phase("qkv"):
            self.conv_wait("qkv")
            self.mk_eps()
            gkv = s.tile([128, D], F32, "gkv")
            gq = s.tile([128, D], F32, "gq")
            self.load_gain(gkv, self.input("kv_norm_g"))
            self.load_gain(gq, self.input("mix_pre_g")[1])
            hb = [s.tile([128, D], F32, "hb") for _ in range(2)]
            junk = s.tile([128, D], BF16, "junk")
            xk = [s.tile([128, D], BF16, "xk") for _ in range(2)]
            xq = [s.tile([128, D], BF16, "xq") for _ in range(2)]
            rs = [s.tile([128, 1], F32, "rs") for _ in range(2)]
            xkT = s.tile([128, 16, 512], BF16, "xkT")
            xqT = s.tile([128, 16, 512], BF16, "xqT")
            wt = [s.tile([128, 16, 512], BF16, "wt") for _ in range(2)]
            kx = [s.tile([128, 512], F32, "kx") for _ in range(2)]
            kr = [s.tile([128, 512], BF16, "kr") for _ in range(3)]
            tA = [s.tile([128, 512], F32, "tA") for _ in range(2)]
            tB = [s.tile([128, 512], F32, "tB") for _ in range(2)]
            kTb = [s.tile([128, 16, 512], BF16, "kTb") for _ in range(2)]
            vb = s.tile([128, 4, D], BF16, "vb")
            cnt = 0
            wc = 0
            pc = 0
            for blk in range(4):
                for ti in range(4):
                    tok = blk * 512 + ti * 128
                    h = hb[cnt % 2]
                    r = rs[cnt % 2]
                    a, b = xk[cnt % 2], xq[cnt % 2]
                    cnt += 1
                    s.dma("sp", lambda e, h=h, tok=tok: e.dma_start(out=h.t[:], in_=hB[tok:tok + 128, :]), h, writes=[h])
                    self.norm_stats(h, r, junk)
                    s.op("dve", lambda e, a=a, h=h, r=r: e.scalar_tensor_tensor(
                        out=a.t[:], in0=h.t[:], scalar=r.t[:], in1=gkv.t[:], op0=ALU.mult, op1=ALU.mult),
                        reads=[h, r, gkv], writes=[a])
                    s.op("dve", lambda e, b=b, h=h, r=r: e.scalar_tensor_tensor(
                        out=b.t[:], in0=h.t[:], scalar=r.t[:], in1=gq.t[:], op0=ALU.mult, op1=ALU.mult),
                        reads=[h, r, gq], writes=[b])
                    self.transpose_to(a, xkT, ti * 128, 0)
                    self.transpose_to(b, xqT, ti * 128, 0)
                for which in range(2):
                    pending = None
                    wsrc = wkv if which == 0 else wq
                    xT = xkT if which == 0 else xqT
                    dstT = kTb[which]
                    for cb in range(4):
                        w = wt[wc % 2]
                        wc += 1
                        s.dma("sp", lambda e, w=w, wsrc=wsrc, cb=cb: e.dma_start(out=w.t[:], in_=wsrc[:, :, cb * 512:(cb + 1) * 512]),
                              w, writes=[w])
                        for ti in range(4):
                            bank = self.pb[pc % 2]
                            kx_, kr_ = kx[pc % 2], kr[pc % 3]
                            pc += 1
                            for kc in range(16):
                                s.op("pe", lambda e, bank=bank, xT=xT, kc=kc, ti=ti, w=w: e.matmul(
                                    bank.t[:], lhsT=xT.t[:, kc, ti * 128:(ti + 1) * 128], rhs=w.t[:, kc, :],
                                    start=(kc == 0), stop=(kc == 15)), reads=[xT, w], writes=[bank])
                            if which == 0:
                                s.op("act", lambda e, kx_=kx_, bank=bank: e.copy(out=kx_.t[:], in_=bank.t[:]), reads=[bank], writes=[kx_])
                            else:
                                s.op("act", lambda e, kx_=kx_, bank=bank: e.mul(out=kx_.t[:], in_=bank.t[:], mul=scale),
                                     reads=[bank], writes=[kx_])
                            self.rope(kx_, kr_, 4, blk * 4 + ti, tA[pc % 2], tB[pc % 2])
                            if pending is not None:
                                pending()

                            def mk(pt=self.pt[pc % 2], kr_=kr_, dstT=dstT, cb=cb, ti=ti):
                                def f():
                                    for hm in range(4):
                                        s.op("pe", lambda e, hm=hm: e.transpose(
                                            out=pt.t[:, hm * 128:(hm + 1) * 128], in_=kr_.t[:, hm * 128:(hm + 1) * 128],
                                            identity=P.ident.t[:]), reads=[kr_, P.ident], writes=[pt])
                                    s.op("dve", lambda e: e.tensor_copy(
                                        out=dstT.t[:, cb * 4:(cb + 1) * 4, ti * 128:(ti + 1) * 128],
                                        in_=pt.t[:, 0:512].rearrange("p (h t) -> p h t", h=4)), reads=[pt], writes=[dstT])
                                return f
                            pending = mk()
                    if pending is not None:
                        pending()
                        pending = None
                    dd = KT if which == 0 else QT
                    s.dma("sp", lambda e, dd=dd, dstT=dstT, blk=blk: e.dma_start(
                        out=dd[:, :, blk * 512:(blk + 1) * 512].rearrange("h p t -> p h t"), in_=dstT.t[:]), dstT, reads=[dstT])
                for cb in range(4):
                    w = wt[wc % 2]
                    wc += 1
                    s.dma("sp", lambda e, w=w, cb=cb: e.dma_start(out=w.t[:], in_=wkv[:, :, D + cb * 512:D + (cb + 1) * 512]), w, writes=[w])
                    for ti in range(4):
                        bank = self.pb[pc % 2]
                        pc += 1
                        for kc in range(16):
                            s.op("pe", lambda e, bank=bank, kc=kc, ti=ti, w=w: e.matmul(
                                bank.t[:], lhsT=xkT.t[:, kc, ti * 128:(ti + 1) * 128], rhs=w.t[:, kc, :],
                                start=(kc == 0), stop=(kc == 15)), reads=[xkT, w], writes=[bank])
                        s.op("act", lambda e, bank=bank, ti=ti, cb=cb: e.copy(out=vb.t[:, ti, cb * 512:(cb + 1) * 512], in_=bank.t[:]),
                             reads=[bank], writes=[vb])
                s.dma("sp", lambda e, blk=blk: e.dma_start(
                    out=V[blk * 512:(blk + 1) * 512, :].rearrange("(t p) n -> p t n", p=128), in_=vb.t[:]), vb, reads=[vb])

    def phase_attn(self):
        s = self.s
        P = self
        hB = self.scratch("hB", [L, D], F32)
        hC = self.scratch("hC", [L, D], F32)
        KT = self.scratch("KT", [16, 128, L], BF16)
        QT = self.scratch("QT", [16, 128, L], BF16)
        V = self.scratch("V", [L, D], BF16)
        wo = self.wb("attn_w_o", [D, D]).rearrange("(kc p) n -> p kc n", p=128)
        with s.phase("attn"):
            self.conv_wait("attn")
            self.mk_eps()
            gpost = s.tile([128, D], F32, "gpost")
            self.load_gain(gpost, self.input("mix_post_g")[1])
            kT = [s.tile([128, L], BF16, "kT") for _ in range(2)]
            qT = [s.tile([128, 512], BF16, "qT") for _ in range(2)]
            vt = [s.tile([128, 16, 257], BF16, "vt") for _ in range(2)]
            for v_ in vt:
                s.op("pool", lambda e, v_=v_: e.memset(v_.t[:, :, 256:257], 1.0), writes=[v_])
            pT = [s.tile([128, 512], BF16, "pT") for _ in range(3)]
            om4 = [s.tile([128, 4, 257], F32, "om") for _ in range(4)]
            obf = s.tile([128, 4, D], BF16, "obf")
            oT = s.tile([128, 16, 512], BF16, "oT")
            rr = [s.tile([128, 1], F32, "rr") for _ in range(4)]
            tt = s.tile([128, 256], F32, "tt")
            oo = s.tile([128, 256], F32, "oo")
            jk2 = s.tile([128, 256], F32, "jk2")
            wt = [s.tile([128, 16, 512], BF16, "wo") for _ in range(2)]
            att = [s.tile([128, D], F32, "att") for _ in range(4)]
            hb = [s.tile([128, D], F32, "hb") for _ in range(2)]
            junk = s.tile([128, D], BF16, "junk")
            rs = [s.tile([128, 1], F32, "rs") for _ in range(2)]
            eps256 = s.tile([128, 1], F32, "eps256")
            s.op("pool", lambda e: e.memset(eps256.t[:], EPS), writes=[eps256])
            stb = [self.pb[4], self.pb[5]]
            kc_ = 0
            pc = 0
            stc = 0
            wc = 0
            cnt = 0
            for sb in range(4):
                nkt = 4 * (sb + 1)
                for hh in range(8):
                    om = om4[(hh % 2) * 2:(hh % 2) * 2 + 2]
                    v_ = vt[(sb * 8 + hh) % 2]
                    s.dma("sp", lambda e, v_=v_, hh=hh, nkt=nkt: e.dma_start(
                        out=v_.t[:, 0:nkt, 0:256],
                        in_=V[0:nkt * 128, hh * 256:(hh + 1) * 256].rearrange("(t p) n -> p t n", p=128)), v_, writes=[v_])
                    for m in range(2):
                        hm = hh * 2 + m
                        k_ = kT[kc_ % 2]
                        q_ = qT[kc_ % 2]
                        kc_ += 1
                        s.dma("sp", lambda e, k_=k_, hm=hm, nkt=nkt: e.dma_start(out=k_.t[:, 0:nkt * 128], in_=KT[hm, :, 0:nkt * 128]),
                              k_, writes=[k_])
                        s.dma("sp", lambda e, q_=q_, hm=hm, sb=sb: e.dma_start(out=q_.t[:], in_=QT[hm, :, sb * 512:(sb + 1) * 512]),
                              q_, writes=[q_])
                        def emit_st(j):
                            nonlocal stc, pc
                            qlo = max(0, j - 4 * sb)
                            n = 512 - qlo * 128
                            st = stb[stc % 2]
                            stc += 1
                            p_ = pT[pc % 3]
                            pc += 1
                            s.op("pe", lambda e, st=st, k_=k_, q_=q_, j=j, qlo=qlo, n=n: e.matmul(
                                st.t[:, 0:n], lhsT=k_.t[:, j * 128:(j + 1) * 128], rhs=q_.t[:, qlo * 128:512],
                                start=True, stop=True), reads=[k_, q_], writes=[st])
                            s.op("act", lambda e, st=st, p_=p_, n=n: e.activation(out=p_.t[:, 0:n], in_=st.t[:, 0:n], func=AF.Exp),
                                 reads=[st], writes=[p_])
                            if j >= 4 * sb:
                                s.op("pool", lambda e, p_=p_: e.tensor_tensor(out=p_.t[:, 0:128], in0=p_.t[:, 0:128],
                                                                              in1=P.cmask.t[:], op=ALU.mult),
                                     reads=[p_, P.cmask], writes=[p_])
                            return p_, qlo
                        nxt = emit_st(0)
                        for j in range(nkt):
                            p_, qlo = nxt
                            if j + 1 < nkt:
                                nxt = emit_st(j + 1)
                            for qt in range(qlo, 4):
                                ob = self.pb[qt]
                                s.op("pe", lambda e, ob=ob, p_=p_, qt=qt, qlo=qlo, v_=v_, j=j, sb=sb: e.matmul(
                                    ob.t[:, 0:257], lhsT=p_.t[:, (qt - qlo) * 128:(qt - qlo + 1) * 128], rhs=v_.t[:, j, :],
                                    start=(j == 0), stop=(j == 4 * sb + qt)), reads=[p_, v_], writes=[ob])
                        o_ = om[m]
                        for qt in range(4):
                            ob = self.pb[qt]
                            if qt % 2 == 0:
                                s.op("act", lambda e, o_=o_, ob=ob, qt=qt: e.copy(out=o_.t[:, qt, :], in_=ob.t[:, 0:257]),
                                     reads=[ob], writes=[o_])
                            else:
                                s.op("dve", lambda e, o_=o_, ob=ob, qt=qt: e.tensor_copy(out=o_.t[:, qt, :], in_=ob.t[:, 0:257]),
                                     reads=[ob], writes=[o_])
                    for qt in range(4):
                        r1, r2, r3 = rr[0], rr[1], rr[2]
                        s.op("dve", lambda e, qt=qt, om0=om[0], om1=om[1]: e.reciprocal(out=r1.t[:], in_=om0.t[:, qt, 256:257]), reads=[om[0]], writes=[r1])
                        s.op("dve", lambda e, qt=qt, om0=om[0], om1=om[1]: e.reciprocal(out=r2.t[:], in_=om1.t[:, qt, 256:257]), reads=[om[1]], writes=[r2])
                        s.op("dve", lambda e: e.tensor_tensor(out=r2.t[:], in0=r2.t[:], in1=P.lam.t[:], op=ALU.mult),
                             reads=[r2, P.lam], writes=[r2])
                        s.op("dve", lambda e, qt=qt, om0=om[0], om1=om[1]: e.tensor_scalar(out=tt.t[:], in0=om1.t[:, qt, 0:256], scalar1=r2.t[:], scalar2=None,
                                                                     op0=ALU.mult), reads=[om[1], r2], writes=[tt])
                        s.op("dve", lambda e, qt=qt, om0=om[0], om1=om[1]: e.scalar_tensor_tensor(out=oo.t[:], in0=om0.t[:, qt, 0:256], scalar=r1.t[:],
                                                                            in1=tt.t[:], op0=ALU.mult, op1=ALU.subtract),
                             reads=[om[0], r1, tt], writes=[oo])
                        s.op("act", lambda e: e.activation(out=jk2.t[:], in_=oo.t[:], func=AF.Square, accum_out=r3.t[:]),
                             reads=[oo], writes=[jk2, r3])
                        s.op("act", lambda e: e.activation(out=r3.t[:], in_=r3.t[:], func=AF.Sqrt, scale=1.0 / 256, bias=eps256.t[:]),
                             reads=[r3, eps256], writes=[r3])
                        s.op("dve", lambda e: e.reciprocal(out=r3.t[:], in_=r3.t[:]), reads=[r3], writes=[r3])
                        s.op("dve", lambda e, qt=qt, hh=hh: e.scalar_tensor_tensor(
                            out=obf.t[:, qt, hh * 256:(hh + 1) * 256], in0=oo.t[:], scalar=r3.t[:], in1=P.gsub.t[:],
                            op0=ALU.mult, op1=ALU.mult), reads=[oo, r3, P.gsub], writes=[obf])
                for qt in range(4):
                    for half in range(2):
                        pt = self.pt[half]
                        for k in range(8):
                            kc = half * 8 + k
                            s.op("pe", lambda e, pt=pt, k=k, kc=kc, qt=qt: e.transpose(
                                out=pt.t[:, k * 128:(k + 1) * 128], in_=obf.t[:, qt, kc * 128:(kc + 1) * 128],
                                identity=P.ident.t[:]), reads=[obf, P.ident], writes=[pt])
                        s.op("dve" if half else "act",
                             (lambda e, pt=pt, half=half, qt=qt: e.tensor_copy(
                                 out=oT.t[:, half * 8:(half + 1) * 8, qt * 128:(qt + 1) * 128],
                                 in_=pt.t[:].rearrange("p (k t) -> p k t", k=8))) if half else
                             (lambda e, pt=pt, half=half, qt=qt: e.copy(
                                 out=oT.t[:, half * 8:(half + 1) * 8, qt * 128:(qt + 1) * 128],
                                 in_=pt.t[:].rearrange("p (k t) -> p k t", k=8))),
                             reads=[pt], writes=[oT])
                for cb in range(4):
                    w = wt[wc % 2]
                    wc += 1
                    s.dma("sp", lambda e, w=w, cb=cb: e.dma_start(out=w.t[:], in_=wo[:, :, cb * 512:(cb + 1) * 512]), w, writes=[w])
                    for qt in range(4):
                        bank = stb[stc % 2]
                        stc += 1
                        for kc in range(16):
                            s.op("pe", lambda e, bank=bank, kc=kc, qt=qt, w=w: e.matmul(
                                bank.t[:], lhsT=oT.t[:, kc, qt * 128:(qt + 1) * 128], rhs=w.t[:, kc, :],
                                start=(kc == 0), stop=(kc == 15)), reads=[oT, w], writes=[bank])
                        a_ = att[qt]
                        s.op("act", lambda e, a_=a_, bank=bank, cb=cb: e.copy(out=a_.t[:, cb * 512:(cb + 1) * 512], in_=bank.t[:]),
                             reads=[bank], writes=[a_])
                for qt in range(4):
                    tok = sb * 512 + qt * 128
                    h = hb[cnt % 2]
                    r = rs[cnt % 2]
                    cnt += 1
                    s.dma("sp", lambda e, h=h, tok=tok: e.dma_start(out=h.t[:], in_=hB[tok:tok + 128, :]), h, writes=[h])
                    self.post_norm_residual(att[qt], h, gpost, r, junk, hC[tok:tok + 128, :])


def _in_map(inputs, b):
    m = {}
    for k, v in inputs.items():
        a = np.asarray(v)
        if k == "x":
            a = a[b]
        m[k] = np.ascontiguousarray(a.reshape(INPUT_SHAPES[k]), dtype=np.float32)
    return m


def kernel(**inputs):
    prog = Prog()
    nc = prog.build()
    used = set(prog.inp.keys())
    in_maps = []
    for b in range(NCORES):
        m = _in_map(inputs, b)
        in_maps.append({k: v for k, v in m.items() if k in used})
    res = run_bass_kernel_spmd(nc, in_maps, core_ids=list(range(NCORES)))
    out = np.stack([np.asarray(res.results[b]["out"], dtype=np.float32) for b in range(NCORES)], axis=0)
    return out
```

```python
import contextlib
import math
import numpy as np
import concourse.bass as bass
import concourse.mybir as mybir
from concourse.bass_utils import run_bass_kernel_spmd

F32 = mybir.dt.float32
BF16 = mybir.dt.bfloat16
I32 = mybir.dt.int32
ALU = mybir.AluOpType
AF = mybir.ActivationFunctionType

D = 2048
L = 2048
DFF = 8192
NCORES = 8
EPS = 1e-6
ENGS = ("pe", "act", "dve", "pool", "sp")
SEM_CHUNK = 30000
TWO_PI = 2.0 * math.pi
FIX_ENG = "dve"
SAME_ENG_ALL = True


class Buf:
    __slots__ = ("name", "last_w", "readers", "dsem", "dbase", "dcount", "last_group", "cur_group_id", "psum")

    def __init__(self, name, psum=False):
        self.name = name
        self.last_w = None
        self.readers = []
        self.dsem = None
        self.dcount = 0
        self.last_group = []
        self.cur_group_id = None
        self.psum = psum


class Op:
    __slots__ = ("eng", "fn", "deps", "is_dma", "dbuf", "dgroup", "ticket", "has_dep")

    def __init__(self, eng, fn):
        self.eng = eng
        self.fn = fn
        self.deps = []
        self.is_dma = False
        self.dbuf = None
        self.dgroup = None
        self.ticket = None
        self.has_dep = False


class Group:
    __slots__ = ("end",)

    def __init__(self):
        self.end = 0


class Tile:
    def __init__(self, sched, t, name, psum=False):
        self.s = sched
        self.t = t
        self.name = name
        self.psum = psum
        self._b = None
        self._ph = -1

    @property
    def b(self):
        if self._ph != self.s.phase_id:
            self._b = Buf(self.name, self.psum)
            self._ph = self.s.phase_id
        return self._b


class Sched:
    def __init__(self, nc, nsem_eng=4, ndma=64):
        self.nc = nc
        self.stack = contextlib.ExitStack()
        self.esem = {e: [self.stack.enter_context(nc.semaphore(f"s_{e}{i}")) for i in range(nsem_eng)]
                     for e in ENGS if e != "sp"}
        self.dpool = [[self.stack.enter_context(nc.semaphore(f"d{i}")), 0] for i in range(ndma)]
        self.ticket = {e: 0 for e in ENGS}
        self.ops = {e: [] for e in ENGS}
        self.phase_id = 0
        self.phase_dma_bufs = []
        self.barrier = []
        self.pstack = None
        self.ntile = 0
        self.total_ops = 0

    def ptile(self, shape, dtype, name=None):
        self.ntile += 1
        name = name or f"t{self.ntile}"
        t = self.stack.enter_context(self.nc.sbuf_tensor(name, list(shape), dtype))
        return Tile(self, t, name)

    def tile(self, shape, dtype, name=None):
        self.ntile += 1
        name = (name or "t") + f"_{self.ntile}"
        t = self.pstack.enter_context(self.nc.sbuf_tensor(name, list(shape), dtype))
        return Tile(self, t, name)

    def psum_tile(self, shape, dtype, name):
        t = self.stack.enter_context(self.nc.psum_tensor(name, list(shape), dtype))
        return Tile(self, t, name, psum=True)

    def dbuf(self, name):
        return Buf(name)

    @staticmethod
    def _b(x):
        return x.b if isinstance(x, Tile) else x

    def _track(self, op, reads, writes):
        deps = op.deps
        wr = [self._b(w) for w in writes]
        for r in reads:
            b = self._b(r)
            if b.psum:
                wr.append(b)
                continue
            if b.last_w is not None:
                deps.append(("raw", b.last_w))
            b.readers.append(op)
        for b in wr:
            if b.last_w is not None:
                deps.append(("raw" if b.psum else "waw", b.last_w))
            for r in b.readers:
                if r is not op:
                    deps.append(("war", r))
            b.readers = []
            b.last_w = op

    def op(self, eng, fn, reads=(), writes=()):
        o = Op(eng, fn)
        self._track(o, reads, writes)
        self.ops[eng].append(o)
        return o

    def dma(self, eng, fn, dtile, reads=(), writes=(), group=None, background=False):
        dbuf = self._b(dtile)
        o = Op(eng, fn)
        o.is_dma = True
        o.dbuf = dbuf
        if dbuf.dsem is None:
            ent = self.dpool.pop(0)
            dbuf.dsem = ent
            dbuf.dcount = ent[1]
            if not background:
                self.phase_dma_bufs.append(dbuf)
        if group is None or dbuf.cur_group_id != group or not dbuf.last_group:
            if dbuf.last_group:
                o.deps.append(("dmaser", dbuf.last_group[-1]))
            dbuf.last_group = [o]
            dbuf.cur_group_id = group
            o.dgroup = Group()
        else:
            first = dbuf.last_group[0]
            for k, d in first.deps:
                if k == "dmaser":
                    o.deps.append((k, d))
            o.dgroup = first.dgroup
            dbuf.last_group.append(o)
        dbuf.dcount += 16
        o.dgroup.end = dbuf.dcount
        self._track(o, reads, writes)
        self.ops[eng].append(o)
        return o

    @contextlib.contextmanager
    def phase(self, name):
        self.pstack = contextlib.ExitStack()
        with self.pstack:
            yield
            self._emit()
        self.pstack = None
        self.phase_id += 1

    def _semval(self, e, tk):
        tk -= 1
        return self.esem[e][tk // SEM_CHUNK], tk % SEM_CHUNK + 1

    def _emit(self, final=False):
        nc = self.nc
        for e in ENGS:
            for o in self.ops[e]:
                for kind, d in o.deps:
                    if d.is_dma:
                        continue
                    if d.eng == o.eng and (d.eng == "pe" or (kind != "raw" and not SAME_ENG_ALL)):
                        continue
                    d.has_dep = True
            for o in reversed(self.ops[e]):
                if not o.is_dma:
                    o.has_dep = True
                    break
        newbar = []
        for e in ENGS:
            t = self.ticket[e]
            for o in self.ops[e]:
                if o.has_dep and not o.is_dma:
                    t += 1
                    o.ticket = t
            if t != self.ticket[e]:
                newbar.append(self._semval(e, t))
            self.ticket[e] = t
        barrier = self.barrier

        def run(e, eng):
            waited = {}
            for sem, val in barrier:
                eng.wait_ge(sem, val)
                waited[id(sem)] = val
            for o in self.ops[e]:
                need = {}
                for kind, d in o.deps:
                    if d.is_dma:
                        if o.is_dma and o.dgroup is d.dgroup:
                            continue
                        sem, val = d.dbuf.dsem[0], d.dgroup.end
                    else:
                        if d.eng == e and (e == "pe" or (kind != "raw" and not SAME_ENG_ALL)):
                            continue
                        sem, val = self._semval(d.eng, d.ticket)
                    key = id(sem)
                    if val > need.get(key, (None, 0))[1]:
                        need[key] = (sem, val)
                for key, (sem, val) in need.items():
                    if waited.get(key, 0) >= val:
                        continue
                    waited[key] = val
                    eng.wait_ge(sem, val)
                ins = o.fn(eng)
                if o.is_dma:
                    ins.then_inc(o.dbuf.dsem[0], 16)
                elif o.ticket is not None:
                    sem, _ = self._semval(e, o.ticket)
                    ins.then_inc(sem, 1)
            if final and e == "sp":
                for b in self.phase_dma_bufs:
                    eng.wait_ge(b.dsem[0], b.dcount)

        with nc.Block(no_gpsimd_drain=True) as block:
            @block.tensor
            def _(eng):
                run("pe", eng)

            @block.scalar
            def _(eng):
                run("act", eng)

            @block.vector
            def _(eng):
                run("dve", eng)

            @block.gpsimd
            def _(eng):
                run("pool", eng)

            @block.sync
            def _(eng):
                run("sp", eng)
        bar = {id(s): (s, v) for s, v in self.barrier}
        for s, v in newbar:
            bar[id(s)] = (s, v)
        for b in self.phase_dma_bufs:
            b.dsem[1] = b.dcount
            bar[id(b.dsem[0])] = (b.dsem[0], b.dcount)
            self.dpool.append(b.dsem)
        self.barrier = list(bar.values())
        self.total_ops += sum(len(v) for v in self.ops.values())
        self.ops = {e: [] for e in ENGS}
        self.phase_dma_bufs = []


INPUT_SHAPES = {
    "x": [L, D], "mix_pre_g": [2, D], "mix_post_g": [2, D], "mlp_pre_g": [2, D], "mlp_post_g": [2, D],
    "ssm_w_in": [D, D], "ssm_a_re": [128, 64], "ssm_a_im": [128, 64], "ssm_log_dt": [128],
    "ssm_b_re": [128, 64, 16], "ssm_b_im": [128, 64, 16], "ssm_c_re": [128, 16, 64], "ssm_c_im": [128, 16, 64],
    "ssm_d": [D], "ssm_w_glu": [D, 2 * D], "kv_norm_g": [D], "w_kv": [D, 2 * D], "attn_w_q": [D, D],
    "lam_q1": [128], "lam_k1": [128], "lam_q2": [128], "lam_k2": [128], "attn_subln_g": [256],
    "attn_w_o": [D, D], "mlp_w_up": [2, D, DFF], "mlp_w_down": [2, DFF, D],
}
ALL_PHASES = ("conv", "prep", "l0a", "l0b", "l0c", "mlp0", "qkv", "attn", "mlp1")


class Prog:
    def __init__(self, phases=ALL_PHASES, dbg=(), ext_in=()):
        self.phases = phases
        self.dbg = set(dbg)
        self.ext_in = set(ext_in)
        self.nc = bass.Bass("TRN2", target_bir_lowering=False)
        self.s = Sched(self.nc)
        self.inp = {}
        self.scr = {}
        self.outputs = []
        self.dbg_done = False

    def input(self, name):
        if name not in self.inp:
            self.inp[name] = self.nc.dram_tensor(name, INPUT_SHAPES[name], F32, kind="ExternalInput").ap()
        return self.inp[name]

    def scratch(self, name, shape, dtype):
        if name not in self.scr:
            if name in self.ext_in:
                kind = "ExternalInput"
            elif name in self.dbg or name == "out":
                kind = "ExternalOutput"
                self.outputs.append(name)
            else:
                kind = "Internal"
            self.scr[name] = self.nc.dram_tensor(name, list(shape), dtype, kind=kind).ap()
        return self.scr[name]

    def build(self):
        s = self.s
        nc = self.nc
        self.pb = [s.psum_tile([128, 512], F32, f"pb{i}") for i in range(6)]
        self.pt = [s.psum_tile([128, 1024], BF16, f"pt{i}") for i in range(2)]
        self.ident = s.ptile([128, 128], BF16, "ident")
        self.cmask = s.ptile([128, 128], BF16, "cmask")
        self.rcos = s.ptile([128, 16, 64], F32, "rcos")
        self.rsin = s.ptile([128, 16, 64], F32, "rsin")
        self.lam = s.ptile([128, 1], F32, "lam")
        self.gsub = s.ptile([128, 256], F32, "gsub")
        self.consts_ready = False
        ph = self.phases
        if "conv" in ph:
            self.phase_conv()
        if "prep" in ph:
            self.phase_prep()
        if "l0a" in ph:
            self.phase_l0a()
        if "l0b" in ph:
            self.phase_l0b()
        if "l0c" in ph:
            self.phase_l0c()
        if "mlp0" in ph:
            self.phase_mlp(0, "hA", "hB")
        if "qkv" in ph:
            self.phase_qkv()
        if "attn" in ph:
            self.phase_attn()
        if "mlp1" in ph:
            self.phase_mlp(1, "hC", "out")
        with s.phase("final"):
            fin = s.tile([128, 1], F32, "fin")
            s.op("dve", lambda e: e.memset(fin.t[:], 1.0), writes=[fin])
            if self.dbg_done:
                dn = self.nc.dram_tensor("done", [128, 1], F32, kind="ExternalOutput").ap()
                s.dma("sp", lambda e: e.dma_start(out=dn, in_=fin.t[:]), fin, reads=[fin])
        with nc.Block(no_gpsimd_drain=True) as block:
            @block.sync
            def _(eng):
                for sem, val in s.barrier:
                    eng.wait_ge(sem, val)
        s.stack.close()
        return nc

    def wb(self, name, shape):
        return self.scratch(name + "_bf", shape, BF16)

    def norm_stats(self, src, rs, junk):
        s = self.s
        ss = rs
        s.op("act", lambda e: e.activation(out=junk.t[:], in_=src.t[:], func=AF.Square, accum_out=ss.t[:]),
             reads=[src], writes=[junk, ss])
        s.op("act", lambda e: e.activation(out=rs.t[:], in_=ss.t[:], func=AF.Sqrt, scale=1.0 / D, bias=self.eps_t.t[:]),
             reads=[ss, self.eps_t], writes=[rs])
        s.op("dve", lambda e: e.reciprocal(out=rs.t[:], in_=rs.t[:]), reads=[rs], writes=[rs])

    def transpose_to(self, xn, dstT, col0, pti):
        s = self.s
        for half in range(2):
            pt = self.pt[(pti + half) % 2]
            for k in range(8):
                kc = half * 8 + k
                s.op("pe", lambda e, pt=pt, k=k, kc=kc: e.transpose(
                    out=pt.t[:, k * 128:(k + 1) * 128], in_=xn.t[:, kc * 128:(kc + 1) * 128], identity=self.ident.t[:]),
                    reads=[xn, self.ident], writes=[pt])
            eng = "act" if half == 0 else "dve"
            if eng == "act":
                s.op("act", lambda e, pt=pt, half=half: e.copy(
                    out=dstT.t[:, half * 8:(half + 1) * 8, col0:col0 + 128],
                    in_=pt.t[:].rearrange("p (k t) -> p k t", k=8)), reads=[pt], writes=[dstT])
            else:
                s.op("dve", lambda e, pt=pt, half=half: e.tensor_copy(
                    out=dstT.t[:, half * 8:(half + 1) * 8, col0:col0 + 128],
                    in_=pt.t[:].rearrange("p (k t) -> p k t", k=8)), reads=[pt], writes=[dstT])

    def load_gain(self, tile_, src_ap):
        self.s.dma("sp", lambda e: e.dma_start(out=tile_.t[:], in_=src_ap.partition_broadcast(128)), tile_, writes=[tile_])

    def post_norm_residual(self, val, hres, gpost, rs, junk, out_ap):
        s = self.s
        self.norm_stats(val, rs, junk)
        s.op("dve", lambda e: e.scalar_tensor_tensor(out=val.t[:], in0=val.t[:], scalar=rs.t[:], in1=gpost.t[:],
                                                     op0=ALU.mult, op1=ALU.mult), reads=[val, rs, gpost], writes=[val])
        s.op("pool", lambda e: e.tensor_tensor(out=val.t[:], in0=val.t[:], in1=hres.t[:], op=ALU.add),
             reads=[val, hres], writes=[val])
        s.dma("sp", lambda e: e.dma_start(out=out_ap, in_=val.t[:]), val, reads=[val])

    def mk_eps(self):
        s = self.s
        self.eps_t = s.tile([128, 1], F32, "eps")
        s.op("pool", lambda e: e.memset(self.eps_t.t[:], EPS), writes=[self.eps_t])

    CONV_SPECS = {"ssm_w_in": ("ssm_w_in", None, D, D), "ssm_w_glu": ("ssm_w_glu", None, D, 2 * D),
                  "mlp_w_up0": ("mlp_w_up", 0, D, DFF), "mlp_w_down0": ("mlp_w_down", 0, DFF, D),
                  "w_kv": ("w_kv", None, D, 2 * D), "attn_w_q": ("attn_w_q", None, D, D), "attn_w_o": ("attn_w_o", None, D, D),
                  "mlp_w_up1": ("mlp_w_up", 1, D, DFF), "mlp_w_down1": ("mlp_w_down", 1, DFF, D)}
    CONV_NEED = {"l0a": ["ssm_w_in"], "l0c": ["ssm_w_glu"], "mlp0": ["mlp_w_up0", "mlp_w_down0"],
                 "qkv": ["w_kv", "attn_w_q"], "attn": ["attn_w_o"], "mlp1": ["mlp_w_up1", "mlp_w_down1"]}
    CONV_PLAN = {"ssm_w_in": "conv", "ssm_w_glu": "l0a", "mlp_w_up0": "l0b", "mlp_w_down0": "l0b", "w_kv": "l0b",
                 "attn_w_q": "l0b", "attn_w_o": "l0b", "mlp_w_up1": "mlp0", "mlp_w_down1": "mlp0"}

    def issue_conv(self, phase):
        s = self.s
        import os
        if phase != "conv":
            return
        wanted = set(sum([self.CONV_NEED.get(p, []) for p in self.phases], []))
        if os.environ.get("CONV_ALL"):
            wanted = set(self.CONV_SPECS)
        self.conv_bufs = {}
        self.conv_deferred = []
        for wname, (name, idx, R, C) in self.CONV_SPECS.items():
            if wname not in wanted:
                continue
            src = self.input(name)
            if idx is not None:
                src = src[idx]
            dst = self.wb(wname, [R, C])
            rows = max(128, (2 * 1024 * 1024) // C)
            b = s.dbuf(f"cv_{wname}")
            self.conv_bufs[wname] = b
            defer = ("l0b" in self.phases) and wname != "ssm_w_in" and not os.environ.get("CONV_ALL")
            for r0 in range(0, R, rows):
                def issue(r0=r0, src=src, dst=dst, rows=rows, b=b):
                    s.dma("pool", lambda e: e.dma_start(out=dst[r0:r0 + rows, :], in_=src[r0:r0 + rows, :]),
                          b, group="cv", background=True)
                if defer:
                    self.conv_deferred.append(issue)
                else:
                    issue()

    def conv_wait(self, phase):
        for wname in self.CONV_NEED.get(phase, []):
            b = getattr(self, "conv_bufs", {}).get(wname)
            if b is not None:
                self.s.barrier.append((b.dsem[0], b.dcount))

    def phase_conv(self):
        s = self.s
        with s.phase("conv"):
            self.issue_conv("conv")

    def phase_prep(self):
        s = self.s
        P = self
        with s.phase("prep"):
            ones = s.tile([128, 128], F32, "ones")
            s.op("pool", lambda e: e.memset(ones.t[:], 1.0), writes=[ones])
            s.op("pool", lambda e: e.affine_select(out=P.ident.t[:], in_=ones.t[:], pattern=[[-1, 128]],
                                                   compare_op=ALU.is_equal, fill=0.0, base=0, channel_multiplier=1),
                 reads=[ones], writes=[P.ident])
            s.op("pool", lambda e: e.affine_select(out=P.cmask.t[:], in_=ones.t[:], pattern=[[1, 128]],
                                                   compare_op=ALU.is_ge, fill=0.0, base=0, channel_multiplier=-1),
                 reads=[ones], writes=[P.cmask])
            import os
            parts = os.environ.get("PREP_PARTS", "rope,ssm,lambda").split(",")
            if "rope" in parts:
                self.prep_rope()
            if "ssm" in parts:
                self.prep_ssm()
            if "lambda" in parts:
                self.prep_lambda()
        self.consts_ready = True

    def sincos(self, ang, shape, sin_out, cos_out, tag):
        s = self.s
        n = len(shape)
        ki = s.tile(shape, I32, "ki" + tag)
        kf = s.tile(shape, F32, "kf" + tag)
        m2 = s.tile(shape, F32, "m2" + tag)
        c1 = 6.28125
        c2 = TWO_PI - c1
        lim = 3.1415925
        s.op("dve", lambda e: e.tensor_scalar(out=ki.t[:], in0=ang.t[:], scalar1=1.0 / TWO_PI, scalar2=None, op0=ALU.mult),
             reads=[ang], writes=[ki])
        s.op("dve", lambda e: e.tensor_copy(out=kf.t[:], in_=ki.t[:]), reads=[ki], writes=[kf])
        s.op("dve", lambda e: e.scalar_tensor_tensor(out=ang.t[:], in0=kf.t[:], scalar=-c1, in1=ang.t[:], op0=ALU.mult, op1=ALU.add),
             reads=[kf, ang], writes=[ang])
        s.op("dve", lambda e: e.scalar_tensor_tensor(out=ang.t[:], in0=kf.t[:], scalar=-c2, in1=ang.t[:], op0=ALU.mult, op1=ALU.add),
             reads=[kf, ang], writes=[ang])
        s.op("dve", lambda e: e.tensor_scalar(out=m2.t[:], in0=ang.t[:], scalar1=math.pi / 2, scalar2=-TWO_PI, op0=ALU.is_gt, op1=ALU.mult),
             reads=[ang], writes=[m2])
        s.op("dve", lambda e: e.scalar_tensor_tensor(out=m2.t[:], in0=ang.t[:], scalar=math.pi / 2, in1=m2.t[:], op0=ALU.add, op1=ALU.add),
             reads=[ang, m2], writes=[m2])
        s.op("dve", lambda e: e.tensor_scalar(out=m2.t[:], in0=m2.t[:], scalar1=-lim, scalar2=lim, op0=ALU.max, op1=ALU.min),
             reads=[m2], writes=[m2])
        s.op("dve", lambda e: e.tensor_scalar(out=ang.t[:], in0=ang.t[:], scalar1=-lim, scalar2=lim, op0=ALU.max, op1=ALU.min),
             reads=[ang], writes=[ang])
        s.op("act", lambda e: e.activation(out=sin_out.t[:], in_=ang.t[:], func=AF.Sin), reads=[ang], writes=[sin_out])
        s.op("act", lambda e: e.activation(out=cos_out.t[:], in_=m2.t[:], func=AF.Sin), reads=[m2], writes=[cos_out])

    def prep_rope(self):
        s = self.s
        P = self
        posf = s.tile([128, 16], F32, "posf")
        fidx = s.tile([128, 64], F32, "fidx")
        invf = s.tile([128, 64], F32, "invf")
        ang = s.tile([128, 16, 64], F32, "rang")
        s.op("pool", lambda e: e.iota(posf.t[:], pattern=[[128, 16]], base=0, channel_multiplier=1,
                                      allow_small_or_imprecise_dtypes=True), writes=[posf])
        s.op("pool", lambda e: e.iota(fidx.t[:], pattern=[[1, 64]], base=0, channel_multiplier=0,
                                      allow_small_or_imprecise_dtypes=True), writes=[fidx])
        s.op("act", lambda e: e.activation(out=invf.t[:], in_=fidx.t[:], func=AF.Exp, scale=-math.log(10000.0) / 64.0),
             reads=[fidx], writes=[invf])
        s.op("dve", lambda e: e.tensor_tensor(out=ang.t[:], in0=posf.t[:].unsqueeze(2).to_broadcast([128, 16, 64]),
                                              in1=invf.t[:].unsqueeze(1).to_broadcast([128, 16, 64]), op=ALU.mult),
             reads=[posf, invf], writes=[ang])
        self.sincos(ang, [128, 16, 64], P.rsin, P.rcos, "r")

    def abar(self, are, aim, ldt_b, shape, tag):
        s = self.s
        step = s.tile(shape, F32, "step" + tag)
        sr = s.tile(shape, F32, "sr" + tag)
        si = s.tile(shape, F32, "si" + tag)
        mag = s.tile(shape, F32, "mag" + tag)
        sn = s.tile(shape, F32, "sn" + tag)
        cs = s.tile(shape, F32, "cs" + tag)
        ar = s.tile(shape, F32, "ar" + tag)
        ai = s.tile(shape, F32, "ai" + tag)
        ldt_tile, ldt_ap = ldt_b
        s.op("act", lambda e: e.activation(out=step.t[:], in_=ldt_ap(), func=AF.Exp), reads=[ldt_tile], writes=[step])
        s.op("dve", lambda e: e.tensor_scalar(out=are.t[:], in0=are.t[:], scalar1=-1e-4, scalar2=None, op0=ALU.min),
             reads=[are], writes=[are])
        s.op("dve", lambda e: e.tensor_tensor(out=sr.t[:], in0=step.t[:], in1=are.t[:], op=ALU.mult), reads=[step, are], writes=[sr])
        s.op("dve", lambda e: e.tensor_tensor(out=si.t[:], in0=step.t[:], in1=aim.t[:], op=ALU.mult), reads=[step, aim], writes=[si])
        s.op("act", lambda e: e.activation(out=mag.t[:], in_=sr.t[:], func=AF.Exp), reads=[sr], writes=[mag])
        self.sincos(si, shape, sn, cs, tag)
        s.op("dve", lambda e: e.tensor_tensor(out=ar.t[:], in0=mag.t[:], in1=cs.t[:], op=ALU.mult), reads=[mag, cs], writes=[ar])
        s.op("dve", lambda e: e.tensor_tensor(out=ai.t[:], in0=mag.t[:], in1=sn.t[:], op=ALU.mult), reads=[mag, sn], writes=[ai])
        return ar, ai

    def prep_ssm(self):
        s = self.s
        P = self
        a_re = self.input("ssm_a_re")
        a_im = self.input("ssm_a_im")
        ldt = self.input("ssm_log_dt")
        b_re = self.input("ssm_b_re")
        b_im = self.input("ssm_b_im")
        c_re = self.input("ssm_c_re")
        c_im = self.input("ssm_c_im")
        dsk = self.input("ssm_d")
        onesf = s.tile([128, 128], F32, "onesf")
        identf = s.tile([128, 128], F32, "identf")
        s.op("pool", lambda e: e.memset(onesf.t[:], 1.0), writes=[onesf])
        s.op("pool", lambda e: e.affine_select(out=identf.t[:], in_=onesf.t[:], pattern=[[-1, 128]],
                                               compare_op=ALU.is_equal, fill=0.0, base=0, channel_multiplier=1),
             reads=[onesf], writes=[identf])
        sel = s.tile([128, 64], F32, "sel")
        s.op("pool", lambda e: e.affine_select(out=sel.t[:], in_=onesf.t[:, 0:64], pattern=[[-2, 64]],
                                               compare_op=ALU.is_ge, fill=0.0, base=0, channel_multiplier=1),
             reads=[onesf], writes=[sel])
        s.op("pool", lambda e: e.affine_select(out=sel.t[:], in_=sel.t[:], pattern=[[2, 64]],
                                               compare_op=ALU.is_ge, fill=0.0, base=1, channel_multiplier=-1),
             reads=[sel], writes=[sel])
        pidx = s.tile([128, 1], I32, "pidx")
        pi2 = s.tile([128, 1], I32, "pi2")
        par1 = s.tile([128, 1], F32, "par1")
        par0 = s.tile([128, 1], F32, "par0")
        s.op("pool", lambda e: e.iota(pidx.t[:], pattern=[[0, 1]], base=0, channel_multiplier=1), writes=[pidx])
        s.op("dve", lambda e: e.tensor_scalar(out=pi2.t[:], in0=pidx.t[:], scalar1=1, scalar2=None, op0=ALU.bitwise_and),
             reads=[pidx], writes=[pi2])
        s.op("dve", lambda e: e.tensor_copy(out=par1.t[:], in_=pi2.t[:]), reads=[pi2], writes=[par1])
        s.op("dve", lambda e: e.tensor_scalar(out=par0.t[:], in0=par1.t[:], scalar1=-1.0, scalar2=1.0, op0=ALU.mult, op1=ALU.add),
             reads=[par1], writes=[par0])
        areS = s.tile([128, 64], F32, "areS")
        aimS = s.tile([128, 64], F32, "aimS")
        ldtS = s.tile([128, 64], F32, "ldtS")
        ldtc = s.tile([128, 1], F32, "ldtc")
        s.dma("sp", lambda e: e.dma_start(out=ldtc.t[:], in_=ldt.rearrange("(g o) -> g o", o=1)), ldtc, writes=[ldtc])
        for k, (dstS, srcD) in enumerate(((areS, a_re), (aimS, a_im), (ldtS, None))):
            ext = s.tile([128, 2, 64], F32, "aext")
            if srcD is not None:
                nat = s.tile([128, 64], F32, "anat")
                s.dma("sp", lambda e, nat=nat, srcD=srcD: e.dma_start(out=nat.t[:], in_=srcD), nat, writes=[nat])
                for two, par in enumerate((par0, par1)):
                    s.op("dve", lambda e, ext=ext, nat=nat, two=two, par=par: e.tensor_scalar(
                        out=ext.t[:, two, :], in0=nat.t[:], scalar1=par.t[:], scalar2=None, op0=ALU.mult),
                        reads=[nat, par], writes=[ext])
            else:
                for two, par in enumerate((par0, par1)):
                    s.op("dve", lambda e, ext=ext, two=two, par=par: e.tensor_scalar(
                        out=ext.t[:, two, :], in0=onesf.t[:, 0:64], scalar1=ldtc.t[:], scalar2=par.t[:], op0=ALU.mult, op1=ALU.mult),
                        reads=[onesf, ldtc, par], writes=[ext])
            bank = self.pb[k % 4]
            s.op("pe", lambda e, bank=bank, ext=ext: e.matmul(bank.t[:, 0:64], lhsT=ext.t[:].rearrange("p a n -> p (a n)"),
                                                              rhs=sel.t[:], start=True, stop=True), reads=[ext, sel], writes=[bank])
            s.op("act", lambda e, bank=bank, dstS=dstS: e.copy(out=dstS.t[:], in_=bank.t[:, 0:64]), reads=[bank], writes=[dstS])
        arS, aiS = self.abar(areS, aimS, (ldtS, lambda: ldtS.t[:]), [128, 64], "S")
        P.APW = s.tile([128, 8, 2, 2, 64], F32, "APW")
        APW = P.APW
        pt1 = s.tile([128, 64], F32, "pw1")
        pt2 = s.tile([128, 64], F32, "pw2")
        s.op("dve", lambda e: e.tensor_copy(out=APW.t[:, 0, 0, 0, :], in_=arS.t[:]), reads=[arS], writes=[APW])
        s.op("dve", lambda e: e.tensor_copy(out=APW.t[:, 0, 0, 1, :], in_=aiS.t[:]), reads=[aiS], writes=[APW])
        for m in range(1, 8):
            s.op("dve", lambda e, m=m: e.tensor_tensor(out=pt1.t[:], in0=APW.t[:, m - 1, 0, 0, :], in1=arS.t[:], op=ALU.mult),
                 reads=[APW, arS], writes=[pt1])
            s.op("dve", lambda e, m=m: e.tensor_tensor(out=pt2.t[:], in0=APW.t[:, m - 1, 0, 1, :], in1=aiS.t[:], op=ALU.mult),
                 reads=[APW, aiS], writes=[pt2])
            s.op("dve", lambda e, m=m: e.tensor_tensor(out=APW.t[:, m, 0, 0, :], in0=pt1.t[:], in1=pt2.t[:], op=ALU.subtract),
                 reads=[pt1, pt2], writes=[APW])
            s.op("dve", lambda e, m=m: e.tensor_tensor(out=pt1.t[:], in0=APW.t[:, m - 1, 0, 0, :], in1=aiS.t[:], op=ALU.mult),
                 reads=[APW, aiS], writes=[pt1])
            s.op("dve", lambda e, m=m: e.tensor_tensor(out=pt2.t[:], in0=APW.t[:, m - 1, 0, 1, :], in1=arS.t[:], op=ALU.mult),
                 reads=[APW, arS], writes=[pt2])
            s.op("dve", lambda e, m=m: e.tensor_tensor(out=APW.t[:, m, 0, 1, :], in0=pt1.t[:], in1=pt2.t[:], op=ALU.add),
                 reads=[pt1, pt2], writes=[APW])
        s.op("dve", lambda e: e.tensor_scalar(out=APW.t[:, :, 1, 0, :], in0=APW.t[:, :, 0, 1, :], scalar1=-1.0, scalar2=None, op0=ALU.mult),
             reads=[APW], writes=[APW])
        s.op("dve", lambda e: e.tensor_copy(out=APW.t[:, :, 1, 1, :], in_=APW.t[:, :, 0, 0, :]), reads=[APW], writes=[APW])
        if self._stage() < 1:
            return
        shS = [128, 64]
        den = s.tile(shS, F32, "den")
        t1 = s.tile(shS, F32, "t1")
        cre = s.tile(shS, F32, "cre")
        cim = s.tile(shS, F32, "cim")
        nr = s.tile(shS, F32, "nr")
        TT = lambda out, a, b, op: s.op("dve", lambda e: e.tensor_tensor(out=out.t[:], in0=a.t[:], in1=b.t[:], op=op),
                                        reads=[a, b], writes=[out])
        TT(den, areS, areS, ALU.mult)
        TT(t1, aimS, aimS, ALU.mult)
        TT(den, den, t1, ALU.add)
        s.op("dve", lambda e: e.reciprocal(out=den.t[:], in_=den.t[:]), reads=[den], writes=[den])
        s.op("dve", lambda e: e.tensor_scalar(out=nr.t[:], in0=arS.t[:], scalar1=-1.0, scalar2=None, op0=ALU.add),
             reads=[arS], writes=[nr])
        TT(cre, nr, areS, ALU.mult)
        TT(t1, aiS, aimS, ALU.mult)
        TT(cre, cre, t1, ALU.add)
        TT(cre, cre, den, ALU.mult)
        TT(cim, aiS, areS, ALU.mult)
        TT(t1, nr, aimS, ALU.mult)
        TT(cim, cim, t1, ALU.subtract)
        TT(cim, cim, den, ALU.mult)
        if self._stage() < 2:
            return
        shB = [128, 64, 16]
        bS = [s.tile(shB, F32, "bSre"), s.tile(shB, F32, "bSim")]
        for bt, bsrc in zip(bS, (b_re, b_im)):
            b2 = bsrc.rearrange("(j two) n q -> two n j q", two=2)
            for two in range(2):
                for jq in range(4):
                    s.dma("sp", lambda e, bt=bt, b2=b2, two=two, jq=jq: e.dma_start(
                        out=bt.t[two * 64:(two + 1) * 64, jq * 16:(jq + 1) * 16, :], in_=b2[two][:, jq * 16:(jq + 1) * 16, :]),
                        bt, writes=[bt], group="b")
        bb = [s.tile(shB, F32, "bbre"), s.tile(shB, F32, "bbim")]
        tb = s.tile(shB, F32, "tb")
        bc = lambda t_: t_.t[:].unsqueeze(2).to_broadcast(shB)
        TB = lambda out, co, bsrc, op=ALU.mult: s.op("dve", lambda e: e.tensor_tensor(out=out.t[:], in0=bsrc.t[:], in1=bc(co), op=op),
                                                     reads=[bsrc, co], writes=[out])
        TB(bb[0], cre, bS[0])
        TB(tb, cim, bS[1])
        TT(bb[0], bb[0], tb, ALU.subtract)
        TB(bb[1], cre, bS[1])
        TB(tb, cim, bS[0])
        TT(bb[1], bb[1], tb, ALU.add)
        if self._stage() < 3:
            return
        P.BW = [s.tile([128, 16, 128], F32, "BWre"), s.tile([128, 16, 128], F32, "BWim")]
        tcnt = 0
        for bw, bsrc in zip(P.BW, bb):
            bext = s.tile([128, 64, 2, 16], F32, "bext")
            s.op("pool", lambda e, bext=bext: e.memset(bext.t[:], 0.0), writes=[bext])
            s.op("dve", lambda e, bext=bext, bsrc=bsrc: e.tensor_copy(out=bext.t[0:64, :, 0, :], in_=bsrc.t[0:64, :, :]),
                 reads=[bsrc], writes=[bext])
            s.op("dve", lambda e, bext=bext, bsrc=bsrc: e.tensor_copy(out=bext.t[64:128, :, 1, :], in_=bsrc.t[64:128, :, :]),
                 reads=[bsrc], writes=[bext])
            for c in range(16):
                bank = self.pb[tcnt % 4]
                tcnt += 1
                s.op("pe", lambda e, bank=bank, bext=bext, c=c: e.transpose(
                    out=bank.t[:, 0:128], in_=bext.t[:, 4 * c:4 * c + 4, :, :].rearrange("p a b q -> p (a b q)"),
                    identity=identf.t[:]), reads=[bext, identf], writes=[bank])
                s.op("act", lambda e, bank=bank, bw=bw, c=c: e.copy(out=bw.t[:, c, :], in_=bank.t[:, 0:128]),
                     reads=[bank], writes=[bw])
        if self._stage() < 4:
            return
        pi = s.tile([128, 1], I32, "pidx4")
        m1 = s.tile([128, 1], F32, "m1")
        m0 = s.tile([128, 1], F32, "m0")
        s.op("dve", lambda e: e.tensor_scalar(out=pi.t[:], in0=pidx.t[:], scalar1=4, scalar2=1, op0=ALU.arith_shift_right,
                                              op1=ALU.bitwise_and), reads=[pidx], writes=[pi])
        s.op("dve", lambda e: e.tensor_copy(out=m1.t[:], in_=pi.t[:]), reads=[pi], writes=[m1])
        s.op("dve", lambda e: e.tensor_scalar(out=m0.t[:], in0=m1.t[:], scalar1=-1.0, scalar2=1.0, op0=ALU.mult, op1=ALU.add),
             reads=[m1], writes=[m0])
        P.CW = [s.tile([128, 64, 32], F32, "CWre"), s.tile([128, 64, 32], F32, "CWim")]
        for cw, csrc, sgn in ((P.CW[0], c_re, 1.0), (P.CW[1], c_im, -1.0)):
            cx = s.tile([128, 16, 64], F32, "cx")
            cext = s.tile([128, 16, 2, 64], F32, "cext")
            cv = csrc.rearrange("(c r) p n -> (r p) c n", r=8)
            for cq in range(4):
                s.dma("sp", lambda e, cx=cx, cv=cv, cq=cq: e.dma_start(out=cx.t[:, cq * 4:(cq + 1) * 4, :], in_=cv[:, cq * 4:(cq + 1) * 4, :]),
                      cx, writes=[cx], group="c")
            s.op("dve", lambda e, cx=cx, cext=cext, sgn=sgn: e.tensor_scalar(out=cext.t[:, :, 0, :], in0=cx.t[:], scalar1=m0.t[:],
                                                                           scalar2=sgn, op0=ALU.mult, op1=ALU.mult),
                 reads=[cx, m0], writes=[cext])
            s.op("dve", lambda e, cx=cx, cext=cext, sgn=sgn: e.tensor_scalar(out=cext.t[:, :, 1, :], in0=cx.t[:], scalar1=m1.t[:],
                                                                           scalar2=sgn, op0=ALU.mult, op1=ALU.mult),
                 reads=[cx, m1], writes=[cext])
            for c in range(16):
                bank = self.pb[tcnt % 4]
                tcnt += 1
                s.op("pe", lambda e, bank=bank, cext=cext, c=c: e.transpose(
                    out=bank.t[:, 0:128], in_=cext.t[:, c, :, :].rearrange("p a n -> p (a n)"),
                    identity=identf.t[:]), reads=[cext, identf], writes=[bank])
                s.op("act", lambda e, bank=bank, cw=cw, c=c: e.copy(
                    out=cw.t[:, 4 * c:4 * c + 4, :], in_=bank.t[:, 0:128].rearrange("p (a m) -> p a m", a=4)),
                    reads=[bank], writes=[cw])
        if self._stage() < 5:
            return
        P.Dcol = s.tile([128, 16], F32, "Dcol")
        dnat = s.tile([16, 128], F32, "dnat")
        s.dma("sp", lambda e: e.dma_start(out=dnat.t[:], in_=dsk.rearrange("(c p) -> c p", p=128)), dnat, writes=[dnat])
        bank = self.pb[tcnt % 4]
        s.op("pe", lambda e: e.transpose(out=bank.t[:, 0:16], in_=dnat.t[:], identity=identf.t[0:16, 0:16]),
             reads=[dnat, identf], writes=[bank])
        s.op("act", lambda e: e.copy(out=P.Dcol.t[:], in_=bank.t[:, 0:16]), reads=[bank], writes=[P.Dcol])
        if self._stage() < 6:
            return
        for nm, tl in self.ssm_const_list():
            dst = self.scratch(nm, [128, int(np.prod(list(tl.t.shape)[1:]))], F32)
            s.dma("sp", lambda e, dst=dst, tl=tl: e.dma_start(out=dst, in_=self.flat2(tl)), tl, reads=[tl])

    def _stage(self):
        import os
        return int(os.environ.get("SSM_STAGE", "99"))

    @staticmethod
    def flat2(tl):
        n = len(list(tl.t.shape))
        if n == 2:
            return tl.t[:]
        if n == 3:
            return tl.t[:].rearrange("p a b -> p (a b)")
        if n == 5:
            return tl.t[:].rearrange("p a b c d -> p (a b c d)")
        return tl.t[:].rearrange("p a b c -> p (a b c)")

    def ssm_const_list(self):
        P = self
        return [("c_APW", P.APW), ("c_BWre", P.BW[0]), ("c_BWim", P.BW[1]),
                ("c_CWre", P.CW[0]), ("c_CWim", P.CW[1]), ("c_D", P.Dcol)]

    def load_ssm_consts(self):
        s = self.s
        P = self
        P.APW = s.tile([128, 8, 2, 2, 64], F32, "APW")
        P.BW = [s.tile([128, 16, 128], F32, "BWre"), s.tile([128, 16, 128], F32, "BWim")]
        P.CW = [s.tile([128, 64, 32], F32, "CWre"), s.tile([128, 64, 32], F32, "CWim")]
        P.Dcol = s.tile([128, 16], F32, "Dcol")
        for nm, tl in self.ssm_const_list():
            src_ = self.scratch(nm, [128, int(np.prod(list(tl.t.shape)[1:]))], F32)
            s.dma("sp", lambda e, src_=src_, tl=tl: e.dma_start(out=self.flat2(tl), in_=src_), tl, writes=[tl])

    def prep_lambda(self):
        s = self.s
        P = self
        lam_init = 0.8 - 0.6 * math.exp(-0.3 * 1)
        P.lam_init = lam_init
        tl = [s.tile([128, 128], F32, f"lam{i}") for i in range(4)]
        for t_, nm in zip(tl, ("lam_q1", "lam_k1", "lam_q2", "lam_k2")):
            self.load_gain(t_, self.input(nm))
        d1 = s.tile([128, 1], F32, "d1")
        d2 = s.tile([128, 1], F32, "d2")
        jk = s.tile([128, 128], F32, "ljk")
        s.op("dve", lambda e: e.tensor_tensor(out=jk.t[:], in0=tl[0].t[:], in1=tl[1].t[:], op=ALU.mult), reads=[tl[0], tl[1]], writes=[jk])
        s.op("dve", lambda e: e.reduce_sum(out=d1.t[:], in_=jk.t[:], axis=mybir.AxisListType.X), reads=[jk], writes=[d1])
        s.op("dve", lambda e: e.tensor_tensor(out=jk.t[:], in0=tl[2].t[:], in1=tl[3].t[:], op=ALU.mult), reads=[tl[2], tl[3]], writes=[jk])
        s.op("dve", lambda e: e.reduce_sum(out=d2.t[:], in_=jk.t[:], axis=mybir.AxisListType.X), reads=[jk], writes=[d2])
        s.op("act", lambda e: e.activation(out=d1.t[:], in_=d1.t[:], func=AF.Exp), reads=[d1], writes=[d1])
        s.op("act", lambda e: e.activation(out=d2.t[:], in_=d2.t[:], func=AF.Exp), reads=[d2], writes=[d2])
        s.op("dve", lambda e: e.scalar_tensor_tensor(out=P.lam.t[:], in0=d1.t[:], scalar=lam_init, in1=d2.t[:], op0=ALU.add,
                                                     op1=ALU.subtract), reads=[d1, d2], writes=[P.lam])
        self.load_gain(P.gsub, self.input("attn_subln_g"))
        s.op("dve", lambda e: e.tensor_scalar(out=P.gsub.t[:], in0=P.gsub.t[:], scalar1=1.0 - lam_init, scalar2=None, op0=ALU.mult),
             reads=[P.gsub], writes=[P.gsub])

    def phase_l0a(self):
        s = self.s
        x = self.input("x")
        win = self.wb("ssm_w_in", [D, D]).rearrange("(kc p) n -> p kc n", p=128)
        UT = self.scratch("UT", [16, 128, L], F32)
        with s.phase("l0a"):
            self.conv_wait("l0a")
            self.mk_eps()
            gpre = s.tile([128, D], F32, "gpre")
            self.load_gain(gpre, self.input("mix_pre_g")[0])
            hb = [s.tile([128, D], F32, "hb") for _ in range(2)]
            junk = s.tile([128, D], BF16, "junk")
            xn = [s.tile([128, D], BF16, "xn") for _ in range(2)]
            rs = [s.tile([128, 1], F32, "rs") for _ in range(2)]
            xnT = s.tile([128, 16, 512], BF16, "xnT")
            wt = [s.tile([128, 16, 512], BF16, "win") for _ in range(4)]
            for cg in range(4):
                s.dma("sp", lambda e, cg=cg: e.dma_start(out=wt[cg].t[:], in_=win[:, :, cg * 512:(cg + 1) * 512]), wt[cg], writes=[wt[cg]])
            ut = [s.tile([128, 16, 512], F32, "ut") for _ in range(1)]
            cnt = 0
            wc = 0
            for blk in range(4):
                for ti in range(4):
                    tok = blk * 512 + ti * 128
                    h = hb[cnt % 2]
                    s.dma("sp", lambda e, h=h, tok=tok: e.dma_start(out=h.t[:], in_=x[tok:tok + 128, :]), h, writes=[h])
                    r = rs[cnt % 2]
                    xx = xn[cnt % 2]
                    self.norm_stats(h, r, junk)
                    s.op("dve", lambda e, xx=xx, h=h, r=r: e.scalar_tensor_tensor(
                        out=xx.t[:], in0=h.t[:], scalar=r.t[:], in1=gpre.t[:], op0=ALU.mult, op1=ALU.mult),
                        reads=[h, r, gpre], writes=[xx])
                    self.transpose_to(xx, xnT, ti * 128, 0)
                    cnt += 1
                u = ut[0]
                for cg in range(4):
                    w = wt[cg]
                    for j in range(4):
                        c = cg * 4 + j
                        bank = self.pb[c % 2]
                        for kc in range(16):
                            s.op("pe", lambda e, bank=bank, w=w, kc=kc, j=j: e.matmul(
                                bank.t[:], lhsT=w.t[:, kc, j * 128:(j + 1) * 128], rhs=xnT.t[:, kc, :],
                                start=(kc == 0), stop=(kc == 15)), reads=[w, xnT], writes=[bank])
                        s.op("act", lambda e, bank=bank, c=c: e.copy(out=u.t[:, c, :], in_=bank.t[:]), reads=[bank], writes=[u])
                s.dma("sp", lambda e, blk=blk: e.dma_start(out=UT[:, :, blk * 512:(blk + 1) * 512].rearrange("c p t -> p c t"),
                                                         in_=u.t[:]), u, reads=[u])

    def phase_l0b(self):
        s = self.s
        P = self
        UT = self.scratch("UT", [16, 128, L], F32)
        ZT = self.scratch("ZT", [16, 128, L], BF16)
        TS = 64
        R = 8
        NB = TS // R
        UB = 128
        with s.phase("l0b"):
            self.load_ssm_consts()
            ut = [s.tile([128, 16, UB], F32, "utb") for _ in range(2)]
            xb = [s.tile([128, TS, 2, 64], F32, "xb") for _ in range(2)]
            xr = [[s.dbuf(f"xr{i}_{r}") for r in range(R)] for i in range(2)]
            zt = [s.tile([128, 16, 512], BF16, "ztb") for _ in range(2)]
            zero = s.tile([128, 2, 64], F32, "zero")
            s.op("pool", lambda e: e.memset(zero.t[:], 0.0), writes=[zero])
            T1 = s.tile([128, NB, 2, 64], F32, "T1")
            T2 = s.tile([128, NB, 2, 64], F32, "T2")
            F1 = s.tile([128, NB, 2, 64], F32, "F1")
            F2 = s.tile([128, NB, 2, 64], F32, "F2")
            CAR = s.tile([128, NB, 2, 64], F32, "CAR")
            c1 = s.tile([128, 2, 64], F32, "c1")
            c2 = s.tile([128, 2, 64], F32, "c2")
            last = [s.tile([128, 2, 64], F32, "last") for _ in range(2)]
            tmp = [s.tile([128, 8, TS], F32, "gtmp") for _ in range(2)]
            yv = [s.tile([128, 8, TS], F32, "gyv") for _ in range(2)]
            sq = [s.tile([128, 8, TS], F32, "gsq") for _ in range(2)]
            sg = [s.tile([128, 8, TS], F32, "gsg") for _ in range(2)]
            APW, BW, CW, Dcol = P.APW, P.BW, P.CW, P.Dcol
            sh4 = [128, NB, 2, 64]
            ybank = [self.pb[4], self.pb[5]]
            gi = 0

            def emit_bu(tc):
                ub = (tc * TS) // UB
                t0 = (tc * TS) % UB
                u = ut[ub % 2]
                if t0 == 0:
                    s.dma("sp", lambda e, u=u, ub=ub: e.dma_start(
                        out=u.t[:], in_=UT[:, :, ub * UB:(ub + 1) * UB].rearrange("c p t -> p c t")), u, writes=[u])
                X = xb[tc % 2]
                XR = xr[tc % 2]
                for cgrp in range(4):
                    for cp in range(4):
                        c = cgrp * 4 + cp
                        for reim in range(2):
                            for r in range(4):
                                bank = self.pb[r]
                                o0 = (cp * 2 + reim) * TS
                                s.op("pe", lambda e, bank=bank, o0=o0, r=r, c=c, reim=reim, u=u, t0=t0: e.matmul(
                                    bank.t[:, o0:o0 + TS], lhsT=BW[reim].t[32 * r:32 * r + 32, c, :],
                                    rhs=u.t[32 * r:32 * r + 32, c, t0:t0 + TS], start=True, stop=True,
                                    tile_position=(32 * r, 0)), reads=[BW[reim], u], writes=[bank])
                    for r in range(4):
                        bank = self.pb[r]
                        j0 = 16 * cgrp + r
                        s.op("act", lambda e, bank=bank, j0=j0, X=X: e.copy(
                            out=X.t[:, :, :, j0:j0 + 13:4].rearrange("p t r c -> p c r t"),
                            in_=bank.t[:].rearrange("p (c r t) -> p c r t", c=4, r=2)), reads=[bank], writes=XR)

            def emit_rest(tc):
                nonlocal gi
                blk = tc // 8
                ub = (tc * TS) // UB
                t0 = (tc * TS) % UB
                z0 = (tc % 8) * TS
                u = ut[ub % 2]
                z = zt[blk % 2]
                X = xb[tc % 2]
                XR = xr[tc % 2]
                Xv = X.t[:].rearrange("p (k r) c j -> p k r c j", r=R)
                A1A = APW.t[:, 0, 0, :, :].unsqueeze(1).to_broadcast(sh4)
                A1B = APW.t[:, 0, 1, :, :].unsqueeze(1).to_broadcast(sh4)
                for r in range(1, R):
                    pre = Xv[:, :, r - 1, 0, :].unsqueeze(2).to_broadcast(sh4)
                    pim = Xv[:, :, r - 1, 1, :].unsqueeze(2).to_broadcast(sh4)
                    s.op("dve", lambda e, pre=pre: e.tensor_tensor(out=T1.t[:], in0=A1A, in1=pre, op=ALU.mult),
                         reads=[APW, XR[r - 1]], writes=[T1])
                    s.op("dve", lambda e, pim=pim: e.tensor_tensor(out=T2.t[:], in0=A1B, in1=pim, op=ALU.mult),
                         reads=[APW, XR[r - 1]], writes=[T2])
                    s.op("dve", lambda e: e.tensor_tensor(out=T1.t[:], in0=T1.t[:], in1=T2.t[:], op=ALU.add),
                         reads=[T1, T2], writes=[T1])
                    s.op("dve", lambda e, r=r, Xv=Xv: e.tensor_tensor(out=Xv[:, :, r, :, :], in0=Xv[:, :, r, :, :], in1=T1.t[:], op=ALU.add),
                         reads=[XR[r], T1], writes=[XR[r]])
                if tc == 0:
                    prev_b, prev_re, prev_im, prev_full = zero, zero.t[:, 0, :], zero.t[:, 1, :], zero.t[:]
                else:
                    Lp = last[(tc - 1) % 2]
                    prev_b = Lp
                    prev_re, prev_im, prev_full = Lp.t[:, 0, :], Lp.t[:, 1, :], Lp.t[:]
                carry_src = (prev_b, prev_full)
                for k in range(NB):
                    if k > 0:
                        prev_b = XR[R - 1]
                        prev_re, prev_im = X.t[:, k * R - 1, 0, :], X.t[:, k * R - 1, 1, :]
                    s.op("dve", lambda e, prev_re=prev_re: e.tensor_tensor(
                        out=c1.t[:], in0=APW.t[:, R - 1, 0, :, :], in1=prev_re.unsqueeze(1).to_broadcast([128, 2, 64]), op=ALU.mult),
                        reads=[APW, prev_b], writes=[c1])
                    s.op("dve", lambda e, prev_im=prev_im: e.tensor_tensor(
                        out=c2.t[:], in0=APW.t[:, R - 1, 1, :, :], in1=prev_im.unsqueeze(1).to_broadcast([128, 2, 64]), op=ALU.mult),
                        reads=[APW, prev_b], writes=[c2])
                    s.op("dve", lambda e: e.tensor_tensor(out=c1.t[:], in0=c1.t[:], in1=c2.t[:], op=ALU.add),
                         reads=[c1, c2], writes=[c1])
                    tk = k * R + R - 1
                    s.op("dve", lambda e, X=X, tk=tk: e.tensor_tensor(out=X.t[:, tk, :, :], in0=X.t[:, tk, :, :], in1=c1.t[:], op=ALU.add),
                         reads=[XR[R - 1], c1], writes=[XR[R - 1]])
                s.op("dve", lambda e, X=X, tc=tc: e.tensor_copy(out=last[tc % 2].t[:], in_=X.t[:, TS - 1, :, :]),
                     reads=[XR[R - 1]], writes=[last[tc % 2]])
                s.op(FIX_ENG, lambda e, src_=carry_src[1]: e.tensor_copy(out=CAR.t[:, 0, :, :], in_=src_),
                     reads=[carry_src[0]], writes=[CAR])
                s.op(FIX_ENG, lambda e, Xv=Xv: e.tensor_copy(out=CAR.t[:, 1:NB, :, :], in_=Xv[:, 0:NB - 1, R - 1, :, :]),
                     reads=[XR[R - 1]], writes=[CAR])
                cre = CAR.t[:, :, 0, :].unsqueeze(2).to_broadcast(sh4)
                cim = CAR.t[:, :, 1, :].unsqueeze(2).to_broadcast(sh4)
                for r in range(R - 1):
                    ApA = APW.t[:, r, 0, :, :].unsqueeze(1).to_broadcast(sh4)
                    ApB = APW.t[:, r, 1, :, :].unsqueeze(1).to_broadcast(sh4)
                    s.op(FIX_ENG, lambda e, ApA=ApA: e.tensor_tensor(out=F1.t[:], in0=ApA, in1=cre, op=ALU.mult),
                         reads=[APW, CAR], writes=[F1])
                    s.op(FIX_ENG, lambda e, ApB=ApB: e.tensor_tensor(out=F2.t[:], in0=ApB, in1=cim, op=ALU.mult),
                         reads=[APW, CAR], writes=[F2])
                    s.op(FIX_ENG, lambda e: e.tensor_tensor(out=F1.t[:], in0=F1.t[:], in1=F2.t[:], op=ALU.add),
                         reads=[F1, F2], writes=[F1])
                    s.op(FIX_ENG, lambda e, r=r, Xv=Xv: e.tensor_tensor(out=Xv[:, :, r, :, :], in0=Xv[:, :, r, :, :], in1=F1.t[:], op=ALU.add),
                         reads=[XR[r], F1], writes=[XR[r]])
                for half in range(2):
                    yb = ybank[half]
                    for cp in range(8):
                        c = half * 8 + cp
                        for r in range(4):
                            j = 4 * c + r
                            for reim in range(2):
                                s.op("pe", lambda e, yb=yb, cp=cp, r=r, j=j, reim=reim, X=X: e.matmul(
                                    yb.t[32 * r:32 * r + 32, cp * TS:(cp + 1) * TS], lhsT=CW[reim].t[:, j, :],
                                    rhs=X.t[:, :, reim, j], start=(reim == 0), stop=(reim == 1),
                                    tile_position=(0, 32 * r)), reads=[CW[reim]] + XR, writes=[yb])
                    tm, y_, sq_, sg_ = tmp[gi % 2], yv[gi % 2], sq[gi % 2], sg[gi % 2]
                    gi += 1
                    cs = slice(half * 8, half * 8 + 8)
                    s.op("pool", lambda e, tm=tm, u=u, cs=cs, t0=t0: e.tensor_tensor(
                        out=tm.t[:], in0=u.t[:, cs, t0:t0 + TS], in1=Dcol.t[:, cs].unsqueeze(2).to_broadcast([128, 8, TS]),
                        op=ALU.mult), reads=[u, Dcol], writes=[tm])
                    s.op("act", lambda e, y_=y_, yb=yb: e.copy(out=y_.t[:], in_=yb.t[:].rearrange("p (c t) -> p c t", c=8)),
                         reads=[yb], writes=[y_])
                    s.op("pool", lambda e, y_=y_, tm=tm: e.tensor_tensor(out=y_.t[:], in0=y_.t[:], in1=tm.t[:], op=ALU.add),
                         reads=[y_, tm], writes=[y_])
                    s.op("act", lambda e, y_=y_, sq_=sq_: e.activation(out=sq_.t[:], in_=y_.t[:], func=AF.Square),
                         reads=[y_], writes=[sq_])
                    s.op("pool", lambda e, sq_=sq_: e.tensor_scalar(out=sq_.t[:], in0=sq_.t[:], scalar1=0.044715, scalar2=1.0,
                                                                   op0=ALU.mult, op1=ALU.add), reads=[sq_], writes=[sq_])
                    s.op("pool", lambda e, sq_=sq_, y_=y_: e.tensor_tensor(out=sq_.t[:], in0=sq_.t[:], in1=y_.t[:], op=ALU.mult),
                         reads=[sq_, y_], writes=[sq_])
                    s.op("act", lambda e, sq_=sq_, sg_=sg_: e.activation(out=sg_.t[:], in_=sq_.t[:], func=AF.Sigmoid,
                                                                         scale=2.0 * math.sqrt(2.0 / math.pi)),
                         reads=[sq_], writes=[sg_])
                    s.op("pool", lambda e, sg_=sg_, y_=y_, z=z, cs=cs, z0=z0: e.tensor_tensor(
                        out=z.t[:, cs, z0:z0 + TS], in0=sg_.t[:], in1=y_.t[:], op=ALU.mult), reads=[sg_, y_], writes=[z])
                if tc % 8 == 7:
                    s.dma("sp", lambda e, z=z, blk=blk: e.dma_start(
                        out=ZT[:, :, blk * 512:(blk + 1) * 512].rearrange("c p t -> p c t"), in_=z.t[:]), z, reads=[z])

            NCH = L // TS
            pend = getattr(self, "conv_deferred", [])
            emit_bu(0)
            for tc in range(NCH):
                if tc + 1 < NCH:
                    emit_bu(tc + 1)
                emit_rest(tc)
                for _ in range(2):
                    if pend:
                        pend.pop(0)()
            while pend:
                pend.pop(0)()

    def phase_l0c(self):
        s = self.s
        x = self.input("x")
        ZT = self.scratch("ZT", [16, 128, L], BF16)
        wglu = self.wb("ssm_w_glu", [D, 2 * D]).rearrange("(kc p) n -> p kc n", p=128)
        hA = self.scratch("hA", [L, D], F32)
        with s.phase("l0c"):
            self.conv_wait("l0c")
            self.mk_eps()
            gpost = s.tile([128, D], F32, "gpost")
            self.load_gain(gpost, self.input("mix_post_g")[0])
            zt = [s.tile([128, 16, 512], BF16, "zt") for _ in range(2)]
            wv = [s.tile([128, 16, 512], BF16, "wv") for _ in range(2)]
            wg = [s.tile([128, 16, 512], BF16, "wg") for _ in range(2)]
            sgt = [s.tile([128, 512], F32, "sgt") for _ in range(2)]
            mix = [s.tile([128, D], F32, "mix") for _ in range(8)]
            hb = [s.tile([128, D], F32, "hb") for _ in range(2)]
            junk = s.tile([128, D], BF16, "junk")
            rs = [s.tile([128, 1], F32, "rs") for _ in range(2)]
            wc = 0
            k = 0
            cnt = 0
            for bp in range(2):
                for b2 in range(2):
                    blk = bp * 2 + b2
                    z = zt[b2]
                    s.dma("sp", lambda e, z=z, blk=blk: e.dma_start(
                        out=z.t[:], in_=ZT[:, :, blk * 512:(blk + 1) * 512].rearrange("c p t -> p c t")), z, writes=[z])
                for j in range(4):
                    a, g = wv[wc % 2], wg[wc % 2]
                    wc += 1
                    s.dma("sp", lambda e, a=a, j=j: e.dma_start(out=a.t[:], in_=wglu[:, :, j * 512:(j + 1) * 512]), a, writes=[a])
                    s.dma("sp", lambda e, g=g, j=j: e.dma_start(out=g.t[:], in_=wglu[:, :, D + j * 512:D + (j + 1) * 512]), g, writes=[g])
                    for b2 in range(2):
                        z = zt[b2]
                        for ti in range(4):
                            pv, pg = self.pb[(k % 2) * 2], self.pb[(k % 2) * 2 + 1]
                            st = sgt[k % 2]
                            k += 1
                            for kc in range(16):
                                s.op("pe", lambda e, pg=pg, z=z, kc=kc, ti=ti, g=g: e.matmul(
                                    pg.t[:], lhsT=z.t[:, kc, ti * 128:(ti + 1) * 128], rhs=g.t[:, kc, :],
                                    start=(kc == 0), stop=(kc == 15)), reads=[z, g], writes=[pg])
                            for kc in range(16):
                                s.op("pe", lambda e, pv=pv, z=z, kc=kc, ti=ti, a=a: e.matmul(
                                    pv.t[:], lhsT=z.t[:, kc, ti * 128:(ti + 1) * 128], rhs=a.t[:, kc, :],
                                    start=(kc == 0), stop=(kc == 15)), reads=[z, a], writes=[pv])
                            s.op("act", lambda e, st=st, pg=pg: e.activation(out=st.t[:], in_=pg.t[:], func=AF.Sigmoid),
                                 reads=[pg], writes=[st])
                            m = mix[b2 * 4 + ti]
                            s.op("dve", lambda e, m=m, pv=pv, st=st, j=j: e.tensor_tensor(
                                out=m.t[:, j * 512:(j + 1) * 512], in0=pv.t[:], in1=st.t[:], op=ALU.mult),
                                reads=[pv, st], writes=[m])
                for b2 in range(2):
                    blk = bp * 2 + b2
                    for ti in range(4):
                        tok = blk * 512 + ti * 128
                        h = hb[cnt % 2]
                        r = rs[cnt % 2]
                        cnt += 1
                        s.dma("sp", lambda e, h=h, tok=tok: e.dma_start(out=h.t[:], in_=x[tok:tok + 128, :]), h, writes=[h])
                        self.post_norm_residual(mix[b2 * 4 + ti], h, gpost, r, junk, hA[tok:tok + 128, :])

    def phase_mlp(self, layer, hin_name, hout_name):
        s = self.s
        hin = self.scratch(hin_name, [L, D], F32)
        hout = self.scratch(hout_name, [L, D], F32)
        wup = self.wb(f"mlp_w_up{layer}", [D, DFF]).rearrange("(kc p) n -> p kc n", p=128)
        wdn = self.wb(f"mlp_w_down{layer}", [DFF, D]).rearrange("(fc p) n -> p fc n", p=128)
        with s.phase(f"mlp{layer}"):
            self.conv_wait(f"mlp{layer}")
            self.mk_eps()
            gpre = s.tile([128, D], F32, "gpre")
            gpost = s.tile([128, D], F32, "gpost")
            self.load_gain(gpre, self.input("mlp_pre_g")[layer])
            self.load_gain(gpost, self.input("mlp_post_g")[layer])
            hb = [s.tile([128, D], F32, "hb") for _ in range(2)]
            junk = s.tile([128, D], BF16, "junk")
            xn = [s.tile([128, D], BF16, "xn") for _ in range(2)]
            rs = [s.tile([128, 1], F32, "rs") for _ in range(2)]
            xnT = s.tile([128, 16, 512], BF16, "xnT")
            hidT = s.tile([128, 64, 512], BF16, "hidT")
            wu = [s.tile([128, 16, 256], BF16, "wu") for _ in range(2)]
            wd = [s.tile([128, 8, 512], BF16, "wd") for _ in range(2)]
            r32 = [s.tile([128, 512], F32, "r32") for _ in range(2)]
            ff = [s.tile([128, D], F32, "ff") for _ in range(4)]
            cnt = 0
            uc = 0
            dc = 0
            fcc = 0
            for blk in range(4):
                for ti in range(4):
                    tok = blk * 512 + ti * 128
                    h = hb[cnt % 2]
                    r = rs[cnt % 2]
                    xx = xn[cnt % 2]
                    cnt += 1
                    s.dma("sp", lambda e, h=h, tok=tok: e.dma_start(out=h.t[:], in_=hin[tok:tok + 128, :]), h, writes=[h])
                    self.norm_stats(h, r, junk)
                    s.op("dve", lambda e, xx=xx, h=h, r=r: e.scalar_tensor_tensor(
                        out=xx.t[:], in0=h.t[:], scalar=r.t[:], in1=gpre.t[:], op0=ALU.mult, op1=ALU.mult),
                        reads=[h, r, gpre], writes=[xx])
                    self.transpose_to(xx, xnT, ti * 128, 0)
                for fg in range(32):
                    w = wu[uc % 2]
                    uc += 1
                    s.dma("sp", lambda e, w=w, fg=fg: e.dma_start(out=w.t[:], in_=wup[:, :, fg * 256:(fg + 1) * 256]), w, writes=[w])
                    for j in range(2):
                        fc = fg * 2 + j
                        bank = self.pb[4 + fcc % 2]
                        rr = r32[fcc % 2]
                        fcc += 1
                        for kc in range(16):
                            s.op("pe", lambda e, bank=bank, w=w, kc=kc, j=j: e.matmul(
                                bank.t[:], lhsT=w.t[:, kc, j * 128:(j + 1) * 128], rhs=xnT.t[:, kc, :],
                                start=(kc == 0), stop=(kc == 15)), reads=[w, xnT], writes=[bank])
                        s.op("act", lambda e, rr=rr, bank=bank: e.activation(out=rr.t[:], in_=bank.t[:], func=AF.Relu),
                             reads=[bank], writes=[rr])
                        s.op("pool", lambda e, rr=rr, fc=fc: e.tensor_tensor(out=hidT.t[:, fc, :], in0=rr.t[:], in1=rr.t[:], op=ALU.mult),
                             reads=[rr], writes=[hidT])
                for c in range(4):
                    for fg in range(8):
                        w = wd[dc % 2]
                        dc += 1
                        s.dma("sp", lambda e, w=w, fg=fg, c=c: e.dma_start(
                            out=w.t[:], in_=wdn[:, fg * 8:(fg + 1) * 8, c * 512:(c + 1) * 512]), w, writes=[w])
                        for ti in range(4):
                            bank = self.pb[ti]
                            for j in range(8):
                                fc = fg * 8 + j
                                s.op("pe", lambda e, bank=bank, fc=fc, ti=ti, w=w, j=j: e.matmul(
                                    bank.t[:], lhsT=hidT.t[:, fc, ti * 128:(ti + 1) * 128], rhs=w.t[:, j, :],
                                    start=(fc == 0), stop=(fc == 63)), reads=[hidT, w], writes=[bank])
                    for ti in range(4):
                        bank = self.pb[ti]
                        f = ff[ti]
                        if ti % 2 == 0:
                            s.op("act", lambda e, f=f, bank=bank, c=c: e.copy(out=f.t[:, c * 512:(c + 1) * 512], in_=bank.t[:]),
                                 reads=[bank], writes=[f])
                        else:
                            s.op("dve", lambda e, f=f, bank=bank, c=c: e.tensor_copy(out=f.t[:, c * 512:(c + 1) * 512], in_=bank.t[:]),
                                 reads=[bank], writes=[f])
                for ti in range(4):
                    tok = blk * 512 + ti * 128
                    h = hb[cnt % 2]
                    r = rs[cnt % 2]
                    cnt += 1
                    s.dma("sp", lambda e, h=h, tok=tok: e.dma_start(out=h.t[:], in_=hin[tok:tok + 128, :]), h, writes=[h])
                    self.post_norm_residual(ff[ti], h, gpost, r, junk, hout[tok:tok + 128, :])

    def rope(self, src, dst, nhm, ti, tA, tB, scale=None):
        s = self.s
        P = self
        cos = P.rcos.t[:, ti, :]
        sin = P.rsin.t[:, ti, :]
        A = src.t[:].rearrange("p (h two f) -> p h two f", two=2, f=64)
        O = dst.t[:].rearrange("p (h two f) -> p h two f", two=2, f=64)
        M1 = tA.t[:].rearrange("p (h two f) -> p h two f", two=2, f=64)
        M2 = tB.t[:].rearrange("p (h two f) -> p h two f", two=2, f=64)
        cb4 = cos.unsqueeze(1).unsqueeze(1).to_broadcast([128, nhm, 2, 64])
        sb3 = sin.unsqueeze(1).to_broadcast([128, nhm, 64])
        s.op("pool", lambda e: e.tensor_tensor(out=M1, in0=A, in1=cb4, op=ALU.mult), reads=[src, P.rcos], writes=[tA])
        s.op("pool", lambda e: e.tensor_tensor(out=M2[:, :, 0, :], in0=A[:, :, 1, :], in1=sb3, op=ALU.mult),
             reads=[src, P.rsin], writes=[tB])
        s.op("pool", lambda e: e.tensor_tensor(out=M2[:, :, 1, :], in0=A[:, :, 0, :], in1=sb3, op=ALU.mult),
             reads=[src, P.rsin], writes=[tB])
        s.op("dve", lambda e: e.tensor_tensor(out=O[:, :, 0, :], in0=M1[:, :, 0, :], in1=M2[:, :, 0, :], op=ALU.subtract),
             reads=[tA, tB], writes=[dst])
        s.op("dve", lambda e: e.tensor_tensor(out=O[:, :, 1, :], in0=M1[:, :, 1, :], in1=M2[:, :, 1, :], op=ALU.add),
             reads=[tA, tB], writes=[dst])

    def phase_qkv(self):
        s = self.s
        P = self
        hB = self.scratch("hB", [L, D], F32)
        wkv = self.wb("w_kv", [D, 2 * D]).rearrange("(kc p) n -> p kc n", p=128)
        wq = self.wb("attn_w_q", [D, D]).rearrange("(kc p) n -> p kc n", p=128)
        KT = self.scratch("KT", [16, 128, L], BF16)
        QT = self.scratch("QT", [16, 128, L], BF16)
        V = self.scratch("V", [L, D], BF16)
        scale = 128 ** -0.5
        with s.phase("qkv"):
            self.conv_wait("qkv")
            self.mk_eps()
            gkv = s.tile([128, D], F32, "gkv")
            gq = s.tile([128, D], F32, "gq")
            self.load_gain(gkv, self.input("kv_norm_g"))
            self.load_gain(gq, self.input("mix_pre_g")[1])
            hb = [s.tile([128, D], F32, "hb") for _ in range(2)]
            junk = s.tile([128, D], BF16, "junk")
            xk = [s.tile([128, D], BF16, "xk") for _ in range(2)]
            xq = [s.tile([128, D], BF16, "xq") for _ in range(2)]
            rs = [s.tile([128, 1], F32, "rs") for _ in range(2)]
            xkT = s.tile([128, 16, 512], BF16, "xkT")
            xqT = s.tile([128, 16, 512], BF16, "xqT")
            wt = [s.tile([128, 16, 512], BF16, "wt") for _ in range(2)]
            kx = [s.tile([128, 512], F32, "kx") for _ in range(2)]
            kr = [s.tile([128, 512], BF16, "kr") for _ in range(3)]
            tA = [s.tile([128, 512], F32, "tA") for _ in range(2)]
            tB = [s.tile([128, 512], F32, "tB") for _ in range(2)]
            kTb = [s.tile([128, 16, 512], BF16, "kTb") for _ in range(2)]
            vb = s.tile([128, 4, D], BF16, "vb")
            cnt = 0
            wc = 0
            pc = 0
            for blk in range(4):
                for ti in range(4):
                    tok = blk * 512 + ti * 128
                    h = hb[cnt % 2]
                    r = rs[cnt % 2]
                    a, b = xk[cnt % 2], xq[cnt % 2]
                    cnt += 1
                    s.dma("sp", lambda e, h=h, tok=tok: e.dma_start(out=h.t[:], in_=hB[tok:tok + 128, :]), h, writes=[h])
                    self.norm_stats(h, r, junk)
                    s.op("dve", lambda e, a=a, h=h, r=r: e.scalar_tensor_tensor(
                        out=a.t[:], in0=h.t[:], scalar=r.t[:], in1=gkv.t[:], op0=ALU.mult, op1=ALU.mult),
                        reads=[h, r, gkv], writes=[a])
                    s.op("dve", lambda e, b=b, h=h, r=r: e.scalar_tensor_tensor(
                        out=b.t[:], in0=h.t[:], scalar=r.t[:], in1=gq.t[:], op0=ALU.mult, op1=ALU.mult),
                        reads=[h, r, gq], writes=[b])
                    self.transpose_to(a, xkT, ti * 128, 0)
                    self.transpose_to(b, xqT, ti * 128, 0)
                for which in range(2):
                    pending = None
                    wsrc = wkv if which == 0 else wq
                    xT = xkT if which == 0 else xqT
                    dstT = kTb[which]
                    for cb in range(4):
                        w = wt[wc % 2]
                        wc += 1
                        s.dma("sp", lambda e, w=w, wsrc=wsrc, cb=cb: e.dma_start(out=w.t[:], in_=wsrc[:, :, cb * 512:(cb + 1) * 512]),
                              w, writes=[w])
                        for ti in range(4):
                            bank = self.pb[pc % 2]
                            kx_, kr_ = kx[pc % 2], kr[pc % 3]
                            pc += 1
                            for kc in range(16):
                                s.op("pe", lambda e, bank=bank, xT=xT, kc=kc, ti=ti, w=w: e.matmul(
                                    bank.t[:], lhsT=xT.t[:, kc, ti * 128:(ti + 1) * 128], rhs=w.t[:, kc, :],
                                    start=(kc == 0), stop=(kc == 15)), reads=[xT, w], writes=[bank])
                            if which == 0:
                                s.op("act", lambda e, kx_=kx_, bank=bank: e.copy(out=kx_.t[:], in_=bank.t[:]), reads=[bank], writes=[kx_])
                            else:
                                s.op("act", lambda e, kx_=kx_, bank=bank: e.mul(out=kx_.t[:], in_=bank.t[:], mul=scale),
                                     reads=[bank], writes=[kx_])
                            self.rope(kx_, kr_, 4, blk * 4 + ti, tA[pc % 2], tB[pc % 2])
                            if pending is not None:
                                pending()

                            def mk(pt=self.pt[pc % 2], kr_=kr_, dstT=dstT, cb=cb, ti=ti):
                                def f():
                                    for hm in range(4):
                                        s.op("pe", lambda e, hm=hm: e.transpose(
                                            out=pt.t[:, hm * 128:(hm + 1) * 128], in_=kr_.t[:, hm * 128:(hm + 1) * 128],
                                            identity=P.ident.t[:]), reads=[kr_, P.ident], writes=[pt])
                                    s.op("dve", lambda e: e.tensor_copy(
                                        out=dstT.t[:, cb * 4:(cb + 1) * 4, ti * 128:(ti + 1) * 128],
                                        in_=pt.t[:, 0:512].rearrange("p (h t) -> p h t", h=4)), reads=[pt], writes=[dstT])
                                return f
                            pending = mk()
                    if pending is not None:
                        pending()
                        pending = None
                    dd = KT if which == 0 else QT
                    s.dma("sp", lambda e, dd=dd, dstT=dstT, blk=blk: e.dma_start(
                        out=dd[:, :, blk * 512:(blk + 1) * 512].rearrange("h p t -> p h t"), in_=dstT.t[:]), dstT, reads=[dstT])
                for cb in range(4):
                    w = wt[wc % 2]
                    wc += 1
                    s.dma("sp", lambda e, w=w, cb=cb: e.dma_start(out=w.t[:], in_=wkv[:, :, D + cb * 512:D + (cb + 1) * 512]), w, writes=[w])
                    for ti in range(4):
                        bank = self.pb[pc % 2]
                        pc += 1
                        for kc in range(16):
                            s.op("pe", lambda e, bank=bank, kc=kc, ti=ti, w=w: e.matmul(
                                bank.t[:], lhsT=xkT.t[:, kc, ti * 128:(ti + 1) * 128], rhs=w.t[:, kc, :],
                                start=(kc == 0), stop=(kc == 15)), reads=[xkT, w], writes=[bank])
                        s.op("act", lambda e, bank=bank, ti=ti, cb=cb: e.copy(out=vb.t[:, ti, cb * 512:(cb + 1) * 512], in_=bank.t[:]),
                             reads=[bank], writes=[vb])
                s.dma("sp", lambda e, blk=blk: e.dma_start(
                    out=V[blk * 512:(blk + 1) * 512, :].rearrange("(t p) n -> p t n", p=128), in_=vb.t[:]), vb, reads=[vb])

    def phase_attn(self):
        s = self.s
        P = self
        hB = self.scratch("hB", [L, D], F32)
        hC = self.scratch("hC", [L, D], F32)
        KT = self.scratch("KT", [16, 128, L], BF16)
        QT = self.scratch("QT", [16, 128, L], BF16)
        V = self.scratch("V", [L, D], BF16)
        wo = self.wb("attn_w_o", [D, D]).rearrange("(kc p) n -> p kc n", p=128)
        with s.phase("attn"):
            self.conv_wait("attn")
            self.mk_eps()
            gpost = s.tile([128, D], F32, "gpost")
            self.load_gain(gpost, self.input("mix_post_g")[1])
            kT = [s.tile([128, L], BF16, "kT") for _ in range(2)]
            qT = [s.tile([128, 512], BF16, "qT") for _ in range(2)]
            vt = [s.tile([128, 16, 257], BF16, "vt") for _ in range(2)]
            for v_ in vt:
                s.op("pool", lambda e, v_=v_: e.memset(v_.t[:, :, 256:257], 1.0), writes=[v_])
            pT = [s.tile([128, 512], BF16, "pT") for _ in range(3)]
            om4 = [s.tile([128, 4, 257], F32, "om") for _ in range(4)]
            obf = s.tile([128, 4, D], BF16, "obf")
            oT = s.tile([128, 16, 512], BF16, "oT")
            rr = [s.tile([128, 1], F32, "rr") for _ in range(4)]
            tt = s.tile([128, 256], F32, "tt")
            oo = s.tile([128, 256], F32, "oo")
            jk2 = s.tile([128, 256], F32, "jk2")
            wt = [s.tile([128, 16, 512], BF16, "wo") for _ in range(2)]
            att = [s.tile([128, D], F32, "att") for _ in range(4)]
            hb = [s.tile([128, D], F32, "hb") for _ in range(2)]
            junk = s.tile([128, D], BF16, "junk")
            rs = [s.tile([128, 1], F32, "rs") for _ in range(2)]
            eps256 = s.tile([128, 1], F32, "eps256")
            s.op("pool", lambda e: e.memset(eps256.t[:], EPS), writes=[eps256])
            stb = [self.pb[4], self.pb[5]]
            kc_ = 0
            pc = 0
            stc = 0
            wc = 0
            cnt = 0
            for sb in range(4):
                nkt = 4 * (sb + 1)
                for hh in range(8):
                    om = om4[(hh % 2) * 2:(hh % 2) * 2 + 2]
                    v_ = vt[(sb * 8 + hh) % 2]
                    s.dma("sp", lambda e, v_=v_, hh=hh, nkt=nkt: e.dma_start(
                        out=v_.t[:, 0:nkt, 0:256],
                        in_=V[0:nkt * 128, hh * 256:(hh + 1) * 256].rearrange("(t p) n -> p t n", p=128)), v_, writes=[v_])
                    for m in range(2):
                        hm = hh * 2 + m
                        k_ = kT[kc_ % 2]
                        q_ = qT[kc_ % 2]
                        kc_ += 1
                        s.dma("sp", lambda e, k_=k_, hm=hm, nkt=nkt: e.dma_start(out=k_.t[:, 0:nkt * 128], in_=KT[hm, :, 0:nkt * 128]),
                              k_, writes=[k_])
                        s.dma("sp", lambda e, q_=q_, hm=hm, sb=sb: e.dma_start(out=q_.t[:], in_=QT[hm, :, sb * 512:(sb + 1) * 512]),
                              q_, writes=[q_])
                        def emit_st(j):
                            nonlocal stc, pc
                            qlo = max(0, j - 4 * sb)
                            n = 512 - qlo * 128
                            st = stb[stc % 2]
                            stc += 1
                            p_ = pT[pc % 3]
                            pc += 1
                            s.op("pe", lambda e, st=st, k_=k_, q_=q_, j=j, qlo=qlo, n=n: e.matmul(
                                st.t[:, 0:n], lhsT=k_.t[:, j * 128:(j + 1) * 128], rhs=q_.t[:, qlo * 128:512],
                                start=True, stop=True), reads=[k_, q_], writes=[st])
                            s.op("act", lambda e, st=st, p_=p_, n=n: e.activation(out=p_.t[:, 0:n], in_=st.t[:, 0:n], func=AF.Exp),
                                 reads=[st], writes=[p_])
                            if j >= 4 * sb:
                                s.op("pool", lambda e, p_=p_: e.tensor_tensor(out=p_.t[:, 0:128], in0=p_.t[:, 0:128],
                                                                              in1=P.cmask.t[:], op=ALU.mult),
                                     reads=[p_, P.cmask], writes=[p_])
                            return p_, qlo
                        nxt = emit_st(0)
                        for j in range(nkt):
                            p_, qlo = nxt
                            if j + 1 < nkt:
                                nxt = emit_st(j + 1)
                            for qt in range(qlo, 4):
                                ob = self.pb[qt]
                                s.op("pe", lambda e, ob=ob, p_=p_, qt=qt, qlo=qlo, v_=v_, j=j, sb=sb: e.matmul(
                                    ob.t[:, 0:257], lhsT=p_.t[:, (qt - qlo) * 128:(qt - qlo + 1) * 128], rhs=v_.t[:, j, :],
                                    start=(j == 0), stop=(j == 4 * sb + qt)), reads=[p_, v_], writes=[ob])
                        o_ = om[m]
                        for qt in range(4):
                            ob = self.pb[qt]
                            if qt % 2 == 0:
                                s.op("act", lambda e, o_=o_, ob=ob, qt=qt: e.copy(out=o_.t[:, qt, :], in_=ob.t[:, 0:257]),
                                     reads=[ob], writes=[o_])
                            else:
                                s.op("dve", lambda e, o_=o_, ob=ob, qt=qt: e.tensor_copy(out=o_.t[:, qt, :], in_=ob.t[:, 0:257]),
                                     reads=[ob], writes=[o_])
                    for qt in range(4):
                        r1, r2, r3 = rr[0], rr[1], rr[2]
                        s.op("dve", lambda e, qt=qt, om0=om[0], om1=om[1]: e.reciprocal(out=r1.t[:], in_=om0.t[:, qt, 256:257]), reads=[om[0]], writes=[r1])
                        s.op("dve", lambda e, qt=qt, om0=om[0], om1=om[1]: e.reciprocal(out=r2.t[:], in_=om1.t[:, qt, 256:257]), reads=[om[1]], writes=[r2])
                        s.op("dve", lambda e: e.tensor_tensor(out=r2.t[:], in0=r2.t[:], in1=P.lam.t[:], op=ALU.mult),
                             reads=[r2, P.lam], writes=[r2])
                        s.op("dve", lambda e, qt=qt, om0=om[0], om1=om[1]: e.tensor_scalar(out=tt.t[:], in0=om1.t[:, qt, 0:256], scalar1=r2.t[:], scalar2=None,
                                                                     op0=ALU.mult), reads=[om[1], r2], writes=[tt])
                        s.op("dve", lambda e, qt=qt, om0=om[0], om1=om[1]: e.scalar_tensor_tensor(out=oo.t[:], in0=om0.t[:, qt, 0:256], scalar=r1.t[:],
                                                                            in1=tt.t[:], op0=ALU.mult, op1=ALU.subtract),
                             reads=[om[0], r1, tt], writes=[oo])
                        s.op("act", lambda e: e.activation(out=jk2.t[:], in_=oo.t[:], func=AF.Square, accum_out=r3.t[:]),
                             reads=[oo], writes=[jk2, r3])
                        s.op("act", lambda e: e.activation(out=r3.t[:], in_=r3.t[:], func=AF.Sqrt, scale=1.0 / 256, bias=eps256.t[:]),
                             reads=[r3, eps256], writes=[r3])
                        s.op("dve", lambda e: e.reciprocal(out=r3.t[:], in_=r3.t[:]), reads=[r3], writes=[r3])
                        s.op("dve", lambda e, qt=qt, hh=hh: e.scalar_tensor_tensor(
                            out=obf.t[:, qt, hh * 256:(hh + 1) * 256], in0=oo.t[:], scalar=r3.t[:], in1=P.gsub.t[:],
                            op0=ALU.mult, op1=ALU.mult), reads=[oo, r3, P.gsub], writes=[obf])
                for qt in range(4):
                    for half in range(2):
                        pt = self.pt[half]
                        for k in range(8):
                            kc = half * 8 + k
                            s.op("pe", lambda e, pt=pt, k=k, kc=kc, qt=qt: e.transpose(
                                out=pt.t[:, k * 128:(k + 1) * 128], in_=obf.t[:, qt, kc * 128:(kc + 1) * 128],
                                identity=P.ident.t[:]), reads=[obf, P.ident], writes=[pt])
                        s.op("dve" if half else "act",
                             (lambda e, pt=pt, half=half, qt=qt: e.tensor_copy(
                                 out=oT.t[:, half * 8:(half + 1) * 8, qt * 128:(qt + 1) * 128],
                                 in_=pt.t[:].rearrange("p (k t) -> p k t", k=8))) if half else
                             (lambda e, pt=pt, half=half, qt=qt: e.copy(
                                 out=oT.t[:, half * 8:(half + 1) * 8, qt * 128:(qt + 1) * 128],
                                 in_=pt.t[:].rearrange("p (k t) -> p k t", k=8))),
                             reads=[pt], writes=[oT])
                for cb in range(4):
                    w = wt[wc % 2]
                    wc += 1
                    s.dma("sp", lambda e, w=w, cb=cb: e.dma_start(out=w.t[:], in_=wo[:, :, cb * 512:(cb + 1) * 512]), w, writes=[w])
                    for qt in range(4):
                        bank = stb[stc % 2]
                        stc += 1
                        for kc in range(16):
                            s.op("pe", lambda e, bank=bank, kc=kc, qt=qt, w=w: e.matmul(
                                bank.t[:], lhsT=oT.t[:, kc, qt * 128:(qt + 1) * 128], rhs=w.t[:, kc, :],
                                start=(kc == 0), stop=(kc == 15)), reads=[oT, w], writes=[bank])
                        a_ = att[qt]
                        s.op("act", lambda e, a_=a_, bank=bank, cb=cb: e.copy(out=a_.t[:, cb * 512:(cb + 1) * 512], in_=bank.t[:]),
                             reads=[bank], writes=[a_])
                for qt in range(4):
                    tok = sb * 512 + qt * 128
                    h = hb[cnt % 2]
                    r = rs[cnt % 2]
                    cnt += 1
                    s.dma("sp", lambda e, h=h, tok=tok: e.dma_start(out=h.t[:], in_=hB[tok:tok + 128, :]), h, writes=[h])
                    self.post_norm_residual(att[qt], h, gpost, r, junk, hC[tok:tok + 128, :])


def _in_map(inputs, b):
    m = {}
    for k, v in inputs.items():
        a = np.asarray(v)
        if k == "x":
            a = a[b]
        m[k] = np.ascontiguousarray(a.reshape(INPUT_SHAPES[k]), dtype=np.float32)
    return m


def kernel(**inputs):
    prog = Prog()
    nc = prog.build()
    used = set(prog.inp.keys())
    in_maps = []
    for b in range(NCORES):
        m = _in_map(inputs, b)
        in_maps.append({k: v for k, v in m.items() if k in used})
    res = run_bass_kernel_spmd(nc, in_maps, core_ids=list(range(NCORES)))
    out = np.stack([np.asarray(res.results[b]["out"], dtype=np.float32) for b in range(NCORES)], axis=0)
    return out
```
